# Optimizing a Trainium2 kernel written in Bass

```python
import math
import jax
import jax.numpy as jnp
from jax import lax
import numpy as np

D_MODEL = 1024
BATCH = 2
SEQ = 16384
DEPTH = 2

GRID_W = 64
CTX_LEN = 256
CHUNK = 128
Q_BLOCK = 128
ROPE_BASE = 10000.0
EPS = 1e-6

BRANCH_DIM = D_MODEL // 2
N_BRANCH = 4
CONV_DIM = BRANCH_DIM
CONV_WIDTH = 31
SSM_INNER = BRANCH_DIM
SSM_HEAD_DIM = 64
SSM_HEADS = SSM_INNER // SSM_HEAD_DIM
SSM_GROUPS = 2
SSM_STATE = 128
SSM_CONV = 5
SSM_XBC = SSM_INNER + 2 * SSM_GROUPS * SSM_STATE
RET_HEADS = 4
RET_QK_DIM = 64
RET_INNER = BRANCH_DIM
RET_V_DIM = RET_INNER // RET_HEADS
MLA_HEADS = 8
MLA_NOPE = 64
MLA_ROPE = 32
MLA_V = BRANCH_DIM // MLA_HEADS
MLA_Q_RANK = 384
MLA_KV_RANK = 256
MLA_INNER = MLA_HEADS * MLA_V
FFN_DIM = ((8 * D_MODEL // 3 + 255) // 256) * 256

IN_SPLITS = (
    2 * CONV_DIM,
    SSM_INNER, SSM_XBC, 2 * SSM_HEADS,
    RET_HEADS * RET_QK_DIM, RET_HEADS * RET_QK_DIM, RET_INNER, RET_INNER,
    MLA_Q_RANK, MLA_KV_RANK, MLA_ROPE,
    N_BRANCH * D_MODEL,
)
N_IN = sum(IN_SPLITS)

kernel_name = 'hybrid_gated_parallel_dit_block'


def split_cols(u, sizes):
    parts, off = [], 0
    for s in sizes:
        parts.append(u[..., off:off + s])
        off += s
    return parts


def flip(a):
    return jnp.flip(a, axis=1)


def rmsnorm(x, g):
    xf = x.astype(jnp.float32)
    y = xf * lax.rsqrt(jnp.mean(xf * xf, axis=-1, keepdims=True) + EPS)
    return y.astype(x.dtype) * g


def layernorm(x, g, b):
    xf = x.astype(jnp.float32)
    mu = jnp.mean(xf, axis=-1, keepdims=True)
    var = jnp.mean(jnp.square(xf - mu), axis=-1, keepdims=True)
    return ((xf - mu) * lax.rsqrt(var + EPS)).astype(x.dtype) * g + b


def modulate(h, shift, scale):
    return h * (1 + scale) + shift


def axial_rope(n, rot_dim, dtype):
    rows = n // GRID_W
    row = jnp.repeat(jnp.arange(rows, dtype=jnp.float32), GRID_W)
    col = jnp.tile(jnp.arange(GRID_W, dtype=jnp.float32), rows)
    nf = rot_dim // 4
    inv = ROPE_BASE ** (-jnp.arange(nf, dtype=jnp.float32) / nf)
    ang = jnp.concatenate([row[:, None] * inv, col[:, None] * inv], axis=-1)
    return jnp.cos(ang).astype(dtype), jnp.sin(ang).astype(dtype)


def apply_rope(x, rope):
    if rope is None:
        return x
    cos, sin = rope[0][:, None, :], rope[1][:, None, :]
    x1, x2 = jnp.split(x, 2, axis=-1)
    return jnp.concatenate([x1 * cos - x2 * sin, x1 * sin + x2 * cos], axis=-1)


def dwconv(x, w, b):
    pad = w.shape[0] // 2
    y = lax.conv_general_dilated(x, w[:, None, :].astype(x.dtype), window_strides=(1,),
                                 padding=((pad, pad),), dimension_numbers=('NWC', 'WIO', 'NWC'),
                                 feature_group_count=x.shape[-1])
    return y + b


def chunked_scan(q, k, v, log_a, h0, need_y):
    f32 = jnp.float32
    b, l, nh, n = k.shape
    p = v.shape[-1]
    nc = l // CHUNK
    k = k.astype(f32).reshape(b, nc, CHUNK, nh, n)
    v = v.astype(f32).reshape(b, nc, CHUNK, nh, p)
    cum = jnp.cumsum(log_a.astype(f32).reshape(b, nc, CHUNK, nh), axis=2)
    total = cum[:, :, -1]
    s_local = jnp.einsum('bclhn,bclhp->bchnp', k * jnp.exp(total[:, :, None] - cum)[..., None], v)

    def step(h, inp):
        s_c, t_c = inp
        return jnp.exp(t_c)[..., None, None] * h + s_c, h

    h_last, h_enter = lax.scan(step, h0.astype(f32), (jnp.moveaxis(s_local, 1, 0), jnp.moveaxis(total, 1, 0)))
    if not need_y:
        return None, h_last
    q = q.astype(f32).reshape(b, nc, CHUNK, nh, n)
    h_enter = jnp.moveaxis(h_enter, 0, 1)
    seg = cum[:, :, :, None, :] - cum[:, :, None, :, :]
    mask = jnp.tril(jnp.ones((CHUNK, CHUNK), dtype=bool))[None, None, :, :, None]
    decay = jnp.exp(jnp.where(mask, seg, -jnp.inf))
    scores = jnp.einsum('bclhn,bcshn->bclsh', q, k) * decay
    y = (jnp.einsum('bclsh,bcshp->bclhp', scores, v)
         + jnp.einsum('bclhn,bchnp->bclhp', q, h_enter) * jnp.exp(cum)[..., None])
    return y.reshape(b, l, nh, p), h_last


def conv_module(u, p):
    a, g = jnp.split(u, 2, axis=-1)
    h = dwconv(a * jax.nn.sigmoid(g), p['conv_w'], p['conv_b'])
    return jax.nn.silu(layernorm(h, p['conv_ln_g'], p['conv_ln_b']))


def ssm_mixer(z, xbc, dt_raw, p, h0, need_y):
    f32 = jnp.float32
    b, l, _ = xbc.shape
    xbc = jax.nn.silu(dwconv(xbc, p['ssm_conv_w'], p['ssm_conv_b']))
    xh, bm, cm = split_cols(xbc, (SSM_INNER, SSM_GROUPS * SSM_STATE, SSM_GROUPS * SSM_STATE))
    xf = xh.reshape(b, l, SSM_HEADS, SSM_HEAD_DIM).astype(f32)
    rep = SSM_HEADS // SSM_GROUPS
    bm = jnp.repeat(bm.reshape(b, l, SSM_GROUPS, SSM_STATE), rep, axis=2)
    cm = jnp.repeat(cm.reshape(b, l, SSM_GROUPS, SSM_STATE), rep, axis=2)
    dt = jax.nn.softplus(dt_raw.astype(f32).reshape(b, l, 2, SSM_HEADS) + p['ssm_dt_bias'].astype(f32))
    log_a = dt * -jnp.exp(p['ssm_a_log'].astype(f32))
    y_f, h_f = chunked_scan(cm, bm, xf * dt[:, :, 0, :, None], log_a[:, :, 0], h0[0], need_y)
    y_b, h_b = chunked_scan(flip(cm), flip(bm), flip(xf * dt[:, :, 1, :, None]), flip(log_a[:, :, 1]), h0[1], need_y)
    if not need_y:
        return None, (h_f, h_b)
    y = y_f + flip(y_b) + p['ssm_d'].astype(f32)[:, None] * xf
    y = y.reshape(b, l, SSM_INNER).astype(z.dtype) * jax.nn.silu(z)
    y = rmsnorm(y.reshape(b, l, SSM_GROUPS, SSM_INNER // SSM_GROUPS), p['ssm_norm_g'].reshape(SSM_GROUPS, -1))
    return y.reshape(b, l, SSM_INNER), (h_f, h_b)


def retention_mixer(q, k, v, g, p, h0, rope, need_y):
    b, l, _ = k.shape
    q = apply_rope(q.reshape(b, l, RET_HEADS, RET_QK_DIM), rope)
    k = apply_rope(k.reshape(b, l, RET_HEADS, RET_QK_DIM), rope) * (RET_QK_DIM ** -0.5)
    v = v.reshape(b, l, RET_HEADS, RET_V_DIM)
    log_gamma = -jnp.exp(p['ret_decay'].astype(jnp.float32))
    la_f = jnp.broadcast_to(log_gamma[0], (b, l, RET_HEADS))
    la_b = jnp.broadcast_to(log_gamma[1], (b, l, RET_HEADS))
    y_f, h_f = chunked_scan(q, k, v, la_f, h0[0], need_y)
    y_b, h_b = chunked_scan(flip(q), flip(k), flip(v), la_b, h0[1], need_y)
    if not need_y:
        return None, (h_f, h_b)
    y = (y_f + flip(y_b)).astype(g.dtype)
    yn = layernorm(y, p['ret_gn_g'].reshape(RET_HEADS, RET_V_DIM), p['ret_gn_b'].reshape(RET_HEADS, RET_V_DIM))
    return jax.nn.silu(g) * yn.reshape(b, l, RET_INNER), (h_f, h_b)


def mla_kv(c_kv, k_r, p, rope):
    b, l, _ = c_kv.shape
    kv = (rmsnorm(c_kv, p['mla_kv_norm_g']) @ p['mla_w_ukv']).reshape(b, l, MLA_HEADS, MLA_NOPE + MLA_V)
    k_rope = apply_rope(k_r[:, :, None, :], rope)
    k = jnp.concatenate([kv[..., :MLA_NOPE], jnp.broadcast_to(k_rope, (b, l, MLA_HEADS, MLA_ROPE))], axis=-1)
    return k, kv[..., MLA_NOPE:]


def mla_q(c_q, p, rope):
    b, l, _ = c_q.shape
    q = (rmsnorm(c_q, p['mla_q_norm_g']) @ p['mla_w_uq']).reshape(b, l, MLA_HEADS, MLA_NOPE + MLA_ROPE)
    return jnp.concatenate([q[..., :MLA_NOPE], apply_rope(q[..., MLA_NOPE:], rope)], axis=-1)


def block_attention(q, k, v):
    b, lq, h, d = q.shape
    nb = lq // Q_BLOCK
    scale = d ** -0.5
    qb = jnp.moveaxis(q.reshape(b, nb, Q_BLOCK, h, d), 1, 0)

    def one(qi):
        s = jnp.einsum('bqhd,bkhd->bhqk', qi, k).astype(jnp.float32) * scale
        w = jax.nn.softmax(s, axis=-1).astype(v.dtype)
        return jnp.einsum('bhqk,bkhd->bqhd', w, v)

    o = lax.map(one, qb)
    return jnp.moveaxis(o, 0, 1).reshape(b, lq, h * v.shape[-1])


def merge_branches(branches, gate_logits, p):
    merged = jax.nn.sigmoid(gate_logits[..., :D_MODEL]) * (branches[0] @ p['w_branch'][0])
    for i in range(1, N_BRANCH):
        gi = jax.nn.sigmoid(gate_logits[..., i * D_MODEL:(i + 1) * D_MODEL])
        merged = merged + gi * (branches[i] @ p['w_branch'][i])
    return merged @ p['w_out']


def token_mixers(hx, hc, p, rope_ret, rope_mla, need_ctx_out):
    f32 = jnp.float32
    b = hx.shape[0]
    (conv_x, z_x, xbc_x, dt_x, rq_x, rk_x, rv_x, rg_x, cq_x, ckv_x, kr_x, gl_x) = split_cols(hx @ p['w_in'], IN_SPLITS)
    (conv_c, z_c, xbc_c, dt_c, rq_c, rk_c, rv_c, rg_c, cq_c, ckv_c, kr_c, gl_c) = split_cols(hc @ p['w_in'], IN_SPLITS)
    zs = jnp.zeros((b, SSM_HEADS, SSM_STATE, SSM_HEAD_DIM), f32)
    zr = jnp.zeros((b, RET_HEADS, RET_QK_DIM, RET_V_DIM), f32)
    ssm_yc, ssm_hc = ssm_mixer(z_c, xbc_c, dt_c, p, (zs, zs), need_ctx_out)
    ret_yc, ret_hc = retention_mixer(rq_c, rk_c, rv_c, rg_c, p, (zr, zr), None, need_ctx_out)
    k_c, v_c = mla_kv(ckv_c, kr_c, p, None)
    ssm_yx, _ = ssm_mixer(z_x, xbc_x, dt_x, p, ssm_hc, True)
    ret_yx, _ = retention_mixer(rq_x, rk_x, rv_x, rg_x, p, ret_hc, rope_ret, True)
    k_x, v_x = mla_kv(ckv_x, kr_x, p, rope_mla)
    att_x = block_attention(mla_q(cq_x, p, rope_mla), jnp.concatenate([k_c, k_x], axis=1),
                            jnp.concatenate([v_c, v_x], axis=1))
    out_x = merge_branches((conv_module(conv_x, p), ssm_yx, ret_yx, att_x), gl_x, p)
    if not need_ctx_out:
        return out_x, None
    att_c = block_attention(mla_q(cq_c, p, None), k_c, v_c)
    out_c = merge_branches((conv_module(conv_c, p), ssm_yc, ret_yc, att_c), gl_c, p)
    return out_x, out_c


def swiglu(h, p):
    a, g = jnp.split(h @ p['w_ffn_in'], 2, axis=-1)
    return (jax.nn.silu(g) * a) @ p['w_ffn_out']


def trunk_layer(xs, cs, c, c_ctx, p, rope_ret, rope_mla, last):
    mod_x = jnp.split((jax.nn.silu(c) @ p['w_ada'] + p['b_ada'])[:, None, :], 6, axis=-1)
    mod_c = jnp.split((jax.nn.silu(c_ctx) @ p['w_ada'] + p['b_ada'])[None, None, :], 6, axis=-1)
    hx = modulate(rmsnorm(xs, p['norm1_g']), mod_x[0], mod_x[1])
    hc = modulate(rmsnorm(cs, p['norm1_g']), mod_c[0], mod_c[1])
    mx, mc = token_mixers(hx, hc, p, rope_ret, rope_mla, not last)
    xs = xs + mod_x[2] * mx
    xs = xs + mod_x[5] * swiglu(modulate(rmsnorm(xs, p['norm2_g']), mod_x[3], mod_x[4]), p)
    if last:
        return xs, cs
    cs = cs + mod_c[2] * mc
    cs = cs + mod_c[5] * swiglu(modulate(rmsnorm(cs, p['norm2_g']), mod_c[3], mod_c[4]), p)
    return xs, cs


def setup_inputs(seed: int = 0) -> dict:
    key = jax.random.key(seed)
    k = jax.random.split(key, 32)
    f32 = jnp.float32
    L, D = DEPTH, D_MODEL

    def nrm(i, shape, scale):
        return jax.random.normal(k[i], shape, f32) * scale

    def gain(i, shape):
        return 1.0 + nrm(i, shape, 0.02)

    dt0 = jnp.exp(jax.random.uniform(k[15], (L, 2, SSM_HEADS), f32, math.log(1e-3), math.log(1e-1)))
    gamma = 1.0 - 2.0 ** (-5.0 - jnp.arange(RET_HEADS, dtype=f32))
    return {
        'x': nrm(0, (BATCH, SEQ, D), 1.0),
        'c': nrm(1, (BATCH, D), 1.0),
        'ctx': nrm(2, (BATCH, CTX_LEN, D), 1.0),
        'c_ctx': nrm(3, (D,), 1.0),
        'w_ada': nrm(4, (L, D, 6 * D), 0.5 * D ** -0.5),
        'b_ada': nrm(5, (L, 6 * D), 0.02),
        'norm1_g': gain(6, (L, D)),
        'norm2_g': gain(7, (L, D)),
        'w_in': nrm(8, (L, D, N_IN), D ** -0.5),
        'conv_w': nrm(9, (L, CONV_WIDTH, CONV_DIM), CONV_WIDTH ** -0.5),
        'conv_b': nrm(10, (L, CONV_DIM), 0.02),
        'conv_ln_g': gain(11, (L, CONV_DIM)),
        'conv_ln_b': nrm(12, (L, CONV_DIM), 0.02),
        'ssm_conv_w': nrm(13, (L, SSM_CONV, SSM_XBC), SSM_CONV ** -0.5),
        'ssm_conv_b': nrm(14, (L, SSM_XBC), 0.02),
        'ssm_dt_bias': dt0 + jnp.log(-jnp.expm1(-dt0)),
        'ssm_a_log': jnp.log(jax.random.uniform(k[16], (L, 2, SSM_HEADS), f32, 1.0, 16.0)),
        'ssm_d': gain(17, (L, SSM_HEADS)),
        'ssm_norm_g': gain(18, (L, SSM_INNER)),
        'ret_decay': jnp.log(-jnp.log(gamma)) + nrm(19, (L, 2, RET_HEADS), 0.05),
        'ret_gn_g': gain(20, (L, RET_INNER)),
        'ret_gn_b': nrm(21, (L, RET_INNER), 0.02),
        'mla_q_norm_g': gain(22, (L, MLA_Q_RANK)),
        'mla_kv_norm_g': gain(23, (L, MLA_KV_RANK)),
        'mla_w_uq': nrm(24, (L, MLA_Q_RANK, MLA_HEADS * (MLA_NOPE + MLA_ROPE)), MLA_Q_RANK ** -0.5),
        'mla_w_ukv': nrm(25, (L, MLA_KV_RANK, MLA_HEADS * (MLA_NOPE + MLA_V)), MLA_KV_RANK ** -0.5),
        'w_branch': nrm(26, (L, N_BRANCH, BRANCH_DIM, D), BRANCH_DIM ** -0.5),
        'w_out': nrm(27, (L, D, D), D ** -0.5),
        'w_ffn_in': nrm(28, (L, D, 2 * FFN_DIM), D ** -0.5),
        'w_ffn_out': nrm(29, (L, FFN_DIM, D), FFN_DIM ** -0.5),
        'final_norm_g': gain(30, (D,)),
    }


def reference(x, c, ctx, c_ctx, w_ada, b_ada, norm1_g, norm2_g, w_in, conv_w, conv_b, conv_ln_g, conv_ln_b,
              ssm_conv_w, ssm_conv_b, ssm_dt_bias, ssm_a_log, ssm_d, ssm_norm_g, ret_decay, ret_gn_g, ret_gn_b,
              mla_q_norm_g, mla_kv_norm_g, mla_w_uq, mla_w_ukv, w_branch, w_out, w_ffn_in, w_ffn_out, final_norm_g):
    n_lat = x.shape[1]
    rope_ret = axial_rope(n_lat, RET_QK_DIM, x.dtype)
    rope_mla = axial_rope(n_lat, MLA_ROPE, x.dtype)
    xs, cs = x, ctx
    for i in range(DEPTH):
        p = {
            'w_ada': w_ada[i], 'b_ada': b_ada[i], 'norm1_g': norm1_g[i], 'norm2_g': norm2_g[i],
            'w_in': w_in[i], 'conv_w': conv_w[i], 'conv_b': conv_b[i], 'conv_ln_g': conv_ln_g[i],
            'conv_ln_b': conv_ln_b[i], 'ssm_conv_w': ssm_conv_w[i], 'ssm_conv_b': ssm_conv_b[i],
            'ssm_dt_bias': ssm_dt_bias[i], 'ssm_a_log': ssm_a_log[i], 'ssm_d': ssm_d[i],
            'ssm_norm_g': ssm_norm_g[i], 'ret_decay': ret_decay[i], 'ret_gn_g': ret_gn_g[i],
            'ret_gn_b': ret_gn_b[i], 'mla_q_norm_g': mla_q_norm_g[i], 'mla_kv_norm_g': mla_kv_norm_g[i],
            'mla_w_uq': mla_w_uq[i], 'mla_w_ukv': mla_w_ukv[i], 'w_branch': w_branch[i], 'w_out': w_out[i],
            'w_ffn_in': w_ffn_in[i], 'w_ffn_out': w_ffn_out[i],
        }
        xs, cs = trunk_layer(xs, cs, c, c_ctx, p, rope_ret, rope_mla, i == DEPTH - 1)
    return rmsnorm(xs, final_norm_g)
```

```python
import math
import numpy as np
import concourse.bass as bass
import concourse.mybir as mybir
from concourse.bass_utils import run_bass_kernel_spmd

F32 = mybir.dt.float32
BF16 = mybir.dt.bfloat16
I32 = mybir.dt.int32
AF = mybir.ActivationFunctionType
ALU = mybir.AluOpType
AX = mybir.AxisListType


class Trk:
    __slots__ = ("w", "r", "wpe", "name", "excl", "pend")

    def __init__(self, name=""):
        self.w = None
        self.r = {}
        self.wpe = False
        self.name = name
        self.pend = 0
        self.excl = False


class Eng:
    def __init__(self, kb, name, h):
        self.kb = kb
        self.name = name
        self.h = h
        self.sem = kb.nc.alloc_semaphore("sem_" + name)
        self.key = "E_" + name
        kb.sems[self.key] = self.sem
        self.count = 0
        self.known = {}


class KB:
    def __init__(self, nc, n_dma_sems=40):
        self.nc = nc
        self.sems = {}
        self.pe = Eng(self, "pe", nc.tensor)
        self.act = Eng(self, "act", nc.scalar)
        self.dve = Eng(self, "dve", nc.vector)
        self.pool = Eng(self, "pool", nc.gpsimd)
        self.sp = Eng(self, "sp", nc.sync)
        self.engs = [self.pe, self.act, self.dve, self.pool, self.sp]
        self.dsems = []
        for i in range(n_dma_sems):
            s = nc.alloc_semaphore("dsem%d" % i)
            k = "D%d" % i
            self.sems[k] = s
            self.dsems.append([k, s, 0])
        self.dnext = 0
        self.psems = []
        for i in range(24):
            s_ = nc.alloc_semaphore("psem%d" % i)
            k = "Q%d" % i
            self.sems[k] = s_
            self.psems.append([k, s_, 0])
        self.pnext = 0
        self.n_inst = 0
        self.uid = 0
        self.deferred = []

    def sb(self, name, shape, dt=F32):
        self.uid += 1
        name = "%s_u%d" % (name, self.uid)
        if getattr(self, "stack", None) is not None:
            return self.stack.enter_context(self.nc.sbuf_tensor(name, list(shape), dt)).ap()
        return self.nc.alloc_sbuf_tensor(name, list(shape), dt).ap()

    def ps(self, name, shape, dt=F32):
        return self.nc.alloc_psum_tensor(name, list(shape), dt).ap()

    def dram(self, name, shape, dt=F32, kind="Internal"):
        return self.nc.dram_tensor(name, list(shape), dt, kind=kind).ap()

    def trk(self, name=""):
        return Trk(name)

    def _wait(self, eng, evs):
        need = {}
        for k, v in evs:
            if need.get(k, 0) < v:
                need[k] = v
        for k, v in need.items():
            if eng.known.get(k, 0) < v:
                eng.h.wait_ge(self.sems[k], v)
                eng.known[k] = v

    def _deps(self, eng, reads, writes, acc):
        if self.deferred:
            for t in reads:
                if t.pend:
                    self.flush_deferred()
                    break
            else:
                for t in writes:
                    if t.pend:
                        self.flush_deferred()
                        break
        evs = []
        for t in reads:
            if t.w is not None:
                evs.append(t.w)
            if t.excl:
                evs.extend((k, v) for k, v in t.r.items() if k != eng.key)
        for t in writes:
            if t.w is not None and not (acc and t.wpe and eng is self.pe):
                evs.append(t.w)
            evs.extend(t.r.items())
        self._wait(eng, evs)

    def _post(self, ev, reads, writes, is_pe):
        k, v = ev
        for t in reads:
            if t.r.get(k, 0) < v:
                t.r[k] = v
        for t in writes:
            t.w = ev
            t.r = {}
            t.wpe = is_pe

    def op(self, eng, fn, reads=(), writes=(), inc=True, acc=False):
        self._deps(eng, reads, writes, acc)
        inst = fn()
        self.n_inst += 1
        if inc:
            eng.count += 1
            inst.then_inc(eng.sem, 1)
            ev = (eng.key, eng.count)
        else:
            ev = (eng.key, eng.count + 1)
        self._post(ev, reads, writes, eng is self.pe)
        return inst

    def dma(self, out, in_, reads=(), writes=(), q=None, **kw):
        q = q or self.sp
        if q is self.pool:
            d = self.psems[self.pnext]
            self.pnext = (self.pnext + 1) % len(self.psems)
        else:
            d = self.dsems[self.dnext]
            self.dnext = (self.dnext + 1) % len(self.dsems)
        self._deps(q, reads, writes, False)
        if d[2] > 0:
            self._wait(q, [(d[0], d[2])])
        inst = q.h.dma_start(out=out, in_=in_, **kw)
        d[2] += 16
        inst.then_inc(d[1], 16)
        self.n_inst += 1
        ev = (d[0], d[2])
        self._post(ev, reads, writes, False)
        return inst

    def dma_deferred(self, out, in_, reads=(), writes=(), defer=8):
        for t in list(reads) + list(writes):
            t.pend += 1
        self.deferred.append((out, in_, list(reads), list(writes)))
        while len(self.deferred) > defer:
            self._emit_deferred()

    def _emit_deferred(self):
        out, in_, reads, writes = self.deferred.pop(0)
        for t in reads + writes:
            t.pend -= 1
        self.dma(out, in_, reads=reads, writes=writes)

    def flush_deferred(self):
        while self.deferred:
            self._emit_deferred()

    def finish(self, trks):
        self.flush_deferred()
        evs = []
        for t in trks:
            if t.w is not None:
                evs.append(t.w)
        self._wait(self.sp, evs)


def _kb_barrier(self):
    self.flush_deferred()
    evs = [(e.key, e.count) for e in self.engs if e.count > 0]
    evs += [(d[0], d[2]) for d in self.dsems + self.psems if d[2] > 0]
    for e in self.engs:
        self._wait(e, evs)


KB.barrier = _kb_barrier


D = 1024
NCTX = 256
EPS = 1e-6
FFN = 2816
C_CONV, C_Z, C_XBC, C_DT, C_RQ, C_RK, C_RV, C_RG, C_CQ, C_CKV, C_KR, C_GL = (
    0, 1024, 1536, 2560, 2576, 2832, 3088, 3600, 4112, 4496, 4752, 4784)
N_IN = 8880
E_RQS, E_RKS, E_KRS, N_EXT = 4784, 5040, 5296, 5328


def rope_tables(L):
    def tab(nf, nrows):
        inv = 10000.0 ** (-np.arange(nf, dtype=np.float32) / nf)
        rows = np.arange(nrows, dtype=np.float32)[:, None] * inv
        cols = np.arange(64, dtype=np.float32)[:, None] * inv
        return rows.astype(np.float32), cols.astype(np.float32)
    nrows = max(L // 64, 1)
    out = {}
    rr, cc = tab(16, nrows)
    TRc = np.ones((128, nrows), np.float32); TRs = np.zeros((128, nrows), np.float32)
    TCc = np.ones((128, 64), np.float32); TCs = np.zeros((128, 64), np.float32)
    for p in range(128):
        d = p % 64
        f = d % 32
        sgn = -1.0 if d < 32 else 1.0
        if f < 16:
            TRc[p] = np.cos(rr[:, f]); TRs[p] = sgn * np.sin(rr[:, f])
        else:
            TCc[p] = np.cos(cc[:, f - 16]); TCs[p] = sgn * np.sin(cc[:, f - 16])
    out["ret"] = (TRc, TRs, TCc, TCs)
    rr, cc = tab(8, nrows)
    TRc = np.ones((128, nrows), np.float32); TRs = np.zeros((128, nrows), np.float32)
    TCc = np.ones((128, 64), np.float32); TCs = np.zeros((128, 64), np.float32)
    for key, prange in (("mla", range(64, 96)), ("mlk", range(0, 32))):
        TRc = np.ones((128, nrows), np.float32); TRs = np.zeros((128, nrows), np.float32)
        TCc = np.ones((128, 64), np.float32); TCs = np.zeros((128, 64), np.float32)
        for p in prange:
            d = p % 32
            f = d % 16
            sgn = -1.0 if d < 16 else 1.0
            if f < 8:
                TRc[p] = np.cos(rr[:, f]); TRs[p] = sgn * np.sin(rr[:, f])
            else:
                TCc[p] = np.cos(cc[:, f - 8]); TCs[p] = sgn * np.sin(cc[:, f - 8])
        out[key] = (TRc, TRs, TCc, TCs)
    return out


class Ctx:
    pass


def build(L, n_layers=2, debug=False, upto="Z"):
    nc = bass.Bass("TRN2", target_bir_lowering=False)
    kb = KB(nc, n_dma_sems=48)
    T = NCTX + L
    NR = max(L // 64, 1)
    g = Ctx()
    g.nc, g.kb, g.L, g.T = nc, kb, L, T

    def din(name, shape):
        return nc.dram_tensor(name, list(shape), F32, kind="ExternalInput").ap()

    NL = 2
    I = {}
    I["x"] = din("x", [L, D]); I["ctx"] = din("ctx", [NCTX, D]); I["c2"] = din("c2", [2, D])
    for nm, shp in [("w_ada", [NL, D, 6 * D]), ("b_ada", [NL, 6 * D]), ("norm1_g", [NL, D]), ("norm2_g", [NL, D]),
                    ("w_in", [NL, D, N_IN]), ("conv_w", [NL, 31, 512]), ("conv_b", [NL, 512]),
                    ("conv_ln_g", [NL, 512]), ("conv_ln_b", [NL, 512]), ("ssm_conv_w", [NL, 5, 1024]),
                    ("ssm_conv_b", [NL, 1024]), ("ssm_dt_bias", [NL, 16]), ("ssm_a_log", [NL, 16]),
                    ("ssm_d", [NL, 8]), ("ssm_norm_g", [NL, 512]), ("ret_decay", [NL, 8]),
                    ("ret_gn_g", [NL, 512]), ("ret_gn_b", [NL, 512]), ("mla_q_norm_g", [NL, 384]),
                    ("mla_kv_norm_g", [NL, 256]), ("mla_w_uq", [NL, 384, 768]), ("mla_w_ukv", [NL, 256, 1024]),
                    ("w_branch", [NL, 4, 512, D]), ("w_out", [NL, D, D]), ("w_ffn_in", [NL, D, 2 * FFN]),
                    ("w_ffn_out", [NL, FFN, D]), ("final_norm_g", [1, D]),
                    ("t_ret", [128, 2 * NR + 128]), ("t_mla", [128, 2 * NR + 128]), ("t_mlk", [128, 2 * NR + 128])]:
        I[nm] = din(nm, shp)
    out_d = nc.dram_tensor("out", [L, D], F32, kind="ExternalOutput").ap()

    dbg = {}

    def scratch(name, shape, dt=BF16):
        if debug:
            ap = nc.dram_tensor(name, list(shape), dt, kind="ExternalOutput").ap()
            dbg[name] = ap
        else:
            ap = nc.dram_tensor(name, list(shape), dt, kind="Internal").ap()
        return ap, kb.trk(name)

    S = {}
    for nm, shp, dt in [("Win", [128, 8, N_EXT], BF16), ("Wgl", [128, 8, 4096], BF16), ("Wuq", [128, 3, 1536], BF16),
                        ("Wk", [128, 2, 512], BF16), ("Wv", [128, 2, 512], BF16), ("Wb", [128, 16, D], BF16),
                        ("Wout", [128, 8, D], BF16), ("Wf1", [128, 8, 2 * FFN], BF16), ("Wf2", [128, 22, D], BF16),
                        ("uT", [512, T], BF16), ("zs", [T, 512], BF16), ("xbcT", [1024, T], BF16),
                        ("dt", [T, 16], F32), ("la", [T, 16], F32),
                        ("qrT", [256, T], BF16), ("krT", [256, T], BF16), ("kr", [T, 256], BF16),
                        ("rv", [T, 512], BF16), ("rg", [T, 512], BF16),
                        ("qmT", [8, 96, T], BF16), ("kmT", [512, T], BF16), ("kropeT", [32, T], BF16),
                        ("vm", [T, 512], BF16),
                        ("ssx", [T, 512], BF16), ("ssB", [T, 256], BF16), ("ssBT", [256, T], BF16),
                        ("ssCT", [256, T], BF16), ("yf", [T, 512], F32), ("ryf", [T, 512], F32),
                        ("br0T", [512, T], BF16), ("br1T", [512, T], BF16), ("br2T", [512, T], BF16),
                        ("br3T", [512, T], BF16), ("xs", [T, D], F32)]:
        S[nm] = scratch(nm, shp, dt)
    for nm, kc, n_ in (("Win", 8, N_EXT), ("Wgl", 8, 4096), ("Wb", 16, D), ("Wout", 8, D), ("Wf1", 8, 2 * FFN), ("Wf2", 22, D)):
        S[nm + "_pk"] = (nc.dram_tensor(nm + "_pk", [128, kc * n_], BF16, kind="Internal").ap(), kb.trk(nm + "_pk"))
    g.S, g.I = S, I

    def V(fn, r=(), w=(), **k): return kb.op(kb.dve, fn, r, w, **k)
    def A(fn, r=(), w=(), **k): return kb.op(kb.act, fn, r, w, **k)
    def P(fn, r=(), w=(), **k): return kb.op(kb.pe, fn, r, w, **k)
    def G(fn, r=(), w=(), **k): return kb.op(kb.pool, fn, r, w, **k)
    g.V, g.A, g.P, g.G = V, A, P, G
    rr = [0]

    def VA(fnv, fna, r=(), w=()):
        rr[0] += 1
        if rr[0] % 2:
            return V(fnv, r, w)
        return A(fna, r, w)

    psb = [(kb.ps("psb%d" % i, [128, 512], F32), kb.trk()) for i in range(8)]
    for _, t_ in psb:
        t_.excl = True
    pi = [0]

    g.ps_skip = set()
    g.psb = psb

    def psum():
        pi[0] = (pi[0] + 1) % 8
        while pi[0] in g.ps_skip:
            pi[0] = (pi[0] + 1) % 8
        return psb[pi[0]]
    g.psum = psum

    class Pool:
        def __init__(s, name, shape, dt, n):
            s.tiles = [(kb.sb("%s%d" % (name, i), shape, dt), kb.trk()) for i in range(n)]
            s.i = 0

        def get(s):
            s.i = (s.i + 1) % len(s.tiles)
            return s.tiles[s.i]

    ident_b = kb.sb("ident_b", [128, 128], BF16); t_c = kb.trk()
    ones_b = kb.sb("ones_b", [128, 128], BF16)
    ones_f = kb.sb("ones_f", [128, 128], F32)
    Uf = kb.sb("Uf", [128, 128], F32)
    Lf = kb.sb("Lf", [128, 128], F32)
    G(lambda: nc.gpsimd.memset(ones_b, 1.0), w=[t_c])
    G(lambda: nc.gpsimd.memset(ones_f, 1.0), w=[t_c])
    G(lambda: nc.gpsimd.affine_select(ident_b, ones_b, pattern=[[-1, 128]], compare_op=ALU.is_equal, fill=0.0,
                                      base=0, channel_multiplier=1), r=[t_c], w=[t_c])
    G(lambda: nc.gpsimd.affine_select(Uf, ones_f, pattern=[[1, 128]], compare_op=ALU.is_ge, fill=0.0,
                                      base=0, channel_multiplier=-1), r=[t_c], w=[t_c])
    G(lambda: nc.gpsimd.affine_select(Lf, ones_f, pattern=[[-1, 128]], compare_op=ALU.is_ge, fill=0.0,
                                      base=0, channel_multiplier=1), r=[t_c], w=[t_c])
    g.ident_b, g.ones_b, g.ones_f, g.Uf, g.Lf, g.t_c = ident_b, ones_b, ones_f, Uf, Lf, t_c
    g.VA = VA
    tab_ret = kb.sb("tab_ret", [128, 2 * NR + 128], F32)
    tab_mla = kb.sb("tab_mla", [128, 2 * NR + 128], F32)
    kb.dma(tab_ret, I["t_ret"], writes=[t_c])
    kb.dma(tab_mla, I["t_mla"], writes=[t_c])
    tab_mlk = kb.sb("tab_mlk", [128, 2 * NR + 128], F32)
    kb.dma(tab_mlk, I["t_mlk"], writes=[t_c])
    g.tab_mlk = tab_mlk
    g.tab_ret, g.tab_mla = tab_ret, tab_mla
    g.out_d = out_d
    g.t_out = kb.trk()
    fin_g = kb.sb("fin_g", [128, D], F32)
    g.fin_g = fin_g
    kb.dma(fin_g, I["final_norm_g"][0].partition_broadcast(128), writes=[t_c])

    tiles = [(0, NCTX, True, 0)]
    for t in range(L // 512):
        tiles.append((NCTX + t * 512, 512, False, t * 512))
    g.tiles = tiles

    modc = kb.sb("modc", [128, 6, 8, 2], F32); t_modc = kb.trk()
    modr = kb.sb("modr", [128, 2, 2, D], F32); t_modr = kb.trk()
    G1c = kb.sb("G1c", [128, 8, 2], F32); G2c = kb.sb("G2c", [128, 8, 2], F32); t_gc = kb.trk()
    colv = kb.sb("colv", [128, 64], F32); t_colv = kb.trk()
    g.modc, g.modr, g.G1c, g.G2c = modc, modr, G1c, G2c

    def coldma(dst, v, k, t_dst):
        for j_ in range(k):
            kb.dma(dst[:, j_:j_ + 1], v[j_ * 128:(j_ + 1) * 128].rearrange("(p o) -> p o", o=1), writes=[t_dst])
    g.coldma = coldma

    for layer in range(n_layers):
        last = (layer == n_layers - 1)
        li = layer
        g.packed, g.pk_off = {}, {}
        kb.barrier()
        with nc.sbuf_tensor("w_stg%d" % li, [128, 3, 2048], F32) as stg_t, nc.sbuf_tensor("w_stb%d" % li, [128, 3, 2048], BF16) as stb_t, \
                nc.sbuf_tensor("w_rs%d" % li, [128, 8], F32) as rs_t:
            stg, stb, rs = stg_t.ap(), stb_t.ap(), rs_t.ap()
            t_stg = [kb.trk() for _ in range(3)]; t_stb = [kb.trk() for _ in range(3)]; t_rs = kb.trk()
            wi = [0]

            def prep(src, dst, t_dst, dc0, n, rowscale=None, mul=None, swap=None):
                i = wi[0] % 3; wi[0] += 1
                kb.dma(stg[:, i, :n], src, writes=[t_stg[i]])
                o = stb[:, i, :n]; s_ = stg[:, i, :n]
                if swap is not None:
                    hd = swap
                    ov = o.rearrange("p (h two e) -> p h two e", two=2, e=hd)
                    sv = s_.rearrange("p (h two e) -> p h two e", two=2, e=hd)
                    m_ = 1.0 if mul is None else mul
                    V(lambda: nc.vector.tensor_scalar(ov[:, :, 0, :], sv[:, :, 1, :], m_, None, ALU.mult),
                      r=[t_stg[i]], w=[t_stb[i]])
                    V(lambda: nc.vector.tensor_scalar(ov[:, :, 1, :], sv[:, :, 0, :], m_, None, ALU.mult),
                      r=[t_stg[i]], w=[t_stb[i]])
                elif rowscale is not None:
                    V(lambda: nc.vector.tensor_scalar(o, s_, rowscale, None, ALU.mult), r=[t_stg[i], t_rs], w=[t_stb[i]])
                elif mul is not None:
                    V(lambda: nc.vector.tensor_scalar(o, s_, mul, None, ALU.mult), r=[t_stg[i]], w=[t_stb[i]])
                else:
                    VA(lambda: nc.vector.tensor_copy(o, s_), lambda: nc.scalar.copy(o, s_), r=[t_stg[i]], w=[t_stb[i]])
                kb.dma(dst, o, reads=[t_stb[i]], writes=[t_dst])

            def prep_mat(src2d, K, N, dname, dk0=0, dc0=0, sc0=0, **kw):
                dst, t_dst = S[dname]
                for kc in range(K // 128):
                    for c0 in range(0, N, 2048):
                        n = min(2048, N - c0)
                        prep(src2d[kc * 128:(kc + 1) * 128, sc0 + c0:sc0 + c0 + n],
                             dst[:, dk0 + kc, dc0 + c0:dc0 + c0 + n], t_dst, 0, n, **kw)

            w_in = I["w_in"][li]
            prep_mat(w_in, D, C_RK, "Win")
            prep_mat(w_in, D, 256, "Win", dc0=C_RK, sc0=C_RK, mul=0.125)
            prep_mat(w_in, D, C_GL - C_RV, "Win", dc0=C_RV, sc0=C_RV)
            prep_mat(w_in, D, 256, "Win", dc0=E_RQS, sc0=C_RQ, swap=32)
            prep_mat(w_in, D, 256, "Win", dc0=E_RKS, sc0=C_RK, swap=32, mul=0.125)
            prep_mat(w_in, D, 32, "Win", dc0=E_KRS, sc0=C_KR, swap=16)
            prep_mat(w_in, D, 4096, "Wgl", sc0=C_GL)
            coldma(rs[:, 0:3], I["mla_q_norm_g"][li], 3, t_rs)
            coldma(rs[:, 3:5], I["mla_kv_norm_g"][li], 2, t_rs)
            uq = I["mla_w_uq"][li]
            dst, t_dst = S["Wuq"]
            for kc in range(3):
                i = wi[0] % 3; wi[0] += 1
                kb.dma(stg[:, i, :768], uq[kc * 128:(kc + 1) * 128, :], writes=[t_stg[i]])
                o = stb[:, i, :1536]; s_ = stg[:, i, :768]
                V(lambda: nc.vector.tensor_scalar(o[:, 0:768], s_, rs[:, kc:kc + 1], None, ALU.mult),
                  r=[t_stg[i], t_rs], w=[t_stb[i]])
                ov = o[:, 768:1536].rearrange("p (h e) -> p h e", e=96)
                sv = o[:, 0:768].rearrange("p (h e) -> p h e", e=96)
                V(lambda: nc.vector.tensor_copy(ov[:, :, 0:64], sv[:, :, 0:64]), r=[t_stb[i]], w=[t_stb[i]])
                V(lambda: nc.vector.tensor_copy(ov[:, :, 64:80], sv[:, :, 80:96]), r=[t_stb[i]], w=[t_stb[i]])
                V(lambda: nc.vector.tensor_copy(ov[:, :, 80:96], sv[:, :, 64:80]), r=[t_stb[i]], w=[t_stb[i]])
                kb.dma(dst[:, kc, :], o, reads=[t_stb[i]], writes=[t_dst])
            ukv = I["mla_w_ukv"][li]
            for kc in range(2):
                i = wi[0] % 3; wi[0] += 1
                kb.dma(stg[:, i, :1024], ukv[kc * 128:(kc + 1) * 128, :], writes=[t_stg[i]])
                o = stb[:, i, :1024]; s_ = stg[:, i, :1024]
                sv = s_.rearrange("p (h two e) -> p h two e", two=2, e=64)
                ov = o.rearrange("p (two h e) -> p two h e", two=2, e=64)
                V(lambda: nc.vector.tensor_scalar(ov[:, 0], sv[:, :, 0, :], rs[:, 3 + kc:4 + kc], None, ALU.mult),
                  r=[t_stg[i], t_rs], w=[t_stb[i]])
                V(lambda: nc.vector.tensor_scalar(ov[:, 1], sv[:, :, 1, :], rs[:, 3 + kc:4 + kc], None, ALU.mult),
                  r=[t_stg[i], t_rs], w=[t_stb[i]])
                kb.dma(S["Wk"][0][:, kc, :], o[:, 0:512], reads=[t_stb[i]], writes=[S["Wk"][1]])
                kb.dma(S["Wv"][0][:, kc, :], o[:, 512:1024], reads=[t_stb[i]], writes=[S["Wv"][1]])
            for b in range(4):
                prep_mat(I["w_branch"][li, b], 512, D, "Wb", dk0=b * 4)
            prep_mat(I["w_out"][li], D, D, "Wout")
            prep_mat(I["w_ffn_in"][li], D, 2 * FFN, "Wf1")
            prep_mat(I["w_ffn_out"][li], FFN, D, "Wf2")
        if upto == "W":
            break
        kb.barrier()
        with nc.sbuf_tensor("m_w%d" % li, [128, 8, 1024], F32) as mw_t, nc.sbuf_tensor("m_cs%d" % li, [128, 8, 2], F32) as cs_t, \
                nc.sbuf_tensor("m_rep%d" % li, [128, 8, 2, 128], F32) as rep_t, nc.sbuf_tensor("m_bc%d" % li, [128, 48], F32) as bc_t, \
                nc.sbuf_tensor("m_br%d" % li, [128, 2, D], F32) as br_t, nc.sbuf_tensor("m_ng%d" % li, [128, 16], F32) as ng_t:
            mw, cs, rep, bc, br, ng = mw_t.ap(), cs_t.ap(), rep_t.ap(), bc_t.ap(), br_t.ap(), ng_t.ap()
            t_mw, t_cs, t_rep, t_bc, t_br, t_ng = [kb.trk() for _ in range(6)]
            for j in range(2):
                for k_ in range(8):
                    kb.dma(cs[:, k_, j:j + 1], I["c2"][j, k_ * 128:(k_ + 1) * 128].rearrange("(p o) -> p o", o=1), writes=[t_cs])
            A(lambda: nc.scalar.activation(cs, cs, AF.Silu), r=[t_cs], w=[t_cs])
            V(lambda: nc.vector.tensor_copy(rep, cs.unsqueeze(3).broadcast_to([128, 8, 2, 128])), r=[t_cs], w=[t_rep])
            coldma(bc, I["b_ada"][li], 48, t_bc)
            kb.dma(br[:, 0, :], I["b_ada"][li, 2 * D:3 * D].partition_broadcast(128), writes=[t_br])
            kb.dma(br[:, 1, :], I["b_ada"][li, 5 * D:6 * D].partition_broadcast(128), writes=[t_br])
            coldma(ng[:, 0:8], I["norm1_g"][li], 8, t_ng)
            coldma(ng[:, 8:16], I["norm2_g"][li], 8, t_ng)
            for m in range(6):
                kb.dma(mw, I["w_ada"][li][:, m * 1024:(m + 1) * 1024].rearrange("(k p) n -> p k n", p=128), writes=[t_mw])
                for n8 in range(8):
                    ps_, tp = psum()
                    for k in range(8):
                        P(lambda: nc.tensor.matmul(ps_[:, 0:2], mw[:, k, n8 * 128:(n8 + 1) * 128], cs[:, k, :],
                                                   start=(k == 0), stop=(k == 7)), r=[t_mw, t_cs], w=[tp], acc=(k > 0), inc=(k == 7))
                    V(lambda: nc.vector.tensor_scalar(modc[:, m, n8, :], ps_[:, 0:2], bc[:, m * 8 + n8:m * 8 + n8 + 1], None, ALU.add),
                      r=[tp, t_bc], w=[t_modc])
                if m in (2, 5):
                    gi = 0 if m == 2 else 1
                    for j in range(2):
                        for hf in range(2):
                            ps_, tp = psum()
                            for k in range(8):
                                P(lambda: nc.tensor.matmul(ps_, rep[:, k, j, :], mw[:, k, hf * 512:(hf + 1) * 512],
                                                           start=(k == 0), stop=(k == 7)), r=[t_mw, t_rep], w=[tp], acc=(k > 0), inc=(k == 7))
                            V(lambda: nc.vector.tensor_tensor(modr[:, j, gi, hf * 512:(hf + 1) * 512], ps_,
                                                              br[:, gi, hf * 512:(hf + 1) * 512], ALU.add), r=[tp, t_br], w=[t_modr])
            for (Gc, ms, o8) in ((G1c, 1, 0), (G2c, 4, 8)):
                V(lambda: nc.vector.tensor_scalar(Gc, modc[:, ms], 1.0, None, ALU.add), r=[t_modc], w=[t_gc])
                V(lambda: nc.vector.tensor_tensor(Gc, Gc, ng[:, o8:o8 + 8].unsqueeze(2).broadcast_to([128, 8, 2]), ALU.mult),
                  r=[t_ng, t_gc], w=[t_gc])
        g.t_modc, g.t_modr, g.t_gc = t_modc, t_modr, t_gc
        if upto == "M":
            break
        from contextlib import ExitStack
        done = False
        for ph in "ABCDEF":
            fn = {"A": phase_A, "B": phase_B, "C": phase_C, "D": phase_D, "E": phase_E, "F": phase_F}[ph]
            kb.barrier()
            with ExitStack() as st:
                kb.stack = st
                fn(g, li) if ph == 'A' else fn(g, li, last)
                kb.barrier()
            kb.stack = None
            if upto == ph:
                done = True
                break
        if done:
            break

    kb.barrier()
    if debug:
        for nm, ap in (("d_modc", modc), ("d_modr", modr), ("d_G1c", G1c)):
            d_ = nc.dram_tensor(nm, list(ap.shape), F32, kind="ExternalOutput").ap()
            kb.dma(d_, ap, reads=[t_modc, t_modr, t_gc], writes=[kb.trk()])
            dbg[nm] = d_
    kb.barrier()
    kb.finish([g.t_out])
    return nc, dbg


ASTOP = 99


class RPool:
    def __init__(s, kb, name, shape, dt, n):
        s.tiles = [(kb.sb("%s%d" % (name, i), shape, dt), kb.trk()) for i in range(n)]
        s.i = 0

    def get(s):
        s.i = (s.i + 1) % len(s.tiles)
        return s.tiles[s.i]


def mk_common(g):
    kb = g.kb
    c = Ctx()
    c.xt = RPool(kb, "c_xt", [128, 4, D], F32, 1)
    c.xn = RPool(kb, "c_xn", [128, 4, D], BF16, 1)
    c.hT = RPool(kb, "c_hT", [128, 8, 512], BF16, 2)
    c.ss = RPool(kb, "c_ss", [128, 8], F32, 2)
    c.junk = RPool(kb, "c_junk", [128, D], BF16, 1)
    c.wt = RPool(kb, "c_wt", [128, 8, 512], BF16, 4)
    c.stb = RPool(kb, "c_stb", [128, 512], BF16, 20)
    c.stf = RPool(kb, "c_stf", [128, 512], F32, 8)
    return c


def norm_tile(g, c, src, t_src, n, Gc, Sc, j, xt_pair=None):
    kb, nc, V, A, P = g.kb, g.nc, g.V, g.A, g.P
    nb = n // 128
    if xt_pair is None:
        xt, t_xt = c.xt.get()
        kb.dma(xt[:, :nb, :], src.rearrange("(b p) d -> p b d", p=128), reads=[t_src], writes=[t_xt])
    else:
        xt, t_xt = xt_pair
    ss, t_ss = c.ss.get()
    junk, t_junk = c.junk.get()
    for b in range(nb):
        A(lambda: nc.scalar.activation(junk, xt[:, b, :], AF.Square, accum_out=ss[:, b:b + 1]), r=[t_xt], w=[t_junk, t_ss])
    V(lambda: nc.vector.tensor_scalar(ss[:, 0:nb], ss[:, 0:nb], 1.0 / D, EPS, ALU.mult, ALU.add), r=[t_ss], w=[t_ss])
    A(lambda: nc.scalar.activation(ss[:, 0:nb], ss[:, 0:nb], AF.Sqrt), r=[t_ss], w=[t_ss])
    V(lambda: nc.vector.reciprocal(ss[:, 0:nb], ss[:, 0:nb]), r=[t_ss], w=[t_ss])
    xn, t_xn = c.xn.get()
    for b in range(nb):
        V(lambda: nc.vector.tensor_scalar(xn[:, b, :], xt[:, b, :], ss[:, b:b + 1], None, ALU.mult), r=[t_xt, t_ss], w=[t_xn])
    hT, t_hT = c.hT.get()
    for k in range(8):
        ps_, tp = g.psum()
        psv = ps_.bitcast(BF16)
        for b in range(nb):
            P(lambda: nc.tensor.transpose(psv[:, b * 128:(b + 1) * 128], xn[:, b, k * 128:(k + 1) * 128], g.ident_b),
              r=[t_xn, g.t_c], w=[tp], acc=(b > 0), inc=(b == nb - 1))
        V(lambda: nc.vector.tensor_scalar(hT[:, k, :n], psv[:, :n], Gc[:, k, j:j + 1], Sc[:, k, j:j + 1], ALU.mult, ALU.add),
          r=[tp, g.t_gc, g.t_modc], w=[t_hT])
    return hT, t_hT, xt, t_xt, ss, t_ss


def wload(g, c, Wname, kc0, KC, c0, w, pool=None):
    W, t_W = g.S[Wname]
    Wp, t_Wp = g.S[Wname + "_pk"]
    key = (Wname, kc0, KC, c0, w)
    off = g.packed.get(key)
    if off is None:
        off = g.pk_off.get(Wname, 0)
        g.pk_off[Wname] = off + KC * w
        g.packed[key] = off
        g.kb.dma(Wp[:, off:off + KC * w].rearrange("p (k n) -> p k n", n=w), W[:, kc0:kc0 + KC, c0:c0 + w],
                 reads=[t_W], writes=[t_Wp])
    wt, t_w = (pool or c.wt).get()
    flat = wt.rearrange("p k n -> p (k n)")
    g.kb.dma(flat[:, :KC * w], Wp[:, off:off + KC * w], reads=[t_Wp], writes=[t_w])
    return flat[:, :KC * w].rearrange("p (k n) -> p k n", n=w), t_w


def fm_chunks(g, c, Wname, KC, c0, ncols, hT, t_h, n, kc0=0, pool=None, msz=128):
    nc, P = g.nc, g.P
    for cc in range(c0, c0 + ncols, 512):
        w = min(512, c0 + ncols - cc)
        wt, t_w = wload(g, c, Wname, kc0, KC, cc, w, pool)
        for j in range(0, w, msz):
            m = min(msz, w - j)
            ps_, tp = g.psum()
            for k in range(KC):
                P(lambda: nc.tensor.matmul(ps_[:m, :n], wt[:, k, j:j + m], hT[:, k, :n], start=(k == 0), stop=(k == KC - 1)),
                  r=[t_w, t_h], w=[tp], acc=(k > 0), inc=(k == KC - 1))
            yield (cc - c0 + j, m, ps_, tp)


def tm_chunks(g, c, Wname, KC, c0, ncols, hT, t_h, n, kc0=0, pool=None):
    nc, P = g.nc, g.P
    for cc in range(c0, c0 + ncols, 512):
        w = min(512, c0 + ncols - cc)
        wt, t_w = wload(g, c, Wname, kc0, KC, cc, w, pool)
        for b in range(n // 128):
            ps_, tp = g.psum()
            for k in range(KC):
                P(lambda: nc.tensor.matmul(ps_[:, :w], hT[:, k, b * 128:(b + 1) * 128], wt[:, k, :w], start=(k == 0), stop=(k == KC - 1)),
                  r=[t_w, t_h], w=[tp], acc=(k > 0), inc=(k == KC - 1))
            yield (cc - c0, w, b, ps_, tp)


def store(g, name, dram_ap, sb_ap, t_sb):
    g.kb.dma_deferred(dram_ap, sb_ap, reads=[t_sb], writes=[g.S[name][1]])


def phase_A(g, li):
    kb, nc, V, A, P, G = g.kb, g.nc, g.V, g.A, g.P, g.G
    S, I, L, T = g.S, g.I, g.L, g.T
    NR = max(L // 64, 1)
    c = mk_common(g)
    cos_r = kb.sb("a_cos_r", [128, 512], F32); sin_r = kb.sb("a_sin_r", [128, 512], F32)
    cos_m = kb.sb("a_cos_m", [128, 512], F32); sin_m = kb.sb("a_sin_m", [128, 512], F32); t_tab = kb.trk()
    cos_k = kb.sb("a_cos_k", [128, 512], F32); sin_k = kb.sb("a_sin_k", [128, 512], F32)
    dtb = kb.sb("a_dtb", [128, 32], F32); t_dtb = kb.trk()
    kb.dma(dtb[:, 0:16], I["ssm_dt_bias"][li].partition_broadcast(128), writes=[t_dtb])
    kb.dma(dtb[:, 16:32], I["ssm_a_log"][li].partition_broadcast(128), writes=[t_dtb])
    A(lambda: nc.scalar.activation(dtb[:, 16:32], dtb[:, 16:32], AF.Exp), r=[t_dtb], w=[t_dtb])
    V(lambda: nc.vector.tensor_scalar(dtb[:, 16:32], dtb[:, 16:32], -1.0, None, ALU.mult), r=[t_dtb], w=[t_dtb])
    cq = RPool(kb, "a_cq", [128, 3, 512], BF16, 1)
    sq = RPool(kb, "a_sq", [128, 3, 512], BF16, 1)
    rsb = RPool(kb, "a_rsb", [128, 512], F32, 2)
    wuq = kb.sb("a_wuq", [128, 3, 1536], BF16); t_wuq = kb.trk()
    wk = kb.sb("a_wk", [128, 2, 512], BF16); wv = kb.sb("a_wv", [128, 2, 512], BF16); t_wkv = kb.trk()
    kb.dma(wuq, S["Wuq"][0], reads=[S["Wuq"][1]], writes=[t_wuq])
    kb.dma(wk, S["Wk"][0], reads=[S["Wk"][1]], writes=[t_wkv])
    kb.dma(wv, S["Wv"][0], reads=[S["Wv"][1]], writes=[t_wkv])
    tk_ = RPool(kb, "a_tk", [128, 4, 128], BF16, 6)
    sm = RPool(kb, "a_sm", [128, 16], F32, 2)
    st16 = RPool(kb, "a_st16", [128, 32], F32, 12)
    SC_Q = 96.0 ** -0.5

    for (t0, n, is_ctx, pos0) in g.tiles:
        j = 1 if is_ctx else 0
        nb = n // 128
        src = (I["ctx"] if is_ctx else I["x"][pos0:pos0 + n]) if li == 0 else S["xs"][0][t0:t0 + n]
        t_src = kb.trk() if li == 0 else S["xs"][1]
        hT, t_h, _, _, _, _ = norm_tile(g, c, src, t_src, n, g.G1c, g.modc[:, 0], j)
        if not is_ctx:
            r0 = pos0 // 64
            for (tab, cs_, sn_) in ((g.tab_ret, cos_r, sin_r), (g.tab_mla, cos_m, sin_m), (g.tab_mlk, cos_k, sin_k)):
                V(lambda: nc.vector.tensor_tensor(cs_.rearrange("p (r c) -> p r c", c=64),
                                                  tab[:, r0:r0 + 8].unsqueeze(2).broadcast_to([128, 8, 64]),
                                                  tab[:, 2 * NR:2 * NR + 64].unsqueeze(1).broadcast_to([128, 8, 64]), ALU.mult),
                  r=[g.t_c], w=[t_tab])
                V(lambda: nc.vector.tensor_tensor(sn_.rearrange("p (r c) -> p r c", c=64),
                                                  tab[:, NR + r0:NR + r0 + 8].unsqueeze(2).broadcast_to([128, 8, 64]),
                                                  tab[:, 2 * NR + 64:2 * NR + 128].unsqueeze(1).broadcast_to([128, 8, 64]), ALU.add),
                  r=[g.t_c], w=[t_tab])
        if ASTOP == 0:
            return
        ga = fm_chunks(g, c, "Win", 8, C_CONV, 512, hT, t_h, n)
        gg = fm_chunks(g, c, "Win", 8, C_CONV + 512, 512, hT, t_h, n)
        for (off, m, pa, tpa), (_, _, pg, tpg) in zip(ga, gg):
            sg, t_sg = c.stf.get()
            A(lambda: nc.scalar.activation(sg[:, :n], pg[:, :n], AF.Sigmoid), r=[tpg], w=[t_sg])
            u, t_u = c.stb.get()
            V(lambda: nc.vector.tensor_tensor(u[:, :n], pa[:, :n], sg[:, :n], ALU.mult), r=[tpa, t_sg], w=[t_u])
            store(g, "uT", S["uT"][0][off:off + 128, t0:t0 + n], u[:, :n], t_u)
        if ASTOP == 1:
            return
        for (c0, nm, fn) in ((C_Z, "zs", AF.Silu), (C_RV, "rv", AF.Copy), (C_RG, "rg", AF.Silu)):
            for (off, w, b, ps_, tp) in tm_chunks(g, c, "Win", 8, c0, 512, hT, t_h, n):
                o, t_o = c.stb.get()
                A(lambda: nc.scalar.activation(o, ps_, fn), r=[tp], w=[t_o])
                store(g, nm, S[nm][0][t0 + b * 128:t0 + (b + 1) * 128, :], o, t_o)
        if ASTOP == 2:
            return
        for (off, m, ps_, tp) in fm_chunks(g, c, "Win", 8, C_XBC, 1024, hT, t_h, n):
            o, t_o = c.stb.get()
            g.VA(lambda: nc.vector.tensor_copy(o[:, :n], ps_[:, :n]), lambda: nc.scalar.copy(o[:, :n], ps_[:, :n]), r=[tp], w=[t_o])
            store(g, "xbcT", S["xbcT"][0][off:off + 128, t0:t0 + n], o[:, :n], t_o)
        if ASTOP == 3:
            return
        for (off, w, b, ps_, tp) in tm_chunks(g, c, "Win", 8, C_DT, 16, hT, t_h, n):
            o, t_o = st16.get()
            V(lambda: nc.vector.tensor_tensor(o[:, 0:16], ps_[:, 0:16], dtb[:, 0:16], ALU.add), r=[tp, t_dtb], w=[t_o])
            A(lambda: nc.scalar.activation(o[:, 0:16], o[:, 0:16], AF.Exp), r=[t_o], w=[t_o])
            A(lambda: nc.scalar.activation(o[:, 0:16], o[:, 0:16], AF.Ln, bias=1.0), r=[t_o], w=[t_o])
            V(lambda: nc.vector.tensor_tensor(o[:, 16:32], o[:, 0:16], dtb[:, 16:32], ALU.mult), r=[t_o, t_dtb], w=[t_o])
            store(g, "dt", S["dt"][0][t0 + b * 128:t0 + (b + 1) * 128, :], o[:, 0:16], t_o)
            store(g, "la", S["la"][0][t0 + b * 128:t0 + (b + 1) * 128, :], o[:, 16:32], t_o)
        if ASTOP == 4:
            return
        for (cN, cS_, nm, is_k) in ((C_RQ, E_RQS, "qrT", False), (C_RK, E_RKS, "krT", True)):
            gn = fm_chunks(g, c, "Win", 8, cN, 256, hT, t_h, n)
            gs = fm_chunks(g, c, "Win", 8, cS_, 256, hT, t_h, n) if not is_ctx else None
            for ci in range(2):
                off, m, pn, tpn = next(gn)
                o, t_o = c.stb.get()
                if is_ctx:
                    V(lambda: nc.vector.tensor_copy(o[:, :n], pn[:, :n]), r=[tpn], w=[t_o])
                else:
                    _, _, pw, tpw = next(gs)
                    t1, t_1 = c.stf.get(); t2, t_2 = c.stf.get()
                    V(lambda: nc.vector.tensor_tensor(t1[:, :n], pn[:, :n], cos_r[:, :n], ALU.mult), r=[tpn, t_tab], w=[t_1])
                    V(lambda: nc.vector.tensor_tensor(t2[:, :n], pw[:, :n], sin_r[:, :n], ALU.mult), r=[tpw, t_tab], w=[t_2])
                    G(lambda: nc.gpsimd.tensor_tensor(o[:, :n], t1[:, :n], t2[:, :n], ALU.add), r=[t_1, t_2], w=[t_o])
                store(g, nm, S[nm][0][off:off + 128, t0:t0 + n], o[:, :n], t_o)
                if is_k:
                    ps_, tp = g.psum(); psv = ps_.bitcast(BF16)
                    for b in range(nb):
                        P(lambda: nc.tensor.transpose(psv[:, b * 128:(b + 1) * 128], o[:, b * 128:(b + 1) * 128], g.ident_b),
                          r=[t_o, g.t_c], w=[tp], acc=(b > 0), inc=(b == nb - 1))
                    kt, t_kt = tk_.get()
                    A(lambda: nc.scalar.copy(kt[:, :nb, :], psv[:, :n].rearrange("p (b f) -> p b f", f=128)), r=[tp], w=[t_kt])
                    store(g, "kr", S["kr"][0][t0:t0 + n, off:off + 128].rearrange("(b p) f -> p b f", p=128), kt[:, :nb, :], t_kt)
            if gs is not None:
                for _ in gs:
                    pass
            for _ in gn:
                pass
        if ASTOP == 5:
            return
        cqT, t_cq = cq.get(); sqT, t_sq = sq.get()
        for (off, m, ps_, tp) in fm_chunks(g, c, "Win", 8, C_CQ, 384, hT, t_h, n):
            ci = off // 128
            V(lambda: nc.vector.tensor_copy(cqT[:, ci, :n], ps_[:, :n]), r=[tp], w=[t_cq])
            A(lambda: nc.scalar.activation(sqT[:, ci, :n], ps_[:, :n], AF.Square), r=[tp], w=[t_sq])
        if ASTOP == 51:
            return
        ps_, tp = g.psum()
        for ci in range(3):
            P(lambda: nc.tensor.matmul(ps_[:, :n], g.ones_b, sqT[:, ci, :n], start=(ci == 0), stop=(ci == 2)),
              r=[t_sq, g.t_c], w=[tp], acc=(ci > 0), inc=(ci == 2))
        rq_, t_rq = rsb.get()
        V(lambda: nc.vector.tensor_scalar(rq_[:, :n], ps_[:, :n], 96.0 / 384.0, EPS * 96.0, ALU.mult, ALU.add), r=[tp], w=[t_rq])
        A(lambda: nc.scalar.activation(rq_[:, :n], rq_[:, :n], AF.Sqrt), r=[t_rq], w=[t_rq])
        V(lambda: nc.vector.reciprocal(rq_[:, :n], rq_[:, :n]), r=[t_rq], w=[t_rq])
        if ASTOP == 50:
            return
        for h in range(8):
            pn, tpn = g.psum()
            for ci in range(3):
                P(lambda: nc.tensor.matmul(pn[:96, :n], wuq[:, ci, h * 96:(h + 1) * 96], cqT[:, ci, :n], start=(ci == 0), stop=(ci == 2)),
                  r=[t_wuq, t_cq], w=[tpn], acc=(ci > 0), inc=(ci == 2))
            o, t_o = c.stb.get()
            if is_ctx:
                V(lambda: nc.vector.tensor_tensor(o[:96, :n], pn[:96, :n], rq_[:96, :n], ALU.mult), r=[tpn, t_rq], w=[t_o])
            else:
                pw, tpw = g.psum()
                for ci in range(3):
                    P(lambda: nc.tensor.matmul(pw[:96, :n], wuq[:, ci, 768 + h * 96:768 + (h + 1) * 96], cqT[:, ci, :n],
                                               start=(ci == 0), stop=(ci == 2)), r=[t_wuq, t_cq], w=[tpw], acc=(ci > 0), inc=(ci == 2))
                t1, t_1 = c.stf.get(); t2, t_2 = c.stf.get()
                V(lambda: nc.vector.tensor_tensor(t1[:96, :n], pn[:96, :n], cos_m[:96, :n], ALU.mult), r=[tpn, t_tab], w=[t_1])
                V(lambda: nc.vector.tensor_tensor(t2[:96, :n], pw[:96, :n], sin_m[:96, :n], ALU.mult), r=[tpw, t_tab], w=[t_2])
                G(lambda: nc.gpsimd.tensor_tensor(t1[:96, :n], t1[:96, :n], t2[:96, :n], ALU.add), r=[t_1, t_2], w=[t_1])
                V(lambda: nc.vector.tensor_tensor(o[:96, :n], t1[:96, :n], rq_[:96, :n], ALU.mult), r=[t_1, t_rq], w=[t_o])
            store(g, "qmT", S["qmT"][0][h, :, t0:t0 + n], o[:96, :n], t_o)
        if ASTOP == 6:
            return
        ckT, t_ck = cq.get(); sk, t_sk = sq.get()
        for (off, m, ps_, tp) in fm_chunks(g, c, "Win", 8, C_CKV, 256, hT, t_h, n):
            ci = off // 128
            V(lambda: nc.vector.tensor_copy(ckT[:, ci, :n], ps_[:, :n]), r=[tp], w=[t_ck])
            A(lambda: nc.scalar.activation(sk[:, ci, :n], ps_[:, :n], AF.Square), r=[tp], w=[t_sk])
        ps_, tp = g.psum()
        for ci in range(2):
            P(lambda: nc.tensor.matmul(ps_[:, :n], g.ones_b, sk[:, ci, :n], start=(ci == 0), stop=(ci == 1)),
              r=[t_sk, g.t_c], w=[tp], acc=(ci > 0), inc=(ci == 1))
        rk_, t_rk = rsb.get()
        V(lambda: nc.vector.tensor_scalar(rk_[:, :n], ps_[:, :n], 1.0 / 256.0, EPS, ALU.mult, ALU.add), r=[tp], w=[t_rk])
        A(lambda: nc.scalar.activation(rk_[:, :n], rk_[:, :n], AF.Sqrt), r=[t_rk], w=[t_rk])
        V(lambda: nc.vector.reciprocal(rk_[:, :n], rk_[:, :n]), r=[t_rk], w=[t_rk])
        for hc in range(4):
            pn, tpn = g.psum()
            for ci in range(2):
                P(lambda: nc.tensor.matmul(pn[:, :n], wk[:, ci, hc * 128:(hc + 1) * 128], ckT[:, ci, :n], start=(ci == 0), stop=(ci == 1)),
                  r=[t_wkv, t_ck], w=[tpn], acc=(ci > 0), inc=(ci == 1))
            o, t_o = c.stb.get()
            V(lambda: nc.vector.tensor_tensor(o[:, :n], pn[:, :n], rk_[:, :n], ALU.mult), r=[tpn, t_rk], w=[t_o])
            store(g, "kmT", S["kmT"][0][hc * 128:(hc + 1) * 128, t0:t0 + n], o[:, :n], t_o)
        if ASTOP == 7:
            return
        smt, t_sm = sm.get()
        ps_, tp = g.psum()
        for b in range(nb):
            for ci in range(2):
                P(lambda: nc.tensor.matmul(ps_[:, b:b + 1], sk[:, ci, b * 128:(b + 1) * 128], g.ones_b[:, 0:1], start=(ci == 0), stop=(ci == 1)),
                  r=[t_sk, g.t_c], w=[tp], acc=(ci > 0 or b > 0), inc=(ci == 1 and b == nb - 1))
        V(lambda: nc.vector.tensor_scalar(smt[:, :nb], ps_[:, :nb], 1.0 / 256.0, EPS, ALU.mult, ALU.add), r=[tp], w=[t_sm])
        A(lambda: nc.scalar.activation(smt[:, :nb], smt[:, :nb], AF.Sqrt), r=[t_sm], w=[t_sm])
        V(lambda: nc.vector.reciprocal(smt[:, :nb], smt[:, :nb]), r=[t_sm], w=[t_sm])
        for b in range(nb):
            pn, tpn = g.psum()
            for ci in range(2):
                P(lambda: nc.tensor.matmul(pn, ckT[:, ci, b * 128:(b + 1) * 128], wv[:, ci, :], start=(ci == 0), stop=(ci == 1)),
                  r=[t_wkv, t_ck], w=[tpn], acc=(ci > 0), inc=(ci == 1))
            o, t_o = c.stb.get()
            A(lambda: nc.scalar.activation(o, pn, AF.Identity, scale=smt[:, b:b + 1]), r=[tpn, t_sm], w=[t_o])
            store(g, "vm", S["vm"][0][t0 + b * 128:t0 + (b + 1) * 128, :], o, t_o)
        if ASTOP == 8:
            return
        gn = fm_chunks(g, c, "Win", 8, C_KR, 32, hT, t_h, n)
        off, m, pn, tpn = next(gn)
        o, t_o = c.stb.get()
        if is_ctx:
            V(lambda: nc.vector.tensor_copy(o[:32, :n], pn[:32, :n]), r=[tpn], w=[t_o])
        else:
            gs = fm_chunks(g, c, "Win", 8, E_KRS, 32, hT, t_h, n)
            _, _, pw, tpw = next(gs)
            t1, t_1 = c.stf.get(); t2, t_2 = c.stf.get()
            V(lambda: nc.vector.tensor_tensor(t1[:32, :n], pn[:32, :n], cos_k[:32, :n], ALU.mult), r=[tpn, t_tab], w=[t_1])
            V(lambda: nc.vector.tensor_tensor(t2[:32, :n], pw[:32, :n], sin_k[:32, :n], ALU.mult), r=[tpw, t_tab], w=[t_2])
            G(lambda: nc.gpsimd.tensor_tensor(o[:32, :n], t1[:32, :n], t2[:32, :n], ALU.add), r=[t_1, t_2], w=[t_o])
            for _ in gs:
                pass
        for _ in gn:
            pass
        store(g, "kropeT", S["kropeT"][0][:, t0:t0 + n], o[:32, :n], t_o)


def phase_B(g, li, last=False):
    kb, nc, V, A, P, G = g.kb, g.nc, g.V, g.A, g.P, g.G
    S, I, L, T = g.S, g.I, g.L, g.T
    cwT = kb.sb("b_cwT", [128, 4, 31], F32); t_cw = kb.trk()
    for tap in range(31):
        for j in range(4):
            kb.dma(cwT[:, j, tap:tap + 1], I["conv_w"][li, tap, j * 128:(j + 1) * 128].rearrange("(p o) -> p o", o=1), writes=[t_cw])
    vec = kb.sb("b_vec", [128, 12], F32); t_vec = kb.trk()
    g.coldma(vec[:, 0:4], I["conv_b"][li], 4, t_vec)
    g.coldma(vec[:, 4:8], I["conv_ln_g"][li], 4, t_vec)
    g.coldma(vec[:, 8:12], I["conv_ln_b"][li], 4, t_vec)
    Dg = kb.sb("b_Dg", [128, 4, 31, 128], BF16); t_Dg = kb.trk()
    for j in range(4):
        for tap in range(31):
            V(lambda: nc.vector.tensor_scalar(Dg[:, j, tap, :], g.ident_b, cwT[:, j, tap:tap + 1], None, ALU.mult),
              r=[t_cw, g.t_c], w=[t_Dg])
    ut_p = RPool(kb, "b_ut", [128, 4, 542], BF16, 2)
    hc_p = RPool(kb, "b_hc", [128, 4, 512], F32, 2)
    hb_p = RPool(kb, "b_hb", [128, 4, 512], BF16, 2)
    sq_p = RPool(kb, "b_sq", [128, 4, 512], BF16, 2)
    st_p = RPool(kb, "b_st", [128, 512], F32, 6)
    ob_p = RPool(kb, "b_ob", [128, 512], BF16, 12)
    uT3 = S["uT"][0].rearrange("(j p) t -> p j t", p=128)
    for (t0, n, is_ctx, pos0) in g.tiles:
        if is_ctx and last:
            continue
        s_lo, s_hi = (0, NCTX) if is_ctx else (NCTX, T)
        lo = max(s_lo, t0 - 15); hi = min(s_hi, t0 + n + 15)
        ut, t_ut = ut_p.get()
        if lo > t0 - 15 or hi < t0 + n + 15:
            V(lambda: nc.vector.memset(ut, 0.0), w=[t_ut])
        kb.dma(ut[:, :, lo - (t0 - 15):hi - (t0 - 15)], uT3[:, :, lo:hi], reads=[S["uT"][1]], writes=[t_ut])
        hc, t_hc = hc_p.get(); hb, t_hb = hb_p.get(); sq, t_sq = sq_p.get()
        for j in range(4):
            ps_, tp = g.psum()
            for tap in range(31):
                P(lambda: nc.tensor.matmul(ps_[:, :n], Dg[:, j, tap, :], ut[:, j, tap:tap + n], start=(tap == 0), stop=(tap == 30)),
                  r=[t_Dg, t_ut], w=[tp], acc=(tap > 0), inc=(tap == 30))
            A(lambda: nc.scalar.activation(hc[:, j, :n], ps_[:, :n], AF.Identity, bias=vec[:, j:j + 1]), r=[tp, t_vec], w=[t_hc])
        V(lambda: nc.vector.tensor_copy(hb[:, :, :n], hc[:, :, :n]), r=[t_hc], w=[t_hb])
        A(lambda: nc.scalar.activation(sq[:, :, :n], hc[:, :, :n], AF.Square), r=[t_hc], w=[t_sq])
        p1, tp1 = g.psum(); p2, tp2 = g.psum()
        for j in range(4):
            P(lambda: nc.tensor.matmul(p1[:, :n], g.ones_b, hb[:, j, :n], start=(j == 0), stop=(j == 3)),
              r=[t_hb, g.t_c], w=[tp1], acc=(j > 0), inc=(j == 3))
        for j in range(4):
            P(lambda: nc.tensor.matmul(p2[:, :n], g.ones_b, sq[:, j, :n], start=(j == 0), stop=(j == 3)),
              r=[t_sq, g.t_c], w=[tp2], acc=(j > 0), inc=(j == 3))
        mu, t_mu = st_p.get(); m2, t_m2 = st_p.get(); rs, t_rs = st_p.get()
        V(lambda: nc.vector.tensor_scalar(mu[:, :n], p1[:, :n], 1.0 / 512.0, None, ALU.mult), r=[tp1], w=[t_mu])
        G(lambda: nc.gpsimd.tensor_tensor(m2[:, :n], mu[:, :n], mu[:, :n], ALU.mult), r=[t_mu], w=[t_m2])
        V(lambda: nc.vector.scalar_tensor_tensor(rs[:, :n], p2[:, :n], 1.0 / 512.0, m2[:, :n], ALU.mult, ALU.subtract),
          r=[tp2, t_m2], w=[t_rs])
        V(lambda: nc.vector.tensor_scalar(rs[:, :n], rs[:, :n], EPS, None, ALU.add), r=[t_rs], w=[t_rs])
        A(lambda: nc.scalar.activation(rs[:, :n], rs[:, :n], AF.Sqrt), r=[t_rs], w=[t_rs])
        V(lambda: nc.vector.reciprocal(rs[:, :n], rs[:, :n]), r=[t_rs], w=[t_rs])
        for j in range(4):
            tm, t_tm = st_p.get()
            G(lambda: nc.gpsimd.tensor_tensor(tm[:, :n], hc[:, j, :n], mu[:, :n], ALU.subtract), r=[t_hc, t_mu], w=[t_tm])
            V(lambda: nc.vector.tensor_tensor(tm[:, :n], tm[:, :n], rs[:, :n], ALU.mult), r=[t_tm, t_rs], w=[t_tm])
            ob, t_ob = ob_p.get()
            A(lambda: nc.scalar.activation(ob[:, :n], tm[:, :n], AF.Silu, scale=vec[:, 4 + j:5 + j], bias=vec[:, 8 + j:9 + j]),
              r=[t_tm, t_vec], w=[t_ob])
            store(g, "br0T", S["br0T"][0][j * 128:(j + 1) * 128, t0:t0 + n], ob[:, :n], t_ob)


def phase_E(g, li, last=False):
    kb, nc, V, A, P, G = g.kb, g.nc, g.V, g.A, g.P, g.G
    S, I, L, T = g.S, g.I, g.L, g.T
    NKT = T // 128
    LOOK = 3
    KT = RPool(kb, "e_kt", [96, T], BF16, 2)
    VAp = RPool(kb, "e_va", [128, NKT, 65], BF16, 2)
    for (va, t_va) in VAp.tiles:
        V(lambda: nc.vector.memset(va[:, :, 64:65], 1.0), w=[t_va])
    QT = RPool(kb, "e_qt", [96, 512], BF16, 3)
    PT = RPool(kb, "e_pt", [128, 512], BF16, 6)
    OT = RPool(kb, "e_ot", [65, 512], F32, 2)
    RD = RPool(kb, "e_rd", [64, 512], F32, 2)
    OB = RPool(kb, "e_ob", [64, 512], BF16, 2)
    g.ps_skip.update((0, 1))
    pos = [g.psb[0], g.psb[1]]
    qi = [0]
    for h in range(8):
        kt, t_kt = KT.get()
        kb.dma(kt[0:64, :], S["kmT"][0][h * 64:(h + 1) * 64, :], reads=[S["kmT"][1]], writes=[t_kt])
        kb.dma(kt[64:96, :], S["kropeT"][0], reads=[S["kropeT"][1]], writes=[t_kt])
        va, t_va = VAp.get()
        kb.dma(va[:, :, 0:64], S["vm"][0][:, h * 64:(h + 1) * 64].rearrange("(k p) d -> p k d", p=128),
               reads=[S["vm"][1]], writes=[t_va])
        its = []
        for (t0, n, is_ctx, pos0) in g.tiles:
            if is_ctx and last:
                continue
            nkt = NCTX // 128 if is_ctx else NKT
            qd = {"t0": t0, "n": n, "nkt": nkt, "qt": None}
            for ki in range(nkt):
                its.append((qd, ki))

        def issue_qk(qd, ki):
            n = qd["n"]
            if qd["qt"] is None:
                qt, t_qt = QT.get()
                kb.dma(qt[:, :n], S["qmT"][0][h, :, qd["t0"]:qd["t0"] + n], reads=[S["qmT"][1]], writes=[t_qt])
                qd["qt"] = (qt, t_qt)
                qi[0] += 1
                qd["po"] = pos[qi[0] % 2]
            qt, t_qt = qd["qt"]
            ps_, tps = g.psum()
            P(lambda: nc.tensor.matmul(ps_[:, :n], kt[:96, ki * 128:(ki + 1) * 128], qt[:96, :n], start=True, stop=True),
              r=[t_kt, t_qt], w=[tps])
            return ps_, tps

        def finish(qd, ki, ps_, tps):
            n, nkt = qd["n"], qd["nkt"]
            po, tpo = qd["po"]
            pt, t_pt = PT.get()
            A(lambda: nc.scalar.activation(pt[:, :n], ps_[:, :n], AF.Exp), r=[tps], w=[t_pt])
            P(lambda: nc.tensor.matmul(po[:65, :n], va[:, ki, 0:65], pt[:, :n], start=(ki == 0), stop=(ki == nkt - 1)),
              r=[t_va, t_pt], w=[tpo], acc=(ki > 0), inc=(ki == nkt - 1))
            if ki < nkt - 1:
                return
            ot, t_ot = OT.get()
            V(lambda: nc.vector.tensor_copy(ot[:65, :n], po[:65, :n]), r=[tpo], w=[t_ot])
            pd, tpd = g.psum()
            P(lambda: nc.tensor.matmul(pd[:64, :n], g.ones_f[64:65, 0:64], ot[64:65, :n], start=True, stop=True),
              r=[t_ot, g.t_c], w=[tpd])
            rd, t_rd = RD.get()
            V(lambda: nc.vector.reciprocal(rd[:64, :n], pd[:64, :n]), r=[tpd], w=[t_rd])
            ob, t_ob = OB.get()
            V(lambda: nc.vector.tensor_tensor(ob[:64, :n], ot[:64, :n], rd[:64, :n], ALU.mult), r=[t_ot, t_rd], w=[t_ob])
            store(g, "br3T", S["br3T"][0][h * 64:(h + 1) * 64, qd["t0"]:qd["t0"] + n], ob[:64, :n], t_ob)

        queue = []
        for (qd, ki) in its:
            ps_, tps = issue_qk(qd, ki)
            queue.append((qd, ki, ps_, tps))
            if len(queue) > LOOK:
                finish(*queue.pop(0))
        while queue:
            finish(*queue.pop(0))
    g.ps_skip.difference_update((0, 1))


def phase_CD_stub(g, li, last=False):
    kb, nc, V = g.kb, g.nc, g.V
    z = kb.sb("cd_z", [128, 2048], BF16); t_z = kb.trk()
    V(lambda: nc.vector.memset(z, 0.0), w=[t_z])
    for nm in ("br2T",):
        for j in range(4):
            for c0 in range(0, g.T, 2048):
                w = min(2048, g.T - c0)
                store(g, nm, g.S[nm][0][j * 128:(j + 1) * 128, c0:c0 + w], z[:, :w], t_z)


def phase_F(g, li, last=False):
    kb, nc, V, A, P, G = g.kb, g.nc, g.V, g.A, g.P, g.G
    S, I, L, T = g.S, g.I, g.L, g.T
    kb_ = kb
    c = Ctx()
    c.xt = RPool(kb, "f_xt", [128, 4, D], F32, 1)
    c.xn = RPool(kb, "f_xn", [128, 4, D], BF16, 1)
    c.hT = RPool(kb, "f_hT", [128, 8, 512], BF16, 1)
    c.ss = RPool(kb, "f_ss", [128, 8], F32, 2)
    c.junk = RPool(kb, "f_junk", [128, D], BF16, 1)
    c.wt = RPool(kb, "f_wt", [128, 8, 512], BF16, 4)
    c.stb = RPool(kb, "f_stb", [128, 512], BF16, 2)
    c.stf = RPool(kb, "f_stf", [128, 512], F32, 8)
    mg_p = RPool(kb, "f_mg", [128, 8, 512], F32, 1)
    mgb_p = RPool(kb, "f_mgb", [128, 8, 512], BF16, 1)
    act_p = RPool(kb, "f_act", [128, 22, 512], BF16, 1)
    wf2_p = RPool(kb, "f_wf2", [128, 22, 512], BF16, 1)
    bt_p = RPool(kb, "f_bt", [128, 4, 512], BF16, 2)
    for (t0, n, is_ctx, pos0) in g.tiles:
        if is_ctx and last:
            continue
        j = 1 if is_ctx else 0
        nb = n // 128
        src = (I["ctx"] if is_ctx else I["x"][pos0:pos0 + n]) if li == 0 else S["xs"][0][t0:t0 + n]
        t_src = kb.trk() if li == 0 else S["xs"][1]
        hT, t_h, xt, t_xt, _, _ = norm_tile(g, c, src, t_src, n, g.G1c, g.modc[:, 0], j)
        mg, t_mg = mg_p.get()
        for i in range(4):
            bt, t_bt = bt_p.get()
            kb.dma(bt[:, :, :n], S["br%dT" % i][0].rearrange("(k p) t -> p k t", p=128)[:, :, t0:t0 + n],
                   reads=[S["br%dT" % i][1]], writes=[t_bt])
            gen_gl = fm_chunks(g, c, "Wgl", 8, i * 1024, 1024, hT, t_h, n)
            gen_pr = fm_chunks(g, c, "Wb", 4, 0, 1024, bt, t_bt, n, kc0=i * 4)
            for n8 in range(8):
                _, _, pg, tpg = next(gen_gl)
                _, _, pp, tpp = next(gen_pr)
                sg, t_sg = c.stf.get()
                A(lambda: nc.scalar.activation(sg[:, :n], pg[:, :n], AF.Sigmoid), r=[tpg], w=[t_sg])
                if i == 0:
                    V(lambda: nc.vector.tensor_tensor(mg[:, n8, :n], pp[:, :n], sg[:, :n], ALU.mult), r=[tpp, t_sg], w=[t_mg])
                else:
                    V(lambda: nc.vector.tensor_tensor(sg[:, :n], pp[:, :n], sg[:, :n], ALU.mult), r=[tpp, t_sg], w=[t_sg])
                    G(lambda: nc.gpsimd.tensor_tensor(mg[:, n8, :n], mg[:, n8, :n], sg[:, :n], ALU.add), r=[t_sg, t_mg], w=[t_mg])
            for _ in gen_gl:
                pass
            for _ in gen_pr:
                pass
        mgb, t_mgb = mgb_p.get()
        A(lambda: nc.scalar.copy(mgb[:, :, :n], mg[:, :, :n]), r=[t_mg], w=[t_mgb])
        for (off, w, b, ps_, tp) in tm_chunks(g, c, "Wout", 8, 0, D, mgb, t_mgb, n):
            tm, t_tm = c.stf.get()
            V(lambda: nc.vector.tensor_tensor(tm, ps_, g.modr[:, j, 0, off:off + 512], ALU.mult), r=[tp, g.t_modr], w=[t_tm])
            G(lambda: nc.gpsimd.tensor_tensor(xt[:, b, off:off + 512], xt[:, b, off:off + 512], tm, ALU.add), r=[t_tm, t_xt], w=[t_xt])
        h2, t_h2, _, _, _, _ = norm_tile(g, c, None, None, n, g.G2c, g.modc[:, 3], j, xt_pair=(xt, t_xt))
        act, t_act = act_p.get()
        gen_a = fm_chunks(g, c, "Wf1", 8, 0, FFN, h2, t_h2, n)
        gen_g = fm_chunks(g, c, "Wf1", 8, FFN, FFN, h2, t_h2, n)
        for kc in range(22):
            _, _, pa, tpa = next(gen_a)
            _, _, pg, tpg = next(gen_g)
            sg, t_sg = c.stf.get()
            A(lambda: nc.scalar.activation(sg[:, :n], pg[:, :n], AF.Silu), r=[tpg], w=[t_sg])
            V(lambda: nc.vector.tensor_tensor(act[:, kc, :n], pa[:, :n], sg[:, :n], ALU.mult), r=[tpa, t_sg], w=[t_act])
        for _ in gen_a:
            pass
        for _ in gen_g:
            pass
        for (off, w, b, ps_, tp) in tm_chunks(g, c, "Wf2", 22, 0, D, act, t_act, n, pool=wf2_p):
            tm, t_tm = c.stf.get()
            V(lambda: nc.vector.tensor_tensor(tm, ps_, g.modr[:, j, 1, off:off + 512], ALU.mult), r=[tp, g.t_modr], w=[t_tm])
            G(lambda: nc.gpsimd.tensor_tensor(xt[:, b, off:off + 512], xt[:, b, off:off + 512], tm, ALU.add), r=[t_tm, t_xt], w=[t_xt])
        if not last:
            kb.dma(S["xs"][0][t0:t0 + n].rearrange("(b p) d -> p b d", p=128), xt[:, :nb, :], reads=[t_xt], writes=[S["xs"][1]])
        else:
            ss, t_ss = c.ss.get(); junk, t_junk = c.junk.get()
            for b in range(nb):
                A(lambda: nc.scalar.activation(junk, xt[:, b, :], AF.Square, accum_out=ss[:, b:b + 1]), r=[t_xt], w=[t_junk, t_ss])
            V(lambda: nc.vector.tensor_scalar(ss[:, 0:nb], ss[:, 0:nb], 1.0 / D, EPS, ALU.mult, ALU.add), r=[t_ss], w=[t_ss])
            A(lambda: nc.scalar.activation(ss[:, 0:nb], ss[:, 0:nb], AF.Sqrt), r=[t_ss], w=[t_ss])
            V(lambda: nc.vector.reciprocal(ss[:, 0:nb], ss[:, 0:nb]), r=[t_ss], w=[t_ss])
            for b in range(nb):
                V(lambda: nc.vector.scalar_tensor_tensor(xt[:, b, :], xt[:, b, :], ss[:, b:b + 1], g.fin_g, ALU.mult, ALU.mult),
                  r=[t_xt, t_ss, g.t_c], w=[t_xt])
            kb.dma(g.out_d[pos0:pos0 + n].rearrange("(b p) d -> p b d", p=128), xt[:, :nb, :], reads=[t_xt], writes=[g.t_out])


def tm_store(g, c_ps, o_sb, t_o, nb, name, dram3, pool):
    kb, nc, P, A = g.kb, g.nc, g.P, g.A
    ps_, tp = g.psum(); psv = ps_.bitcast(BF16)
    for b in range(nb):
        P(lambda: nc.tensor.transpose(psv[:, b * 128:(b + 1) * 128], o_sb[:, b * 128:(b + 1) * 128], g.ident_b),
          r=[t_o, g.t_c], w=[tp], acc=(b > 0), inc=(b == nb - 1))
    kt, t_kt = pool.get()
    A(lambda: nc.scalar.copy(kt[:, :nb, :], psv[:, :nb * 128].rearrange("p (b f) -> p b f", f=128)), r=[tp], w=[t_kt])
    store(g, name, dram3, kt[:, :nb, :], t_kt)


def phase_C(g, li, last=False):
    kb, nc, V, A, P, G = g.kb, g.nc, g.V, g.A, g.P, g.G
    S, I, L, T = g.S, g.I, g.L, g.T
    cw = kb.sb("c_cw", [128, 8, 5], F32); t_cw = kb.trk()
    for tap in range(5):
        for j in range(8):
            kb.dma(cw[:, j, tap:tap + 1], I["ssm_conv_w"][li, tap, j * 128:(j + 1) * 128].rearrange("(p o) -> p o", o=1), writes=[t_cw])
    scb = kb.sb("c_scb", [128, 8], F32); t_scb = kb.trk()
    g.coldma(scb, I["ssm_conv_b"][li], 8, t_scb)
    Dg = kb.sb("c_Dg", [128, 8, 5, 128], BF16); t_Dg = kb.trk()
    for j in range(8):
        for tap in range(5):
            V(lambda: nc.vector.tensor_scalar(Dg[:, j, tap, :], g.ident_b, cw[:, j, tap:tap + 1], None, ALU.mult), r=[t_cw, g.t_c], w=[t_Dg])
    ut_p = RPool(kb, "c_ut", [128, 8, 516], BF16, 2)
    ob_p = RPool(kb, "c_ob", [128, 512], BF16, 12)
    tk_p = RPool(kb, "c_tk", [128, 4, 128], BF16, 12)
    x3 = S["xbcT"][0].rearrange("(j p) t -> p j t", p=128)
    for (t0, n, is_ctx, pos0) in g.tiles:
        nb = n // 128
        s_lo, s_hi = (0, NCTX) if is_ctx else (NCTX, T)
        lo = max(s_lo, t0 - 2); hi = min(s_hi, t0 + n + 2)
        ut, t_ut = ut_p.get()
        if lo > t0 - 2 or hi < t0 + n + 2:
            V(lambda: nc.vector.memset(ut, 0.0), w=[t_ut])
        kb.dma(ut[:, :, lo - (t0 - 2):hi - (t0 - 2)], x3[:, :, lo:hi], reads=[S["xbcT"][1]], writes=[t_ut])
        for j in range(8):
            ps_, tp = g.psum()
            for tap in range(5):
                P(lambda: nc.tensor.matmul(ps_[:, :n], Dg[:, j, tap, :], ut[:, j, tap:tap + n], start=(tap == 0), stop=(tap == 4)),
                  r=[t_Dg, t_ut], w=[tp], acc=(tap > 0), inc=(tap == 4))
            o, t_o = ob_p.get()
            A(lambda: nc.scalar.activation(o[:, :n], ps_[:, :n], AF.Silu, bias=scb[:, j:j + 1]), r=[tp, t_scb], w=[t_o])
            if j < 4:
                tm_store(g, None, o, t_o, nb, "ssx", S["ssx"][0][t0:t0 + n, j * 128:(j + 1) * 128].rearrange("(b p) f -> p b f", p=128), tk_p)
            elif j < 6:
                store(g, "ssBT", S["ssBT"][0][(j - 4) * 128:(j - 3) * 128, t0:t0 + n], o[:, :n], t_o)
                tm_store(g, None, o, t_o, nb, "ssB", S["ssB"][0][t0:t0 + n, (j - 4) * 128:(j - 3) * 128].rearrange("(b p) f -> p b f", p=128), tk_p)
            else:
                store(g, "ssCT", S["ssCT"][0][(j - 6) * 128:(j - 5) * 128, t0:t0 + n], o[:, :n], t_o)
    if ASTOP == 60:
        return
    NC_ = T // 128
    nctx = NCTX // 128
    Dr = kb.sb("c_Dr", [128, 8], F32); ngr = kb.sb("c_ngr", [128, 512], F32); t_cst = kb.trk()
    kb.dma(Dr, I["ssm_d"][li].partition_broadcast(128), writes=[t_cst])
    kb.dma(ngr, I["ssm_norm_g"][li].partition_broadcast(128), writes=[t_cst])
    hst = kb.sb("c_h", [128, 512], F32); hb = kb.sb("c_hb", [128, 512], BF16); t_h = kb.trk(); t_hb = kb.trk()
    sm_p = RPool(kb, "c_sm", [128, 16], F32, 4)
    s8_p = RPool(kb, "c_s8", [128, 8], F32, 12)
    x_p = RPool(kb, "c_x", [128, 512], BF16, 3)
    b_p = RPool(kb, "c_b", [128, 256], BF16, 3)
    bt_p = RPool(kb, "c_bt", [128, 2, 128], BF16, 3)
    ct_p = RPool(kb, "c_ct", [128, 2, 128], BF16, 3)
    xd_p = RPool(kb, "c_xd", [128, 512], BF16, 4)
    gm_p = RPool(kb, "c_gm", [128, 2, 128], F32, 2)
    R_p = RPool(kb, "c_R", [128, 8, 128], F32, 2)
    E_p = RPool(kb, "c_E", [128, 8, 128], F32, 2)
    pm_p = RPool(kb, "c_pm", [128, 8, 128], BF16, 2)
    f5_p = RPool(kb, "c_f5", [128, 512], F32, 12)
    z_p = RPool(kb, "c_z", [128, 512], BF16, 2)
    o5_p = RPool(kb, "c_o5", [128, 512], BF16, 2)
    jk_p = RPool(kb, "c_jk", [128, 256], BF16, 1)
    BT3 = S["ssBT"][0].rearrange("(g p) t -> p g t", p=128)
    CT3 = S["ssCT"][0].rearrange("(g p) t -> p g t", p=128)
    br3 = S["br1T"][0].rearrange("(j p) t -> p j t", p=128)

    def b8(ap, w):
        return ap.unsqueeze(2).broadcast_to([128, 8, w])

    for d in range(2):
        Ud = g.Uf if d == 0 else g.Lf
        order = list(range(NC_)) if d == 0 else (list(range(nctx - 1, -1, -1)) + list(range(NC_ - 1, nctx - 1, -1)))
        V(lambda: nc.vector.memset(hst, 0.0), w=[t_h])
        V(lambda: nc.vector.memset(hb, 0.0), w=[t_hb])
        for ck in order:
            tk = ck * 128
            is_ctx = ck < nctx
            sm, t_sm = sm_p.get()
            kb.dma(sm[:, 0:8], S["la"][0][tk:tk + 128, d * 8:(d + 1) * 8], reads=[S["la"][1]], writes=[t_sm])
            kb.dma(sm[:, 8:16], S["dt"][0][tk:tk + 128, d * 8:(d + 1) * 8], reads=[S["dt"][1]], writes=[t_sm])
            la8, dt8 = sm[:, 0:8], sm[:, 8:16]
            x, t_x = x_p.get(); kb.dma(x, S["ssx"][0][tk:tk + 128, :], reads=[S["ssx"][1]], writes=[t_x])
            Bt, t_B = b_p.get(); kb.dma(Bt, S["ssB"][0][tk:tk + 128, :], reads=[S["ssB"][1]], writes=[t_B])
            BT, t_BT = bt_p.get(); kb.dma(BT, BT3[:, :, tk:tk + 128], reads=[S["ssBT"][1]], writes=[t_BT])
            CT, t_CT = ct_p.get(); kb.dma(CT, CT3[:, :, tk:tk + 128], reads=[S["ssCT"][1]], writes=[t_CT])
            pc, tpc = g.psum()
            P(lambda: nc.tensor.matmul(pc[:, 0:8], Ud, la8, start=True, stop=True), r=[g.t_c, t_sm], w=[tpc])
            P(lambda: nc.tensor.matmul(pc[:, 8:16], g.ones_f, la8, start=True, stop=True), r=[g.t_c, t_sm], w=[tpc], acc=True)
            cum, t_cum = s8_p.get(); tot, t_tot = s8_p.get()
            V(lambda: nc.vector.tensor_copy(cum, pc[:, 0:8]), r=[tpc], w=[t_cum])
            V(lambda: nc.vector.tensor_copy(tot, pc[:, 8:16]), r=[tpc], w=[t_tot])
            ecum, t_ec = s8_p.get(); dst, t_ds = s8_p.get(); etot, t_et = s8_p.get(); dtd, t_dd = s8_p.get()
            A(lambda: nc.scalar.activation(ecum, cum, AF.Exp), r=[t_cum], w=[t_ec])
            V(lambda: nc.vector.tensor_tensor(dst, tot, cum, ALU.subtract), r=[t_tot, t_cum], w=[t_ds])
            A(lambda: nc.scalar.activation(dst, dst, AF.Exp), r=[t_ds], w=[t_ds])
            A(lambda: nc.scalar.activation(etot, tot, AF.Exp), r=[t_tot], w=[t_et])
            V(lambda: nc.vector.tensor_tensor(dtd, dst, dt8, ALU.mult), r=[t_ds, t_sm], w=[t_dd])
            xdt, t_xdt = xd_p.get(); xd, t_xd = xd_p.get()
            x3_ = x.rearrange("p (h e) -> p h e", e=64)
            V(lambda: nc.vector.tensor_tensor(xdt.rearrange("p (h e) -> p h e", e=64), x3_, b8(dt8, 64), ALU.mult), r=[t_x, t_sm], w=[t_xdt])
            V(lambda: nc.vector.tensor_tensor(xd.rearrange("p (h e) -> p h e", e=64), x3_, b8(dtd, 64), ALU.mult), r=[t_x, t_dd], w=[t_xd])
            pg, tpg = g.psum()
            for gi in range(2):
                P(lambda: nc.tensor.matmul(pg[:, gi * 128:(gi + 1) * 128], BT[:, gi, :], CT[:, gi, :], start=True, stop=True),
                  r=[t_BT, t_CT], w=[tpg], acc=(gi > 0))
            gm, t_gm = gm_p.get()
            V(lambda: nc.vector.tensor_tensor(gm, pg[:, 0:256].rearrange("p (g l) -> p g l", l=128),
                                              Ud.unsqueeze(1).broadcast_to([128, 2, 128]), ALU.mult), r=[tpg, g.t_c], w=[t_gm])
            Rt, t_R = R_p.get()
            V(lambda: nc.vector.tensor_tensor(Rt, b8(la8, 128), Ud.unsqueeze(1).broadcast_to([128, 8, 128]), ALU.mult), r=[t_sm, g.t_c], w=[t_R])
            Et, t_E = E_p.get()
            for hf in range(2):
                pb, tpb = g.psum()
                P(lambda: nc.tensor.matmul(pb, g.ones_f, Rt[:, hf * 4:(hf + 1) * 4, :].rearrange("p h l -> p (h l)"), start=True, stop=True),
                  r=[g.t_c, t_R], w=[tpb])
                for hh in range(4):
                    h = hf * 4 + hh
                    V(lambda: nc.vector.tensor_scalar(Et[:, h, :], pb[:, hh * 128:(hh + 1) * 128], cum[:, h:h + 1], 0.0, ALU.subtract, ALU.min),
                      r=[tpb, t_cum], w=[t_E])
            A(lambda: nc.scalar.activation(Et, Et, AF.Exp), r=[t_E], w=[t_E])
            pm, t_pm = pm_p.get()
            for gi in range(2):
                V(lambda: nc.vector.tensor_tensor(pm[:, gi * 4:(gi + 1) * 4, :], Et[:, gi * 4:(gi + 1) * 4, :],
                                                  gm[:, gi, :].unsqueeze(1).broadcast_to([128, 4, 128]), ALU.mult), r=[t_E, t_gm], w=[t_pm])
            py, tpy = g.psum()
            for h in range(8):
                P(lambda: nc.tensor.matmul(py[:, h * 64:(h + 1) * 64], pm[:, h, :], xdt[:, h * 64:(h + 1) * 64], start=True, stop=True),
                  r=[t_pm, t_xdt], w=[tpy], acc=(h > 0), inc=(h == 7))
            pi_, tpi = g.psum()
            for gi in range(2):
                P(lambda: nc.tensor.matmul(pi_[:, gi * 256:(gi + 1) * 256], CT[:, gi, :], hb[:, gi * 256:(gi + 1) * 256], start=True, stop=True),
                  r=[t_CT, t_hb], w=[tpi], acc=(gi > 0), inc=(gi == 1))
            t1, t_t1 = f5_p.get(); y, t_y = f5_p.get()
            V(lambda: nc.vector.tensor_tensor(t1.rearrange("p (h e) -> p h e", e=64), pi_.rearrange("p (h e) -> p h e", e=64), b8(ecum, 64), ALU.mult),
              r=[tpi, t_ec], w=[t_t1])
            V(lambda: nc.vector.tensor_tensor(y, py, t1, ALU.add), r=[tpy, t_t1], w=[t_y])
            pS, tpS = g.psum()
            for gi in range(2):
                P(lambda: nc.tensor.matmul(pS[:, gi * 256:(gi + 1) * 256], Bt[:, gi * 128:(gi + 1) * 128], xd[:, gi * 256:(gi + 1) * 256], start=True, stop=True),
                  r=[t_B, t_xd], w=[tpS], acc=(gi > 0), inc=(gi == 1))
            V(lambda: nc.vector.tensor_tensor(hst.rearrange("p (h e) -> p h e", e=64), hst.rearrange("p (h e) -> p h e", e=64), b8(etot, 64), ALU.mult),
              r=[t_h, t_et], w=[t_h])
            V(lambda: nc.vector.tensor_tensor(hst, hst, pS, ALU.add), r=[t_h, tpS], w=[t_h])
            A(lambda: nc.scalar.copy(hb, hst), r=[t_h], w=[t_hb])
            if d == 0:
                kb.dma(S["yf"][0][tk:tk + 128, :], y, reads=[t_y], writes=[S["yf"][1]])
                continue
            if is_ctx and last:
                continue
            yf, t_yf = f5_p.get(); kb.dma(yf, S["yf"][0][tk:tk + 128, :], reads=[S["yf"][1]], writes=[t_yf])
            zt, t_z = z_p.get(); kb.dma(zt, S["zs"][0][tk:tk + 128, :], reads=[S["zs"][1]], writes=[t_z])
            G(lambda: nc.gpsimd.tensor_tensor(y, y, yf, ALU.add), r=[t_y, t_yf], w=[t_y])
            V(lambda: nc.vector.tensor_tensor(t1.rearrange("p (h e) -> p h e", e=64), x3_, b8(Dr, 64), ALU.mult), r=[t_x, t_cst], w=[t_t1])
            G(lambda: nc.gpsimd.tensor_tensor(y, y, t1, ALU.add), r=[t_y, t_t1], w=[t_y])
            V(lambda: nc.vector.tensor_tensor(y, y, zt, ALU.mult), r=[t_y, t_z], w=[t_y])
            ss, t_ss = s8_p.get(); jk, t_jk = jk_p.get()
            for gi in range(2):
                A(lambda: nc.scalar.activation(jk, y[:, gi * 256:(gi + 1) * 256], AF.Square, accum_out=ss[:, gi:gi + 1]), r=[t_y], w=[t_jk, t_ss])
            V(lambda: nc.vector.tensor_scalar(ss[:, 0:2], ss[:, 0:2], 1.0 / 256.0, EPS, ALU.mult, ALU.add), r=[t_ss], w=[t_ss])
            A(lambda: nc.scalar.activation(ss[:, 0:2], ss[:, 0:2], AF.Sqrt), r=[t_ss], w=[t_ss])
            V(lambda: nc.vector.reciprocal(ss[:, 0:2], ss[:, 0:2]), r=[t_ss], w=[t_ss])
            o5, t_o5 = o5_p.get()
            for gi in range(2):
                V(lambda: nc.vector.scalar_tensor_tensor(o5[:, gi * 256:(gi + 1) * 256], y[:, gi * 256:(gi + 1) * 256], ss[:, gi:gi + 1],
                                                         ngr[:, gi * 256:(gi + 1) * 256], ALU.mult, ALU.mult), r=[t_y, t_ss, t_cst], w=[t_o5])
            ps_, tp = g.psum(); psv = ps_.bitcast(BF16)
            for j in range(4):
                P(lambda: nc.tensor.transpose(psv[:, j * 128:(j + 1) * 128], o5[:, j * 128:(j + 1) * 128], g.ident_b),
                  r=[t_o5, g.t_c], w=[tp], acc=(j > 0), inc=(j == 3))
            kt, t_kt = tk_p.get()
            A(lambda: nc.scalar.copy(kt, psv[:, 0:512].rearrange("p (j t) -> p j t", t=128)), r=[tp], w=[t_kt])
            store(g, "br1T", br3[:, :, tk:tk + 128], kt, t_kt)


def phase_D(g, li, last=False):
    kb, nc, V, A, P, G = g.kb, g.nc, g.V, g.A, g.P, g.G
    S, I, L, T = g.S, g.I, g.L, g.T
    NC_ = T // 128
    nctx = NCTX // 128
    lgb = kb.sb("d_lgb", [128, 8], F32); t_k = kb.trk()
    kb.dma(lgb, I["ret_decay"][li].partition_broadcast(128), writes=[t_k])
    A(lambda: nc.scalar.activation(lgb, lgb, AF.Exp), r=[t_k], w=[t_k])
    V(lambda: nc.vector.tensor_scalar(lgb, lgb, -1.0, None, ALU.mult), r=[t_k], w=[t_k])
    gr = kb.sb("d_gr", [128, 2, 512], F32)
    kb.dma(gr[:, 0, :], I["ret_gn_g"][li].partition_broadcast(128), writes=[t_k])
    kb.dma(gr[:, 1, :], I["ret_gn_b"][li].partition_broadcast(128), writes=[t_k])
    ii = kb.sb("d_ii", [128, 132], I32); ff = kb.sb("d_ff", [128, 133], F32)
    G(lambda: nc.gpsimd.iota(ii[:, 0:128], pattern=[[1, 128]], base=0, channel_multiplier=-1), w=[t_k])
    for cidx, (base, cm) in enumerate(((1, 1), (127, -1), (128, -1), (0, 1))):
        G(lambda: nc.gpsimd.iota(ii[:, 128 + cidx:129 + cidx], pattern=[[0, 1]], base=base, channel_multiplier=cm), w=[t_k])
    V(lambda: nc.vector.tensor_copy(ff[:, 0:132], ii), r=[t_k], w=[t_k])
    A(lambda: nc.scalar.activation(ff[:, 0:128], ff[:, 0:128], AF.Abs), r=[t_k], w=[t_k])
    V(lambda: nc.vector.memset(ff[:, 132:133], 128.0), r=[t_k], w=[t_k])
    Dm = kb.sb("d_Dm", [128, 2, 4, 128], F32)
    vec = kb.sb("d_vec", [128, 2, 3, 4], F32)
    for d in range(2):
        Ud = g.Uf if d == 0 else g.Lf
        for h in range(4):
            k = d * 4 + h
            A(lambda: nc.scalar.activation(Dm[:, d, h, :], ff[:, 0:128], AF.Exp, scale=lgb[:, k:k + 1]), r=[t_k], w=[t_k])
            V(lambda: nc.vector.tensor_tensor(Dm[:, d, h, :], Dm[:, d, h, :], Ud, ALU.mult), r=[t_k, g.t_c], w=[t_k])
            c_e, c_d = (128, 129) if d == 0 else (130, 131)
            A(lambda: nc.scalar.activation(vec[:, d, 0, h:h + 1], ff[:, c_e:c_e + 1], AF.Exp, scale=lgb[:, k:k + 1]), r=[t_k], w=[t_k])
            A(lambda: nc.scalar.activation(vec[:, d, 1, h:h + 1], ff[:, c_d:c_d + 1], AF.Exp, scale=lgb[:, k:k + 1]), r=[t_k], w=[t_k])
            A(lambda: nc.scalar.activation(vec[:, d, 2, h:h + 1], ff[:, 132:133], AF.Exp, scale=lgb[:, k:k + 1]), r=[t_k], w=[t_k])
    hst = kb.sb("d_h", [64, 4, 128], F32); hb = kb.sb("d_hb", [64, 4, 128], BF16); t_h = kb.trk(); t_hb = kb.trk()
    q_p = RPool(kb, "d_q", [64, 4, 128], BF16, 3)
    k_p = RPool(kb, "d_k", [64, 4, 128], BF16, 3)
    km_p = RPool(kb, "d_km", [128, 256], BF16, 3)
    kd_p = RPool(kb, "d_kd", [128, 256], BF16, 2)
    v_p = RPool(kb, "d_v", [128, 512], BF16, 3)
    pm_p = RPool(kb, "d_pm", [128, 4, 128], BF16, 2)
    f5_p = RPool(kb, "d_f5", [128, 512], F32, 12)
    g_p = RPool(kb, "d_g", [128, 512], BF16, 2)
    o5_p = RPool(kb, "d_o5", [128, 512], BF16, 2)
    s8_p = RPool(kb, "d_s8", [128, 16], F32, 4)
    jk_p = RPool(kb, "d_jk", [128, 128], BF16, 1)
    tk_p = RPool(kb, "d_tk", [128, 4, 128], BF16, 10)
    q3 = S["qrT"][0].rearrange("(h d) t -> d h t", d=64)
    k3 = S["krT"][0].rearrange("(h d) t -> d h t", d=64)
    br3 = S["br2T"][0].rearrange("(j p) t -> p j t", p=128)

    def b4(ap, w):
        return ap.unsqueeze(2).broadcast_to([ap.shape[0], 4, w])

    for d in range(2):
        order = list(range(NC_)) if d == 0 else (list(range(nctx - 1, -1, -1)) + list(range(NC_ - 1, nctx - 1, -1)))
        V(lambda: nc.vector.memset(hst, 0.0), w=[t_h])
        V(lambda: nc.vector.memset(hb, 0.0), w=[t_hb])
        for ck in order:
            tk = ck * 128
            is_ctx = ck < nctx
            qT, t_q = q_p.get(); kb.dma(qT, q3[:, :, tk:tk + 128], reads=[S["qrT"][1]], writes=[t_q])
            kT, t_kT = k_p.get(); kb.dma(kT, k3[:, :, tk:tk + 128], reads=[S["krT"][1]], writes=[t_kT])
            km, t_km = km_p.get(); kb.dma(km, S["kr"][0][tk:tk + 128, :], reads=[S["kr"][1]], writes=[t_km])
            v, t_v = v_p.get(); kb.dma(v, S["rv"][0][tk:tk + 128, :], reads=[S["rv"][1]], writes=[t_v])
            pg, tpg = g.psum()
            for h in range(4):
                P(lambda: nc.tensor.matmul(pg[:, h * 128:(h + 1) * 128], kT[:, h, :], qT[:, h, :], start=True, stop=True),
                  r=[t_kT, t_q], w=[tpg], acc=(h > 0), inc=(h == 3))
            pm, t_pm = pm_p.get()
            V(lambda: nc.vector.tensor_tensor(pm, pg.rearrange("p (h l) -> p h l", l=128), Dm[:, d], ALU.mult), r=[tpg, t_k], w=[t_pm])
            py, tpy = g.psum()
            for h in range(4):
                P(lambda: nc.tensor.matmul(py[:, h * 128:(h + 1) * 128], pm[:, h, :], v[:, h * 128:(h + 1) * 128], start=True, stop=True),
                  r=[t_pm, t_v], w=[tpy], acc=(h > 0), inc=(h == 3))
            pi_, tpi = g.psum()
            for h in range(4):
                P(lambda: nc.tensor.matmul(pi_[:, h * 128:(h + 1) * 128], qT[:, h, :], hb[:, h, :], start=True, stop=True),
                  r=[t_q, t_hb], w=[tpi], acc=(h > 0), inc=(h == 3))
            t1, t_t1 = f5_p.get(); y, t_y = f5_p.get()
            V(lambda: nc.vector.tensor_tensor(t1.rearrange("p (h e) -> p h e", e=128), pi_.rearrange("p (h e) -> p h e", e=128),
                                              b4(vec[:, d, 0, :], 128), ALU.mult), r=[tpi, t_k], w=[t_t1])
            V(lambda: nc.vector.tensor_tensor(y, py, t1, ALU.add), r=[tpy, t_t1], w=[t_y])
            kd, t_kd = kd_p.get()
            V(lambda: nc.vector.tensor_tensor(kd.rearrange("p (h e) -> p h e", e=64), km.rearrange("p (h e) -> p h e", e=64),
                                              b4(vec[:, d, 1, :], 64), ALU.mult), r=[t_km, t_k], w=[t_kd])
            pS, tpS = g.psum()
            for h in range(4):
                P(lambda: nc.tensor.matmul(pS[:64, h * 128:(h + 1) * 128], kd[:, h * 64:(h + 1) * 64], v[:, h * 128:(h + 1) * 128], start=True, stop=True),
                  r=[t_kd, t_v], w=[tpS], acc=(h > 0), inc=(h == 3))
            V(lambda: nc.vector.tensor_tensor(hst, hst, b4(vec[:64, d, 2, :], 128), ALU.mult), r=[t_h, t_k], w=[t_h])
            V(lambda: nc.vector.tensor_tensor(hst, hst, pS[:64, :].rearrange("p (h e) -> p h e", e=128), ALU.add), r=[t_h, tpS], w=[t_h])
            A(lambda: nc.scalar.copy(hb, hst), r=[t_h], w=[t_hb])
            if d == 0:
                kb.dma(S["ryf"][0][tk:tk + 128, :], y, reads=[t_y], writes=[S["ryf"][1]])
                continue
            if is_ctx and last:
                continue
            yf, t_yf = f5_p.get(); kb.dma(yf, S["ryf"][0][tk:tk + 128, :], reads=[S["ryf"][1]], writes=[t_yf])
            gt, t_g = g_p.get(); kb.dma(gt, S["rg"][0][tk:tk + 128, :], reads=[S["rg"][1]], writes=[t_g])
            G(lambda: nc.gpsimd.tensor_tensor(y, y, yf, ALU.add), r=[t_y, t_yf], w=[t_y])
            st, t_st = s8_p.get(); jk, t_jk = jk_p.get()
            for h in range(4):
                A(lambda: nc.scalar.activation(jk, y[:, h * 128:(h + 1) * 128], AF.Identity, accum_out=st[:, h:h + 1]), r=[t_y], w=[t_jk, t_st])
                A(lambda: nc.scalar.activation(jk, y[:, h * 128:(h + 1) * 128], AF.Square, accum_out=st[:, 4 + h:5 + h]), r=[t_y], w=[t_jk, t_st])
            V(lambda: nc.vector.tensor_scalar(st[:, 0:8], st[:, 0:8], 1.0 / 128.0, None, ALU.mult), r=[t_st], w=[t_st])
            V(lambda: nc.vector.tensor_tensor(st[:, 8:12], st[:, 0:4], st[:, 0:4], ALU.mult), r=[t_st], w=[t_st])
            V(lambda: nc.vector.tensor_tensor(st[:, 12:16], st[:, 4:8], st[:, 8:12], ALU.subtract), r=[t_st], w=[t_st])
            V(lambda: nc.vector.tensor_scalar(st[:, 12:16], st[:, 12:16], EPS, None, ALU.add), r=[t_st], w=[t_st])
            A(lambda: nc.scalar.activation(st[:, 12:16], st[:, 12:16], AF.Sqrt), r=[t_st], w=[t_st])
            V(lambda: nc.vector.reciprocal(st[:, 12:16], st[:, 12:16]), r=[t_st], w=[t_st])
            for h in range(4):
                V(lambda: nc.vector.tensor_scalar(y[:, h * 128:(h + 1) * 128], y[:, h * 128:(h + 1) * 128], st[:, h:h + 1], st[:, 12 + h:13 + h],
                                                  ALU.subtract, ALU.mult), r=[t_y, t_st], w=[t_y])
            V(lambda: nc.vector.tensor_tensor(y, y, gr[:, 0, :], ALU.mult), r=[t_y, t_k], w=[t_y])
            G(lambda: nc.gpsimd.tensor_tensor(y, y, gr[:, 1, :], ALU.add), r=[t_y, t_k], w=[t_y])
            o5, t_o5 = o5_p.get()
            V(lambda: nc.vector.tensor_tensor(o5, y, gt, ALU.mult), r=[t_y, t_g], w=[t_o5])
            ps_, tp = g.psum(); psv = ps_.bitcast(BF16)
            for j in range(4):
                P(lambda: nc.tensor.transpose(psv[:, j * 128:(j + 1) * 128], o5[:, j * 128:(j + 1) * 128], g.ident_b),
                  r=[t_o5, g.t_c], w=[tp], acc=(j > 0), inc=(j == 3))
            kt, t_kt = tk_p.get()
            A(lambda: nc.scalar.copy(kt, psv[:, 0:512].rearrange("p (j t) -> p j t", t=128)), r=[tp], w=[t_kt])
            store(g, "br2T", br3[:, :, tk:tk + 128], kt, t_kt)


_CACHE = {}


def kernel(**inputs):
    x = np.asarray(inputs["x"], dtype=np.float32)
    B, L, _ = x.shape
    if L not in _CACHE:
        _CACHE[L] = build(L, n_layers=2, debug=False)[0]
    nc = _CACHE[L]
    tabs = rope_tables(L)

    def pack(t):
        return np.ascontiguousarray(np.concatenate([t[0], t[1], t[2], t[3]], axis=1), dtype=np.float32)

    def core_in(b):
        d = {"x": np.ascontiguousarray(x[b]), "ctx": np.ascontiguousarray(inputs["ctx"][b], dtype=np.float32),
             "c2": np.ascontiguousarray(np.stack([inputs["c"][b], inputs["c_ctx"]]), dtype=np.float32),
             "t_ret": pack(tabs["ret"]), "t_mla": pack(tabs["mla"]), "t_mlk": pack(tabs["mlk"])}
        for k, v in inputs.items():
            if k in ("x", "ctx", "c", "c_ctx"):
                continue
            v = np.asarray(v, dtype=np.float32)
            if k in ("ssm_dt_bias", "ssm_a_log", "ret_decay"):
                v = v.reshape(2, -1)
            if k == "final_norm_g":
                v = v.reshape(1, -1)
            d[k] = np.ascontiguousarray(v)
        return d

    res = run_bass_kernel_spmd(nc, [core_in(b) for b in range(B)], core_ids=list(range(B)))
    return np.stack([np.asarray(res.results[b]["out"], dtype=np.float32) for b in range(B)], axis=0)
```

```python
import math
import numpy as np
import concourse.bass as bass
import concourse.mybir as mybir
from concourse.bass_utils import run_bass_kernel_spmd

F32 = mybir.dt.float32
BF16 = mybir.dt.bfloat16
I32 = mybir.dt.int32
AF = mybir.ActivationFunctionType
ALU = mybir.AluOpType
AX = mybir.AxisListType


class Trk:
    __slots__ = ("w", "r", "wpe", "name", "excl", "pend")

    def __init__(self, name=""):
        self.w = None
        self.r = {}
        self.wpe = False
        self.name = name
        self.pend = 0
        self.excl = False


class Eng:
    def __init__(self, kb, name, h):
        self.kb = kb
        self.name = name
        self.h = h
        self.sem = kb.nc.alloc_semaphore("sem_" + name)
        self.key = "E_" + name
        kb.sems[self.key] = self.sem
        self.count = 0
        self.known = {}


class KB:
    def __init__(self, nc, n_dma_sems=40):
        self.nc = nc
        self.sems = {}
        self.pe = Eng(self, "pe", nc.tensor)
        self.act = Eng(self, "act", nc.scalar)
        self.dve = Eng(self, "dve", nc.vector)
        self.pool = Eng(self, "pool", nc.gpsimd)
        self.sp = Eng(self, "sp", nc.sync)
        self.engs = [self.pe, self.act, self.dve, self.pool, self.sp]
        self.dsems = []
        for i in range(n_dma_sems):
            s = nc.alloc_semaphore("dsem%d" % i)
            k = "D%d" % i
            self.sems[k] = s
            self.dsems.append([k, s, 0])
        self.dnext = 0
        self.psems = []
        for i in range(24):
            s_ = nc.alloc_semaphore("psem%d" % i)
            k = "Q%d" % i
            self.sems[k] = s_
            self.psems.append([k, s_, 0])
        self.pnext = 0
        self.n_inst = 0
        self.uid = 0
        self.deferred = []

    def sb(self, name, shape, dt=F32):
        self.uid += 1
        name = "%s_u%d" % (name, self.uid)
        if getattr(self, "stack", None) is not None:
            return self.stack.enter_context(self.nc.sbuf_tensor(name, list(shape), dt)).ap()
        return self.nc.alloc_sbuf_tensor(name, list(shape), dt).ap()

    def ps(self, name, shape, dt=F32):
        return self.nc.alloc_psum_tensor(name, list(shape), dt).ap()

    def dram(self, name, shape, dt=F32, kind="Internal"):
        return self.nc.dram_tensor(name, list(shape), dt, kind=kind).ap()

    def trk(self, name=""):
        return Trk(name)

    def _wait(self, eng, evs):
        need = {}
        for k, v in evs:
            if need.get(k, 0) < v:
                need[k] = v
        for k, v in need.items():
            if eng.known.get(k, 0) < v:
                eng.h.wait_ge(self.sems[k], v)
                eng.known[k] = v

    def _deps(self, eng, reads, writes, acc):
        if self.deferred:
            for t in reads:
                if t.pend:
                    self.flush_deferred()
                    break
            else:
                for t in writes:
                    if t.pend:
                        self.flush_deferred()
                        break
        evs = []
        for t in reads:
            if t.w is not None:
                evs.append(t.w)
            if t.excl:
                evs.extend((k, v) for k, v in t.r.items() if k != eng.key)
        for t in writes:
            if t.w is not None and not (acc and t.wpe and eng is self.pe):
                evs.append(t.w)
            evs.extend(t.r.items())
        self._wait(eng, evs)

    def _post(self, ev, reads, writes, is_pe):
        k, v = ev
        for t in reads:
            if t.r.get(k, 0) < v:
                t.r[k] = v
        for t in writes:
            t.w = ev
            t.r = {}
            t.wpe = is_pe

    def op(self, eng, fn, reads=(), writes=(), inc=True, acc=False):
        self._deps(eng, reads, writes, acc)
        inst = fn()
        self.n_inst += 1
        if inc:
            eng.count += 1
            inst.then_inc(eng.sem, 1)
            ev = (eng.key, eng.count)
        else:
            ev = (eng.key, eng.count + 1)
        self._post(ev, reads, writes, eng is self.pe)
        return inst

    def dma(self, out, in_, reads=(), writes=(), q=None, **kw):
        q = q or self.sp
        if q is self.pool:
            d = self.psems[self.pnext]
            self.pnext = (self.pnext + 1) % len(self.psems)
        else:
            d = self.dsems[self.dnext]
            self.dnext = (self.dnext + 1) % len(self.dsems)
        self._deps(q, reads, writes, False)
        if d[2] > 0:
            self._wait(q, [(d[0], d[2])])
        inst = q.h.dma_start(out=out, in_=in_, **kw)
        d[2] += 16
        inst.then_inc(d[1], 16)
        self.n_inst += 1
        ev = (d[0], d[2])
        self._post(ev, reads, writes, False)
        return inst

    def dma_deferred(self, out, in_, reads=(), writes=(), defer=8):
        for t in list(reads) + list(writes):
            t.pend += 1
        self.deferred.append((out, in_, list(reads), list(writes)))
        while len(self.deferred) > defer:
            self._emit_deferred()

    def _emit_deferred(self):
        out, in_, reads, writes = self.deferred.pop(0)
        for t in reads + writes:
            t.pend -= 1
        self.dma(out, in_, reads=reads, writes=writes)

    def flush_deferred(self):
        while self.deferred:
            self._emit_deferred()

    def finish(self, trks):
        self.flush_deferred()
        evs = []
        for t in trks:
            if t.w is not None:
                evs.append(t.w)
        self._wait(self.sp, evs)


def _kb_barrier(self):
    self.flush_deferred()
    evs = [(e.key, e.count) for e in self.engs if e.count > 0]
    evs += [(d[0], d[2]) for d in self.dsems + self.psems if d[2] > 0]
    for e in self.engs:
        self._wait(e, evs)


KB.barrier = _kb_barrier


D = 1024
NCTX = 256
EPS = 1e-6
FFN = 2816
C_CONV, C_Z, C_XBC, C_DT, C_RQ, C_RK, C_RV, C_RG, C_CQ, C_CKV, C_KR, C_GL = (
    0, 1024, 1536, 2560, 2576, 2832, 3088, 3600, 4112, 4496, 4752, 4784)
N_IN = 8880
E_RQS, E_RKS, E_KRS, N_EXT = 4784, 5040, 5296, 5328


def rope_tables(L):
    def tab(nf, nrows):
        inv = 10000.0 ** (-np.arange(nf, dtype=np.float32) / nf)
        rows = np.arange(nrows, dtype=np.float32)[:, None] * inv
        cols = np.arange(64, dtype=np.float32)[:, None] * inv
        return rows.astype(np.float32), cols.astype(np.float32)
    nrows = max(L // 64, 1)
    out = {}
    rr, cc = tab(16, nrows)
    TRc = np.ones((128, nrows), np.float32); TRs = np.zeros((128, nrows), np.float32)
    TCc = np.ones((128, 64), np.float32); TCs = np.zeros((128, 64), np.float32)
    for p in range(128):
        d = p % 64
        f = d % 32
        sgn = -1.0 if d < 32 else 1.0
        if f < 16:
            TRc[p] = np.cos(rr[:, f]); TRs[p] = sgn * np.sin(rr[:, f])
        else:
            TCc[p] = np.cos(cc[:, f - 16]); TCs[p] = sgn * np.sin(cc[:, f - 16])
    out["ret"] = (TRc, TRs, TCc, TCs)
    rr, cc = tab(8, nrows)
    TRc = np.ones((128, nrows), np.float32); TRs = np.zeros((128, nrows), np.float32)
    TCc = np.ones((128, 64), np.float32); TCs = np.zeros((128, 64), np.float32)
    for key, prange in (("mla", range(64, 96)), ("mlk", range(0, 32))):
        TRc = np.ones((128, nrows), np.float32); TRs = np.zeros((128, nrows), np.float32)
        TCc = np.ones((128, 64), np.float32); TCs = np.zeros((128, 64), np.float32)
        for p in prange:
            d = p % 32
            f = d % 16
            sgn = -1.0 if d < 16 else 1.0
            if f < 8:
                TRc[p] = np.cos(rr[:, f]); TRs[p] = sgn * np.sin(rr[:, f])
            else:
                TCc[p] = np.cos(cc[:, f - 8]); TCs[p] = sgn * np.sin(cc[:, f - 8])
        out[key] = (TRc, TRs, TCc, TCs)
    return out


class Ctx:
    pass


def build(L, n_layers=2, debug=False, upto="Z"):
    nc = bass.Bass("TRN2", target_bir_lowering=False)
    kb = KB(nc, n_dma_sems=48)
    T = NCTX + L
    NR = max(L // 64, 1)
    g = Ctx()
    g.nc, g.kb, g.L, g.T = nc, kb, L, T

    def din(name, shape):
        return nc.dram_tensor(name, list(shape), F32, kind="ExternalInput").ap()

    NL = 2
    I = {}
    I["x"] = din("x", [L, D]); I["ctx"] = din("ctx", [NCTX, D]); I["c2"] = din("c2", [2, D])
    for nm, shp in [("w_ada", [NL, D, 6 * D]), ("b_ada", [NL, 6 * D]), ("norm1_g", [NL, D]), ("norm2_g", [NL, D]),
                    ("w_in", [NL, D, N_IN]), ("conv_w", [NL, 31, 512]), ("conv_b", [NL, 512]),
                    ("conv_ln_g", [NL, 512]), ("conv_ln_b", [NL, 512]), ("ssm_conv_w", [NL, 5, 1024]),
                    ("ssm_conv_b", [NL, 1024]), ("ssm_dt_bias", [NL, 16]), ("ssm_a_log", [NL, 16]),
                    ("ssm_d", [NL, 8]), ("ssm_norm_g", [NL, 512]), ("ret_decay", [NL, 8]),
                    ("ret_gn_g", [NL, 512]), ("ret_gn_b", [NL, 512]), ("mla_q_norm_g", [NL, 384]),
                    ("mla_kv_norm_g", [NL, 256]), ("mla_w_uq", [NL, 384, 768]), ("mla_w_ukv", [NL, 256, 1024]),
                    ("w_branch", [NL, 4, 512, D]), ("w_out", [NL, D, D]), ("w_ffn_in", [NL, D, 2 * FFN]),
                    ("w_ffn_out", [NL, FFN, D]), ("final_norm_g", [1, D]),
                    ("t_ret", [128, 2 * NR + 128]), ("t_mla", [128, 2 * NR + 128]), ("t_mlk", [128, 2 * NR + 128])]:
        I[nm] = din(nm, shp)
    out_d = nc.dram_tensor("out", [L, D], F32, kind="ExternalOutput").ap()

    dbg = {}

    def scratch(name, shape, dt=BF16):
        if debug:
            ap = nc.dram_tensor(name, list(shape), dt, kind="ExternalOutput").ap()
            dbg[name] = ap
        else:
            ap = nc.dram_tensor(name, list(shape), dt, kind="Internal").ap()
        return ap, kb.trk(name)

    S = {}
    for nm, shp, dt in [("Win", [128, 8, N_EXT], BF16), ("Wgl", [128, 8, 4096], BF16), ("Wuq", [128, 3, 1536], BF16),
                        ("Wk", [128, 2, 512], BF16), ("Wv", [128, 2, 512], BF16), ("Wb", [128, 16, D], BF16),
                        ("Wout", [128, 8, D], BF16), ("Wf1", [128, 8, 2 * FFN], BF16), ("Wf2", [128, 22, D], BF16),
                        ("uT", [512, T], BF16), ("zs", [T, 512], BF16), ("xbcT", [1024, T], BF16),
                        ("dt", [T, 16], F32), ("la", [T, 16], F32),
                        ("qrT", [256, T], BF16), ("krT", [256, T], BF16), ("kr", [T, 256], BF16),
                        ("rv", [T, 512], BF16), ("rg", [T, 512], BF16),
                        ("qmT", [8, 96, T], BF16), ("kmT", [512, T], BF16), ("kropeT", [32, T], BF16),
                        ("vm", [T, 512], BF16),
                        ("ssx", [T, 512], BF16), ("ssB", [T, 256], BF16), ("ssBT", [256, T], BF16),
                        ("ssCT", [256, T], BF16), ("yf", [T, 512], F32), ("ryf", [T, 512], F32),
                        ("br0T", [512, T], BF16), ("br1T", [512, T], BF16), ("br2T", [512, T], BF16),
                        ("br3T", [512, T], BF16), ("xs", [T, D], F32)]:
        S[nm] = scratch(nm, shp, dt)
    for nm, kc, n_ in (("Win", 8, N_EXT), ("Wgl", 8, 4096), ("Wb", 16, D), ("Wout", 8, D), ("Wf1", 8, 2 * FFN), ("Wf2", 22, D)):
        S[nm + "_pk"] = (nc.dram_tensor(nm + "_pk", [128, kc * n_], BF16, kind="Internal").ap(), kb.trk(nm + "_pk"))
    g.S, g.I = S, I

    def V(fn, r=(), w=(), **k): return kb.op(kb.dve, fn, r, w, **k)
    def A(fn, r=(), w=(), **k): return kb.op(kb.act, fn, r, w, **k)
    def P(fn, r=(), w=(), **k): return kb.op(kb.pe, fn, r, w, **k)
    def G(fn, r=(), w=(), **k): return kb.op(kb.pool, fn, r, w, **k)
    g.V, g.A, g.P, g.G = V, A, P, G
    rr = [0]

    def VA(fnv, fna, r=(), w=()):
        rr[0] += 1
        if rr[0] % 2:
            return V(fnv, r, w)
        return A(fna, r, w)

    psb = [(kb.ps("psb%d" % i, [128, 512], F32), kb.trk()) for i in range(8)]
    for _, t_ in psb:
        t_.excl = True
    pi = [0]

    g.ps_skip = set()
    g.psb = psb

    def psum():
        pi[0] = (pi[0] + 1) % 8
        while pi[0] in g.ps_skip:
            pi[0] = (pi[0] + 1) % 8
        return psb[pi[0]]
    g.psum = psum

    class Pool:
        def __init__(s, name, shape, dt, n):
            s.tiles = [(kb.sb("%s%d" % (name, i), shape, dt), kb.trk()) for i in range(n)]
            s.i = 0

        def get(s):
            s.i = (s.i + 1) % len(s.tiles)
            return s.tiles[s.i]

    ident_b = kb.sb("ident_b", [128, 128], BF16); t_c = kb.trk()
    ones_b = kb.sb("ones_b", [128, 128], BF16)
    ones_f = kb.sb("ones_f", [128, 128], F32)
    Uf = kb.sb("Uf", [128, 128], F32)
    Lf = kb.sb("Lf", [128, 128], F32)
    G(lambda: nc.gpsimd.memset(ones_b, 1.0), w=[t_c])
    G(lambda: nc.gpsimd.memset(ones_f, 1.0), w=[t_c])
    G(lambda: nc.gpsimd.affine_select(ident_b, ones_b, pattern=[[-1, 128]], compare_op=ALU.is_equal, fill=0.0,
                                      base=0, channel_multiplier=1), r=[t_c], w=[t_c])
    G(lambda: nc.gpsimd.affine_select(Uf, ones_f, pattern=[[1, 128]], compare_op=ALU.is_ge, fill=0.0,
                                      base=0, channel_multiplier=-1), r=[t_c], w=[t_c])
    G(lambda: nc.gpsimd.affine_select(Lf, ones_f, pattern=[[-1, 128]], compare_op=ALU.is_ge, fill=0.0,
                                      base=0, channel_multiplier=1), r=[t_c], w=[t_c])
    g.ident_b, g.ones_b, g.ones_f, g.Uf, g.Lf, g.t_c = ident_b, ones_b, ones_f, Uf, Lf, t_c
    g.VA = VA
    tab_ret = kb.sb("tab_ret", [128, 2 * NR + 128], F32)
    tab_mla = kb.sb("tab_mla", [128, 2 * NR + 128], F32)
    kb.dma(tab_ret, I["t_ret"], writes=[t_c])
    kb.dma(tab_mla, I["t_mla"], writes=[t_c])
    tab_mlk = kb.sb("tab_mlk", [128, 2 * NR + 128], F32)
    kb.dma(tab_mlk, I["t_mlk"], writes=[t_c])
    g.tab_mlk = tab_mlk
    g.tab_ret, g.tab_mla = tab_ret, tab_mla
    g.out_d = out_d
    g.t_out = kb.trk()
    fin_g = kb.sb("fin_g", [128, D], F32)
    g.fin_g = fin_g
    kb.dma(fin_g, I["final_norm_g"][0].partition_broadcast(128), writes=[t_c])

    tiles = [(0, NCTX, True, 0)]
    for t in range(L // 512):
        tiles.append((NCTX + t * 512, 512, False, t * 512))
    g.tiles = tiles

    modc = kb.sb("modc", [128, 6, 8, 2], F32); t_modc = kb.trk()
    modr = kb.sb("modr", [128, 2, 2, D], F32); t_modr = kb.trk()
    G1c = kb.sb("G1c", [128, 8, 2], F32); G2c = kb.sb("G2c", [128, 8, 2], F32); t_gc = kb.trk()
    colv = kb.sb("colv", [128, 64], F32); t_colv = kb.trk()
    g.modc, g.modr, g.G1c, g.G2c = modc, modr, G1c, G2c

    def coldma(dst, v, k, t_dst):
        for j_ in range(k):
            kb.dma(dst[:, j_:j_ + 1], v[j_ * 128:(j_ + 1) * 128].rearrange("(p o) -> p o", o=1), writes=[t_dst])
    g.coldma = coldma

    for layer in range(n_layers):
        last = (layer == n_layers - 1)
        li = layer
        g.packed, g.pk_off = {}, {}
        g.wpf = None
        kb.barrier()
        with nc.sbuf_tensor("w_stg%d" % li, [128, 3, 2048], F32) as stg_t, nc.sbuf_tensor("w_stb%d" % li, [128, 3, 2048], BF16) as stb_t, \
                nc.sbuf_tensor("w_rs%d" % li, [128, 8], F32) as rs_t:
            stg, stb, rs = stg_t.ap(), stb_t.ap(), rs_t.ap()
            t_stg = [kb.trk() for _ in range(3)]; t_stb = [kb.trk() for _ in range(3)]; t_rs = kb.trk()
            wi = [0]

            def prep(src, dst, t_dst, dc0, n, rowscale=None, mul=None, swap=None):
                i = wi[0] % 3; wi[0] += 1
                kb.dma(stg[:, i, :n], src, writes=[t_stg[i]])
                o = stb[:, i, :n]; s_ = stg[:, i, :n]
                if swap is not None:
                    hd = swap
                    ov = o.rearrange("p (h two e) -> p h two e", two=2, e=hd)
                    sv = s_.rearrange("p (h two e) -> p h two e", two=2, e=hd)
                    m_ = 1.0 if mul is None else mul
                    V(lambda: nc.vector.tensor_scalar(ov[:, :, 0, :], sv[:, :, 1, :], m_, None, ALU.mult),
                      r=[t_stg[i]], w=[t_stb[i]])
                    V(lambda: nc.vector.tensor_scalar(ov[:, :, 1, :], sv[:, :, 0, :], m_, None, ALU.mult),
                      r=[t_stg[i]], w=[t_stb[i]])
                elif rowscale is not None:
                    V(lambda: nc.vector.tensor_scalar(o, s_, rowscale, None, ALU.mult), r=[t_stg[i], t_rs], w=[t_stb[i]])
                elif mul is not None:
                    V(lambda: nc.vector.tensor_scalar(o, s_, mul, None, ALU.mult), r=[t_stg[i]], w=[t_stb[i]])
                else:
                    VA(lambda: nc.vector.tensor_copy(o, s_), lambda: nc.scalar.copy(o, s_), r=[t_stg[i]], w=[t_stb[i]])
                kb.dma(dst, o, reads=[t_stb[i]], writes=[t_dst])

            def prep_mat(src2d, K, N, dname, dk0=0, dc0=0, sc0=0, **kw):
                dst, t_dst = S[dname]
                for kc in range(K // 128):
                    for c0 in range(0, N, 2048):
                        n = min(2048, N - c0)
                        prep(src2d[kc * 128:(kc + 1) * 128, sc0 + c0:sc0 + c0 + n],
                             dst[:, dk0 + kc, dc0 + c0:dc0 + c0 + n], t_dst, 0, n, **kw)

            w_in = I["w_in"][li]
            prep_mat(w_in, D, C_RK, "Win")
            prep_mat(w_in, D, 256, "Win", dc0=C_RK, sc0=C_RK, mul=0.125)
            prep_mat(w_in, D, C_GL - C_RV, "Win", dc0=C_RV, sc0=C_RV)
            prep_mat(w_in, D, 256, "Win", dc0=E_RQS, sc0=C_RQ, swap=32)
            prep_mat(w_in, D, 256, "Win", dc0=E_RKS, sc0=C_RK, swap=32, mul=0.125)
            prep_mat(w_in, D, 32, "Win", dc0=E_KRS, sc0=C_KR, swap=16)
            prep_mat(w_in, D, 4096, "Wgl", sc0=C_GL)
            coldma(rs[:, 0:3], I["mla_q_norm_g"][li], 3, t_rs)
            coldma(rs[:, 3:5], I["mla_kv_norm_g"][li], 2, t_rs)
            uq = I["mla_w_uq"][li]
            dst, t_dst = S["Wuq"]
            for kc in range(3):
                i = wi[0] % 3; wi[0] += 1
                kb.dma(stg[:, i, :768], uq[kc * 128:(kc + 1) * 128, :], writes=[t_stg[i]])
                o = stb[:, i, :1536]; s_ = stg[:, i, :768]
                V(lambda: nc.vector.tensor_scalar(o[:, 0:768], s_, rs[:, kc:kc + 1], None, ALU.mult),
                  r=[t_stg[i], t_rs], w=[t_stb[i]])
                ov = o[:, 768:1536].rearrange("p (h e) -> p h e", e=96)
                sv = o[:, 0:768].rearrange("p (h e) -> p h e", e=96)
                V(lambda: nc.vector.tensor_copy(ov[:, :, 0:64], sv[:, :, 0:64]), r=[t_stb[i]], w=[t_stb[i]])
                V(lambda: nc.vector.tensor_copy(ov[:, :, 64:80], sv[:, :, 80:96]), r=[t_stb[i]], w=[t_stb[i]])
                V(lambda: nc.vector.tensor_copy(ov[:, :, 80:96], sv[:, :, 64:80]), r=[t_stb[i]], w=[t_stb[i]])
                kb.dma(dst[:, kc, :], o, reads=[t_stb[i]], writes=[t_dst])
            ukv = I["mla_w_ukv"][li]
            for kc in range(2):
                i = wi[0] % 3; wi[0] += 1
                kb.dma(stg[:, i, :1024], ukv[kc * 128:(kc + 1) * 128, :], writes=[t_stg[i]])
                o = stb[:, i, :1024]; s_ = stg[:, i, :1024]
                sv = s_.rearrange("p (h two e) -> p h two e", two=2, e=64)
                ov = o.rearrange("p (two h e) -> p two h e", two=2, e=64)
                V(lambda: nc.vector.tensor_scalar(ov[:, 0], sv[:, :, 0, :], rs[:, 3 + kc:4 + kc], None, ALU.mult),
                  r=[t_stg[i], t_rs], w=[t_stb[i]])
                V(lambda: nc.vector.tensor_scalar(ov[:, 1], sv[:, :, 1, :], rs[:, 3 + kc:4 + kc], None, ALU.mult),
                  r=[t_stg[i], t_rs], w=[t_stb[i]])
                kb.dma(S["Wk"][0][:, kc, :], o[:, 0:512], reads=[t_stb[i]], writes=[S["Wk"][1]])
                kb.dma(S["Wv"][0][:, kc, :], o[:, 512:1024], reads=[t_stb[i]], writes=[S["Wv"][1]])
            for b in range(4):
                prep_mat(I["w_branch"][li, b], 512, D, "Wb", dk0=b * 4)
            prep_mat(I["w_out"][li], D, D, "Wout")
            prep_mat(I["w_ffn_in"][li], D, 2 * FFN, "Wf1")
            prep_mat(I["w_ffn_out"][li], FFN, D, "Wf2")
        if upto == "W":
            break
        kb.barrier()
        with nc.sbuf_tensor("m_w%d" % li, [128, 8, 1024], F32) as mw_t, nc.sbuf_tensor("m_cs%d" % li, [128, 8, 2], F32) as cs_t, \
                nc.sbuf_tensor("m_rep%d" % li, [128, 8, 2, 128], F32) as rep_t, nc.sbuf_tensor("m_bc%d" % li, [128, 48], F32) as bc_t, \
                nc.sbuf_tensor("m_br%d" % li, [128, 2, D], F32) as br_t, nc.sbuf_tensor("m_ng%d" % li, [128, 16], F32) as ng_t:
            mw, cs, rep, bc, br, ng = mw_t.ap(), cs_t.ap(), rep_t.ap(), bc_t.ap(), br_t.ap(), ng_t.ap()
            t_mw, t_cs, t_rep, t_bc, t_br, t_ng = [kb.trk() for _ in range(6)]
            for j in range(2):
                for k_ in range(8):
                    kb.dma(cs[:, k_, j:j + 1], I["c2"][j, k_ * 128:(k_ + 1) * 128].rearrange("(p o) -> p o", o=1), writes=[t_cs])
            A(lambda: nc.scalar.activation(cs, cs, AF.Silu), r=[t_cs], w=[t_cs])
            V(lambda: nc.vector.tensor_copy(rep, cs.unsqueeze(3).broadcast_to([128, 8, 2, 128])), r=[t_cs], w=[t_rep])
            coldma(bc, I["b_ada"][li], 48, t_bc)
            kb.dma(br[:, 0, :], I["b_ada"][li, 2 * D:3 * D].partition_broadcast(128), writes=[t_br])
            kb.dma(br[:, 1, :], I["b_ada"][li, 5 * D:6 * D].partition_broadcast(128), writes=[t_br])
            coldma(ng[:, 0:8], I["norm1_g"][li], 8, t_ng)
            coldma(ng[:, 8:16], I["norm2_g"][li], 8, t_ng)
            for m in range(6):
                kb.dma(mw, I["w_ada"][li][:, m * 1024:(m + 1) * 1024].rearrange("(k p) n -> p k n", p=128), writes=[t_mw])
                for n8 in range(8):
                    ps_, tp = psum()
                    for k in range(8):
                        P(lambda: nc.tensor.matmul(ps_[:, 0:2], mw[:, k, n8 * 128:(n8 + 1) * 128], cs[:, k, :],
                                                   start=(k == 0), stop=(k == 7)), r=[t_mw, t_cs], w=[tp], acc=(k > 0), inc=(k == 7))
                    V(lambda: nc.vector.tensor_scalar(modc[:, m, n8, :], ps_[:, 0:2], bc[:, m * 8 + n8:m * 8 + n8 + 1], None, ALU.add),
                      r=[tp, t_bc], w=[t_modc])
                if m in (2, 5):
                    gi = 0 if m == 2 else 1
                    for j in range(2):
                        for hf in range(2):
                            ps_, tp = psum()
                            for k in range(8):
                                P(lambda: nc.tensor.matmul(ps_, rep[:, k, j, :], mw[:, k, hf * 512:(hf + 1) * 512],
                                                           start=(k == 0), stop=(k == 7)), r=[t_mw, t_rep], w=[tp], acc=(k > 0), inc=(k == 7))
                            V(lambda: nc.vector.tensor_tensor(modr[:, j, gi, hf * 512:(hf + 1) * 512], ps_,
                                                              br[:, gi, hf * 512:(hf + 1) * 512], ALU.add), r=[tp, t_br], w=[t_modr])
            for (Gc, ms, o8) in ((G1c, 1, 0), (G2c, 4, 8)):
                V(lambda: nc.vector.tensor_scalar(Gc, modc[:, ms], 1.0, None, ALU.add), r=[t_modc], w=[t_gc])
                V(lambda: nc.vector.tensor_tensor(Gc, Gc, ng[:, o8:o8 + 8].unsqueeze(2).broadcast_to([128, 8, 2]), ALU.mult),
                  r=[t_ng, t_gc], w=[t_gc])
        g.t_modc, g.t_modr, g.t_gc = t_modc, t_modr, t_gc
        if upto == "M":
            break
        from contextlib import ExitStack
        done = False
        for ph in "ABCDEF":
            fn = {"A": phase_A, "B": phase_B, "C": phase_C, "D": phase_D, "E": phase_E, "F": phase_F}[ph]
            kb.barrier()
            with ExitStack() as st:
                kb.stack = st
                fn(g, li) if ph == 'A' else fn(g, li, last)
                g.wpf = None
                kb.barrier()
            kb.stack = None
            if upto == ph:
                done = True
                break
        if done:
            break

    kb.barrier()
    if debug:
        for nm, ap in (("d_modc", modc), ("d_modr", modr), ("d_G1c", G1c)):
            d_ = nc.dram_tensor(nm, list(ap.shape), F32, kind="ExternalOutput").ap()
            kb.dma(d_, ap, reads=[t_modc, t_modr, t_gc], writes=[kb.trk()])
            dbg[nm] = d_
    kb.barrier()
    kb.finish([g.t_out])
    return nc, dbg


ASTOP = 99


class RPool:
    def __init__(s, kb, name, shape, dt, n):
        s.tiles = [(kb.sb("%s%d" % (name, i), shape, dt), kb.trk()) for i in range(n)]
        s.i = 0

    def get(s):
        s.i = (s.i + 1) % len(s.tiles)
        return s.tiles[s.i]


def mk_common(g):
    kb = g.kb
    c = Ctx()
    c.xt = RPool(kb, "c_xt", [128, 4, D], F32, 1)
    c.xn = RPool(kb, "c_xn", [128, 4, D], BF16, 1)
    c.hT = RPool(kb, "c_hT", [128, 8, 512], BF16, 2)
    c.ss = RPool(kb, "c_ss", [128, 8], F32, 2)
    c.junk = RPool(kb, "c_junk", [128, D], BF16, 1)
    c.wt = RPool(kb, "c_wt", [128, 8, 512], BF16, 4)
    c.stb = RPool(kb, "c_stb", [128, 512], BF16, 20)
    c.stf = RPool(kb, "c_stf", [128, 512], F32, 8)
    return c


def norm_tile(g, c, src, t_src, n, Gc, Sc, j, xt_pair=None):
    kb, nc, V, A, P = g.kb, g.nc, g.V, g.A, g.P
    nb = n // 128
    if xt_pair is None:
        xt, t_xt = c.xt.get()
        kb.dma(xt[:, :nb, :], src.rearrange("(b p) d -> p b d", p=128), reads=[t_src], writes=[t_xt])
    else:
        xt, t_xt = xt_pair
    ss, t_ss = c.ss.get()
    junk, t_junk = c.junk.get()
    for b in range(nb):
        A(lambda: nc.scalar.activation(junk, xt[:, b, :], AF.Square, accum_out=ss[:, b:b + 1]), r=[t_xt], w=[t_junk, t_ss])
    V(lambda: nc.vector.tensor_scalar(ss[:, 0:nb], ss[:, 0:nb], 1.0 / D, EPS, ALU.mult, ALU.add), r=[t_ss], w=[t_ss])
    A(lambda: nc.scalar.activation(ss[:, 0:nb], ss[:, 0:nb], AF.Sqrt), r=[t_ss], w=[t_ss])
    V(lambda: nc.vector.reciprocal(ss[:, 0:nb], ss[:, 0:nb]), r=[t_ss], w=[t_ss])
    xn, t_xn = c.xn.get()
    for b in range(nb):
        V(lambda: nc.vector.tensor_scalar(xn[:, b, :], xt[:, b, :], ss[:, b:b + 1], None, ALU.mult), r=[t_xt, t_ss], w=[t_xn])
    hT, t_hT = c.hT.get()
    for k in range(8):
        ps_, tp = g.psum()
        psv = ps_.bitcast(BF16)
        for b in range(nb):
            P(lambda: nc.tensor.transpose(psv[:, b * 128:(b + 1) * 128], xn[:, b, k * 128:(k + 1) * 128], g.ident_b),
              r=[t_xn, g.t_c], w=[tp], acc=(b > 0), inc=(b == nb - 1))
        V(lambda: nc.vector.tensor_scalar(hT[:, k, :n], psv[:, :n], Gc[:, k, j:j + 1], Sc[:, k, j:j + 1], ALU.mult, ALU.add),
          r=[tp, g.t_gc, g.t_modc], w=[t_hT])
    return hT, t_hT, xt, t_xt, ss, t_ss


def _wissue(g, c, Wname, kc0, KC, c0, w, pool):
    W, t_W = g.S[Wname]
    Wp, t_Wp = g.S[Wname + "_pk"]
    key = (Wname, kc0, KC, c0, w)
    off = g.packed.get(key)
    if off is None:
        off = g.pk_off.get(Wname, 0)
        g.pk_off[Wname] = off + KC * w
        g.packed[key] = off
        g.kb.dma(Wp[:, off:off + KC * w].rearrange("p (k n) -> p k n", n=w), W[:, kc0:kc0 + KC, c0:c0 + w],
                 reads=[t_W], writes=[t_Wp])
    wt, t_w = (pool or c.wt).get()
    flat = wt.rearrange("p k n -> p (k n)")
    g.kb.dma(flat[:, :KC * w], Wp[:, off:off + KC * w], reads=[t_Wp], writes=[t_w])
    return flat[:, :KC * w].rearrange("p (k n) -> p k n", n=w), t_w


def wload(g, c, Wname, kc0, KC, c0, w, pool=None):
    req = (Wname, kc0, KC, c0, w, pool)
    st = g.wpf
    if st is None or st["mode"] == "plain":
        return _wissue(g, c, *req)
    if st["mode"] == "record":
        st["seq"].append(req)
        return _wissue(g, c, *req)
    seq = st["seq"]; n = len(seq)
    r = st["consumed"]
    assert seq[r % n][:5] == req[:5], (seq[r % n][:5], req[:5])
    while st["issued"] <= r + st["depth"]:
        q = seq[st["issued"] % n]
        if q[5] is not None and st["issued"] > r:
            break
        st["map"][st["issued"]] = _wissue(g, c, *q)
        st["issued"] += 1
    st["consumed"] += 1
    return st["map"].pop(r)


def wpf_tile(g, is_ctx):
    st = g.wpf
    if is_ctx:
        st["mode"] = "plain"
    elif not st["seq"]:
        st["mode"] = "record"
    else:
        st["mode"] = "prefetch"


def wpf_begin(g):
    g.wpf = {"mode": "plain", "seq": [], "consumed": 0, "issued": 0, "map": {}, "depth": 2}


def fm_chunks(g, c, Wname, KC, c0, ncols, hT, t_h, n, kc0=0, pool=None, msz=128):
    nc, P = g.nc, g.P
    for cc in range(c0, c0 + ncols, 512):
        w = min(512, c0 + ncols - cc)
        wt, t_w = wload(g, c, Wname, kc0, KC, cc, w, pool)
        for j in range(0, w, msz):
            m = min(msz, w - j)
            ps_, tp = g.psum()
            for k in range(KC):
                P(lambda: nc.tensor.matmul(ps_[:m, :n], wt[:, k, j:j + m], hT[:, k, :n], start=(k == 0), stop=(k == KC - 1)),
                  r=[t_w, t_h], w=[tp], acc=(k > 0), inc=(k == KC - 1))
            yield (cc - c0 + j, m, ps_, tp)


def tm_chunks(g, c, Wname, KC, c0, ncols, hT, t_h, n, kc0=0, pool=None):
    nc, P = g.nc, g.P
    for cc in range(c0, c0 + ncols, 512):
        w = min(512, c0 + ncols - cc)
        wt, t_w = wload(g, c, Wname, kc0, KC, cc, w, pool)
        for b in range(n // 128):
            ps_, tp = g.psum()
            for k in range(KC):
                P(lambda: nc.tensor.matmul(ps_[:, :w], hT[:, k, b * 128:(b + 1) * 128], wt[:, k, :w], start=(k == 0), stop=(k == KC - 1)),
                  r=[t_w, t_h], w=[tp], acc=(k > 0), inc=(k == KC - 1))
            yield (cc - c0, w, b, ps_, tp)


def store(g, name, dram_ap, sb_ap, t_sb):
    g.kb.dma_deferred(dram_ap, sb_ap, reads=[t_sb], writes=[g.S[name][1]])


def phase_A(g, li):
    kb, nc, V, A, P, G = g.kb, g.nc, g.V, g.A, g.P, g.G
    S, I, L, T = g.S, g.I, g.L, g.T
    NR = max(L // 64, 1)
    c = mk_common(g)
    cos_r = kb.sb("a_cos_r", [128, 512], F32); sin_r = kb.sb("a_sin_r", [128, 512], F32)
    cos_m = kb.sb("a_cos_m", [128, 512], F32); sin_m = kb.sb("a_sin_m", [128, 512], F32); t_tab = kb.trk()
    cos_k = kb.sb("a_cos_k", [128, 512], F32); sin_k = kb.sb("a_sin_k", [128, 512], F32)
    dtb = kb.sb("a_dtb", [128, 32], F32); t_dtb = kb.trk()
    kb.dma(dtb[:, 0:16], I["ssm_dt_bias"][li].partition_broadcast(128), writes=[t_dtb])
    kb.dma(dtb[:, 16:32], I["ssm_a_log"][li].partition_broadcast(128), writes=[t_dtb])
    A(lambda: nc.scalar.activation(dtb[:, 16:32], dtb[:, 16:32], AF.Exp), r=[t_dtb], w=[t_dtb])
    V(lambda: nc.vector.tensor_scalar(dtb[:, 16:32], dtb[:, 16:32], -1.0, None, ALU.mult), r=[t_dtb], w=[t_dtb])
    cq = RPool(kb, "a_cq", [128, 3, 512], BF16, 1)
    sq = RPool(kb, "a_sq", [128, 3, 512], BF16, 1)
    rsb = RPool(kb, "a_rsb", [128, 512], F32, 2)
    wuq = kb.sb("a_wuq", [128, 3, 1536], BF16); t_wuq = kb.trk()
    wk = kb.sb("a_wk", [128, 2, 512], BF16); wv = kb.sb("a_wv", [128, 2, 512], BF16); t_wkv = kb.trk()
    kb.dma(wuq, S["Wuq"][0], reads=[S["Wuq"][1]], writes=[t_wuq])
    kb.dma(wk, S["Wk"][0], reads=[S["Wk"][1]], writes=[t_wkv])
    kb.dma(wv, S["Wv"][0], reads=[S["Wv"][1]], writes=[t_wkv])
    tk_ = RPool(kb, "a_tk", [128, 4, 128], BF16, 6)
    sm = RPool(kb, "a_sm", [128, 16], F32, 2)
    st16 = RPool(kb, "a_st16", [128, 32], F32, 12)
    SC_Q = 96.0 ** -0.5

    wpf_begin(g)
    for (t0, n, is_ctx, pos0) in g.tiles:
        wpf_tile(g, is_ctx)
        j = 1 if is_ctx else 0
        nb = n // 128
        src = (I["ctx"] if is_ctx else I["x"][pos0:pos0 + n]) if li == 0 else S["xs"][0][t0:t0 + n]
        t_src = kb.trk() if li == 0 else S["xs"][1]
        hT, t_h, _, _, _, _ = norm_tile(g, c, src, t_src, n, g.G1c, g.modc[:, 0], j)
        if not is_ctx:
            r0 = pos0 // 64
            for (tab, cs_, sn_) in ((g.tab_ret, cos_r, sin_r), (g.tab_mla, cos_m, sin_m), (g.tab_mlk, cos_k, sin_k)):
                V(lambda: nc.vector.tensor_tensor(cs_.rearrange("p (r c) -> p r c", c=64),
                                                  tab[:, r0:r0 + 8].unsqueeze(2).broadcast_to([128, 8, 64]),
                                                  tab[:, 2 * NR:2 * NR + 64].unsqueeze(1).broadcast_to([128, 8, 64]), ALU.mult),
                  r=[g.t_c], w=[t_tab])
                V(lambda: nc.vector.tensor_tensor(sn_.rearrange("p (r c) -> p r c", c=64),
                                                  tab[:, NR + r0:NR + r0 + 8].unsqueeze(2).broadcast_to([128, 8, 64]),
                                                  tab[:, 2 * NR + 64:2 * NR + 128].unsqueeze(1).broadcast_to([128, 8, 64]), ALU.add),
                  r=[g.t_c], w=[t_tab])
        if ASTOP == 0:
            return
        ga = fm_chunks(g, c, "Win", 8, C_CONV, 512, hT, t_h, n)
        gg = fm_chunks(g, c, "Win", 8, C_CONV + 512, 512, hT, t_h, n)
        for (off, m, pa, tpa), (_, _, pg, tpg) in zip(ga, gg):
            sg, t_sg = c.stf.get()
            A(lambda: nc.scalar.activation(sg[:, :n], pg[:, :n], AF.Sigmoid), r=[tpg], w=[t_sg])
            u, t_u = c.stb.get()
            V(lambda: nc.vector.tensor_tensor(u[:, :n], pa[:, :n], sg[:, :n], ALU.mult), r=[tpa, t_sg], w=[t_u])
            store(g, "uT", S["uT"][0][off:off + 128, t0:t0 + n], u[:, :n], t_u)
        if ASTOP == 1:
            return
        for (c0, nm, fn) in ((C_Z, "zs", AF.Silu), (C_RV, "rv", AF.Copy), (C_RG, "rg", AF.Silu)):
            for (off, w, b, ps_, tp) in tm_chunks(g, c, "Win", 8, c0, 512, hT, t_h, n):
                o, t_o = c.stb.get()
                A(lambda: nc.scalar.activation(o, ps_, fn), r=[tp], w=[t_o])
                store(g, nm, S[nm][0][t0 + b * 128:t0 + (b + 1) * 128, :], o, t_o)
        if ASTOP == 2:
            return
        for (off, m, ps_, tp) in fm_chunks(g, c, "Win", 8, C_XBC, 1024, hT, t_h, n):
            o, t_o = c.stb.get()
            g.VA(lambda: nc.vector.tensor_copy(o[:, :n], ps_[:, :n]), lambda: nc.scalar.copy(o[:, :n], ps_[:, :n]), r=[tp], w=[t_o])
            store(g, "xbcT", S["xbcT"][0][off:off + 128, t0:t0 + n], o[:, :n], t_o)
        if ASTOP == 3:
            return
        for (off, w, b, ps_, tp) in tm_chunks(g, c, "Win", 8, C_DT, 16, hT, t_h, n):
            o, t_o = st16.get()
            V(lambda: nc.vector.tensor_tensor(o[:, 0:16], ps_[:, 0:16], dtb[:, 0:16], ALU.add), r=[tp, t_dtb], w=[t_o])
            A(lambda: nc.scalar.activation(o[:, 0:16], o[:, 0:16], AF.Exp), r=[t_o], w=[t_o])
            A(lambda: nc.scalar.activation(o[:, 0:16], o[:, 0:16], AF.Ln, bias=1.0), r=[t_o], w=[t_o])
            V(lambda: nc.vector.tensor_tensor(o[:, 16:32], o[:, 0:16], dtb[:, 16:32], ALU.mult), r=[t_o, t_dtb], w=[t_o])
            store(g, "dt", S["dt"][0][t0 + b * 128:t0 + (b + 1) * 128, :], o[:, 0:16], t_o)
            store(g, "la", S["la"][0][t0 + b * 128:t0 + (b + 1) * 128, :], o[:, 16:32], t_o)
        if ASTOP == 4:
            return
        for (cN, cS_, nm, is_k) in ((C_RQ, E_RQS, "qrT", False), (C_RK, E_RKS, "krT", True)):
            gn = fm_chunks(g, c, "Win", 8, cN, 256, hT, t_h, n)
            gs = fm_chunks(g, c, "Win", 8, cS_, 256, hT, t_h, n) if not is_ctx else None
            for ci in range(2):
                off, m, pn, tpn = next(gn)
                o, t_o = c.stb.get()
                if is_ctx:
                    V(lambda: nc.vector.tensor_copy(o[:, :n], pn[:, :n]), r=[tpn], w=[t_o])
                else:
                    _, _, pw, tpw = next(gs)
                    t1, t_1 = c.stf.get(); t2, t_2 = c.stf.get()
                    V(lambda: nc.vector.tensor_tensor(t1[:, :n], pn[:, :n], cos_r[:, :n], ALU.mult), r=[tpn, t_tab], w=[t_1])
                    V(lambda: nc.vector.tensor_tensor(t2[:, :n], pw[:, :n], sin_r[:, :n], ALU.mult), r=[tpw, t_tab], w=[t_2])
                    G(lambda: nc.gpsimd.tensor_tensor(o[:, :n], t1[:, :n], t2[:, :n], ALU.add), r=[t_1, t_2], w=[t_o])
                store(g, nm, S[nm][0][off:off + 128, t0:t0 + n], o[:, :n], t_o)
                if is_k:
                    ps_, tp = g.psum(); psv = ps_.bitcast(BF16)
                    for b in range(nb):
                        P(lambda: nc.tensor.transpose(psv[:, b * 128:(b + 1) * 128], o[:, b * 128:(b + 1) * 128], g.ident_b),
                          r=[t_o, g.t_c], w=[tp], acc=(b > 0), inc=(b == nb - 1))
                    kt, t_kt = tk_.get()
                    A(lambda: nc.scalar.copy(kt[:, :nb, :], psv[:, :n].rearrange("p (b f) -> p b f", f=128)), r=[tp], w=[t_kt])
                    store(g, "kr", S["kr"][0][t0:t0 + n, off:off + 128].rearrange("(b p) f -> p b f", p=128), kt[:, :nb, :], t_kt)
            if gs is not None:
                for _ in gs:
                    pass
            for _ in gn:
                pass
        if ASTOP == 5:
            return
        cqT, t_cq = cq.get(); sqT, t_sq = sq.get()
        for (off, m, ps_, tp) in fm_chunks(g, c, "Win", 8, C_CQ, 384, hT, t_h, n):
            ci = off // 128
            V(lambda: nc.vector.tensor_copy(cqT[:, ci, :n], ps_[:, :n]), r=[tp], w=[t_cq])
            A(lambda: nc.scalar.activation(sqT[:, ci, :n], ps_[:, :n], AF.Square), r=[tp], w=[t_sq])
        if ASTOP == 51:
            return
        ps_, tp = g.psum()
        for ci in range(3):
            P(lambda: nc.tensor.matmul(ps_[:, :n], g.ones_b, sqT[:, ci, :n], start=(ci == 0), stop=(ci == 2)),
              r=[t_sq, g.t_c], w=[tp], acc=(ci > 0), inc=(ci == 2))
        rq_, t_rq = rsb.get()
        V(lambda: nc.vector.tensor_scalar(rq_[:, :n], ps_[:, :n], 96.0 / 384.0, EPS * 96.0, ALU.mult, ALU.add), r=[tp], w=[t_rq])
        A(lambda: nc.scalar.activation(rq_[:, :n], rq_[:, :n], AF.Sqrt), r=[t_rq], w=[t_rq])
        V(lambda: nc.vector.reciprocal(rq_[:, :n], rq_[:, :n]), r=[t_rq], w=[t_rq])
        if ASTOP == 50:
            return
        for h in range(8):
            pn, tpn = g.psum()
            for ci in range(3):
                P(lambda: nc.tensor.matmul(pn[:96, :n], wuq[:, ci, h * 96:(h + 1) * 96], cqT[:, ci, :n], start=(ci == 0), stop=(ci == 2)),
                  r=[t_wuq, t_cq], w=[tpn], acc=(ci > 0), inc=(ci == 2))
            o, t_o = c.stb.get()
            if is_ctx:
                V(lambda: nc.vector.tensor_tensor(o[:96, :n], pn[:96, :n], rq_[:96, :n], ALU.mult), r=[tpn, t_rq], w=[t_o])
            else:
                pw, tpw = g.psum()
                for ci in range(3):
                    P(lambda: nc.tensor.matmul(pw[:96, :n], wuq[:, ci, 768 + h * 96:768 + (h + 1) * 96], cqT[:, ci, :n],
                                               start=(ci == 0), stop=(ci == 2)), r=[t_wuq, t_cq], w=[tpw], acc=(ci > 0), inc=(ci == 2))
                t1, t_1 = c.stf.get(); t2, t_2 = c.stf.get()
                V(lambda: nc.vector.tensor_tensor(t1[:96, :n], pn[:96, :n], cos_m[:96, :n], ALU.mult), r=[tpn, t_tab], w=[t_1])
                V(lambda: nc.vector.tensor_tensor(t2[:96, :n], pw[:96, :n], sin_m[:96, :n], ALU.mult), r=[tpw, t_tab], w=[t_2])
                G(lambda: nc.gpsimd.tensor_tensor(t1[:96, :n], t1[:96, :n], t2[:96, :n], ALU.add), r=[t_1, t_2], w=[t_1])
                V(lambda: nc.vector.tensor_tensor(o[:96, :n], t1[:96, :n], rq_[:96, :n], ALU.mult), r=[t_1, t_rq], w=[t_o])
            store(g, "qmT", S["qmT"][0][h, :, t0:t0 + n], o[:96, :n], t_o)
        if ASTOP == 6:
            return
        ckT, t_ck = cq.get(); sk, t_sk = sq.get()
        for (off, m, ps_, tp) in fm_chunks(g, c, "Win", 8, C_CKV, 256, hT, t_h, n):
            ci = off // 128
            V(lambda: nc.vector.tensor_copy(ckT[:, ci, :n], ps_[:, :n]), r=[tp], w=[t_ck])
            A(lambda: nc.scalar.activation(sk[:, ci, :n], ps_[:, :n], AF.Square), r=[tp], w=[t_sk])
        ps_, tp = g.psum()
        for ci in range(2):
            P(lambda: nc.tensor.matmul(ps_[:, :n], g.ones_b, sk[:, ci, :n], start=(ci == 0), stop=(ci == 1)),
              r=[t_sk, g.t_c], w=[tp], acc=(ci > 0), inc=(ci == 1))
        rk_, t_rk = rsb.get()
        V(lambda: nc.vector.tensor_scalar(rk_[:, :n], ps_[:, :n], 1.0 / 256.0, EPS, ALU.mult, ALU.add), r=[tp], w=[t_rk])
        A(lambda: nc.scalar.activation(rk_[:, :n], rk_[:, :n], AF.Sqrt), r=[t_rk], w=[t_rk])
        V(lambda: nc.vector.reciprocal(rk_[:, :n], rk_[:, :n]), r=[t_rk], w=[t_rk])
        for hc in range(4):
            pn, tpn = g.psum()
            for ci in range(2):
                P(lambda: nc.tensor.matmul(pn[:, :n], wk[:, ci, hc * 128:(hc + 1) * 128], ckT[:, ci, :n], start=(ci == 0), stop=(ci == 1)),
                  r=[t_wkv, t_ck], w=[tpn], acc=(ci > 0), inc=(ci == 1))
            o, t_o = c.stb.get()
            V(lambda: nc.vector.tensor_tensor(o[:, :n], pn[:, :n], rk_[:, :n], ALU.mult), r=[tpn, t_rk], w=[t_o])
            store(g, "kmT", S["kmT"][0][hc * 128:(hc + 1) * 128, t0:t0 + n], o[:, :n], t_o)
        if ASTOP == 7:
            return
        smt, t_sm = sm.get()
        ps_, tp = g.psum()
        for b in range(nb):
            for ci in range(2):
                P(lambda: nc.tensor.matmul(ps_[:, b:b + 1], sk[:, ci, b * 128:(b + 1) * 128], g.ones_b[:, 0:1], start=(ci == 0), stop=(ci == 1)),
                  r=[t_sk, g.t_c], w=[tp], acc=(ci > 0 or b > 0), inc=(ci == 1 and b == nb - 1))
        V(lambda: nc.vector.tensor_scalar(smt[:, :nb], ps_[:, :nb], 1.0 / 256.0, EPS, ALU.mult, ALU.add), r=[tp], w=[t_sm])
        A(lambda: nc.scalar.activation(smt[:, :nb], smt[:, :nb], AF.Sqrt), r=[t_sm], w=[t_sm])
        V(lambda: nc.vector.reciprocal(smt[:, :nb], smt[:, :nb]), r=[t_sm], w=[t_sm])
        for b in range(nb):
            pn, tpn = g.psum()
            for ci in range(2):
                P(lambda: nc.tensor.matmul(pn, ckT[:, ci, b * 128:(b + 1) * 128], wv[:, ci, :], start=(ci == 0), stop=(ci == 1)),
                  r=[t_wkv, t_ck], w=[tpn], acc=(ci > 0), inc=(ci == 1))
            o, t_o = c.stb.get()
            A(lambda: nc.scalar.activation(o, pn, AF.Identity, scale=smt[:, b:b + 1]), r=[tpn, t_sm], w=[t_o])
            store(g, "vm", S["vm"][0][t0 + b * 128:t0 + (b + 1) * 128, :], o, t_o)
        if ASTOP == 8:
            return
        gn = fm_chunks(g, c, "Win", 8, C_KR, 32, hT, t_h, n)
        off, m, pn, tpn = next(gn)
        o, t_o = c.stb.get()
        if is_ctx:
            V(lambda: nc.vector.tensor_copy(o[:32, :n], pn[:32, :n]), r=[tpn], w=[t_o])
        else:
            gs = fm_chunks(g, c, "Win", 8, E_KRS, 32, hT, t_h, n)
            _, _, pw, tpw = next(gs)
            t1, t_1 = c.stf.get(); t2, t_2 = c.stf.get()
            V(lambda: nc.vector.tensor_tensor(t1[:32, :n], pn[:32, :n], cos_k[:32, :n], ALU.mult), r=[tpn, t_tab], w=[t_1])
            V(lambda: nc.vector.tensor_tensor(t2[:32, :n], pw[:32, :n], sin_k[:32, :n], ALU.mult), r=[tpw, t_tab], w=[t_2])
            G(lambda: nc.gpsimd.tensor_tensor(o[:32, :n], t1[:32, :n], t2[:32, :n], ALU.add), r=[t_1, t_2], w=[t_o])
            for _ in gs:
                pass
        for _ in gn:
            pass
        store(g, "kropeT", S["kropeT"][0][:, t0:t0 + n], o[:32, :n], t_o)


def phase_B(g, li, last=False):
    kb, nc, V, A, P, G = g.kb, g.nc, g.V, g.A, g.P, g.G
    S, I, L, T = g.S, g.I, g.L, g.T
    cwT = kb.sb("b_cwT", [128, 4, 31], F32); t_cw = kb.trk()
    for tap in range(31):
        for j in range(4):
            kb.dma(cwT[:, j, tap:tap + 1], I["conv_w"][li, tap, j * 128:(j + 1) * 128].rearrange("(p o) -> p o", o=1), writes=[t_cw])
    vec = kb.sb("b_vec", [128, 12], F32); t_vec = kb.trk()
    g.coldma(vec[:, 0:4], I["conv_b"][li], 4, t_vec)
    g.coldma(vec[:, 4:8], I["conv_ln_g"][li], 4, t_vec)
    g.coldma(vec[:, 8:12], I["conv_ln_b"][li], 4, t_vec)
    Dg = kb.sb("b_Dg", [128, 4, 31, 128], BF16); t_Dg = kb.trk()
    for j in range(4):
        for tap in range(31):
            V(lambda: nc.vector.tensor_scalar(Dg[:, j, tap, :], g.ident_b, cwT[:, j, tap:tap + 1], None, ALU.mult),
              r=[t_cw, g.t_c], w=[t_Dg])
    ut_p = RPool(kb, "b_ut", [128, 4, 542], BF16, 2)
    hc_p = RPool(kb, "b_hc", [128, 4, 512], F32, 2)
    hb_p = RPool(kb, "b_hb", [128, 4, 512], BF16, 2)
    sq_p = RPool(kb, "b_sq", [128, 4, 512], BF16, 2)
    st_p = RPool(kb, "b_st", [128, 512], F32, 6)
    ob_p = RPool(kb, "b_ob", [128, 512], BF16, 12)
    uT3 = S["uT"][0].rearrange("(j p) t -> p j t", p=128)
    for (t0, n, is_ctx, pos0) in g.tiles:
        if is_ctx and last:
            continue
        s_lo, s_hi = (0, NCTX) if is_ctx else (NCTX, T)
        lo = max(s_lo, t0 - 15); hi = min(s_hi, t0 + n + 15)
        ut, t_ut = ut_p.get()
        if lo > t0 - 15 or hi < t0 + n + 15:
            V(lambda: nc.vector.memset(ut, 0.0), w=[t_ut])
        kb.dma(ut[:, :, lo - (t0 - 15):hi - (t0 - 15)], uT3[:, :, lo:hi], reads=[S["uT"][1]], writes=[t_ut])
        hc, t_hc = hc_p.get(); hb, t_hb = hb_p.get(); sq, t_sq = sq_p.get()
        for j in range(4):
            ps_, tp = g.psum()
            for tap in range(31):
                P(lambda: nc.tensor.matmul(ps_[:, :n], Dg[:, j, tap, :], ut[:, j, tap:tap + n], start=(tap == 0), stop=(tap == 30)),
                  r=[t_Dg, t_ut], w=[tp], acc=(tap > 0), inc=(tap == 30))
            A(lambda: nc.scalar.activation(hc[:, j, :n], ps_[:, :n], AF.Identity, bias=vec[:, j:j + 1]), r=[tp, t_vec], w=[t_hc])
        V(lambda: nc.vector.tensor_copy(hb[:, :, :n], hc[:, :, :n]), r=[t_hc], w=[t_hb])
        A(lambda: nc.scalar.activation(sq[:, :, :n], hc[:, :, :n], AF.Square), r=[t_hc], w=[t_sq])
        p1, tp1 = g.psum(); p2, tp2 = g.psum()
        for j in range(4):
            P(lambda: nc.tensor.matmul(p1[:, :n], g.ones_b, hb[:, j, :n], start=(j == 0), stop=(j == 3)),
              r=[t_hb, g.t_c], w=[tp1], acc=(j > 0), inc=(j == 3))
        for j in range(4):
            P(lambda: nc.tensor.matmul(p2[:, :n], g.ones_b, sq[:, j, :n], start=(j == 0), stop=(j == 3)),
              r=[t_sq, g.t_c], w=[tp2], acc=(j > 0), inc=(j == 3))
        mu, t_mu = st_p.get(); m2, t_m2 = st_p.get(); rs, t_rs = st_p.get()
        V(lambda: nc.vector.tensor_scalar(mu[:, :n], p1[:, :n], 1.0 / 512.0, None, ALU.mult), r=[tp1], w=[t_mu])
        G(lambda: nc.gpsimd.tensor_tensor(m2[:, :n], mu[:, :n], mu[:, :n], ALU.mult), r=[t_mu], w=[t_m2])
        V(lambda: nc.vector.scalar_tensor_tensor(rs[:, :n], p2[:, :n], 1.0 / 512.0, m2[:, :n], ALU.mult, ALU.subtract),
          r=[tp2, t_m2], w=[t_rs])
        V(lambda: nc.vector.tensor_scalar(rs[:, :n], rs[:, :n], EPS, None, ALU.add), r=[t_rs], w=[t_rs])
        A(lambda: nc.scalar.activation(rs[:, :n], rs[:, :n], AF.Sqrt), r=[t_rs], w=[t_rs])
        V(lambda: nc.vector.reciprocal(rs[:, :n], rs[:, :n]), r=[t_rs], w=[t_rs])
        for j in range(4):
            tm, t_tm = st_p.get()
            G(lambda: nc.gpsimd.tensor_tensor(tm[:, :n], hc[:, j, :n], mu[:, :n], ALU.subtract), r=[t_hc, t_mu], w=[t_tm])
            V(lambda: nc.vector.tensor_tensor(tm[:, :n], tm[:, :n], rs[:, :n], ALU.mult), r=[t_tm, t_rs], w=[t_tm])
            ob, t_ob = ob_p.get()
            A(lambda: nc.scalar.activation(ob[:, :n], tm[:, :n], AF.Silu, scale=vec[:, 4 + j:5 + j], bias=vec[:, 8 + j:9 + j]),
              r=[t_tm, t_vec], w=[t_ob])
            store(g, "br0T", S["br0T"][0][j * 128:(j + 1) * 128, t0:t0 + n], ob[:, :n], t_ob)


def phase_E(g, li, last=False):
    kb, nc, V, A, P, G = g.kb, g.nc, g.V, g.A, g.P, g.G
    S, I, L, T = g.S, g.I, g.L, g.T
    NKT = T // 128
    LOOK = 3
    KT = RPool(kb, "e_kt", [96, T], BF16, 2)
    VAp = RPool(kb, "e_va", [128, NKT, 65], BF16, 2)
    for (va, t_va) in VAp.tiles:
        V(lambda: nc.vector.memset(va[:, :, 64:65], 1.0), w=[t_va])
    QT = RPool(kb, "e_qt", [96, 512], BF16, 3)
    PT = RPool(kb, "e_pt", [128, 512], BF16, 6)
    OT = RPool(kb, "e_ot", [65, 512], F32, 2)
    RD = RPool(kb, "e_rd", [64, 512], F32, 2)
    OB = RPool(kb, "e_ob", [64, 512], BF16, 2)
    g.ps_skip.update((0, 1))
    pos = [g.psb[0], g.psb[1]]
    qi = [0]
    for h in range(8):
        kt, t_kt = KT.get()
        kb.dma(kt[0:64, :], S["kmT"][0][h * 64:(h + 1) * 64, :], reads=[S["kmT"][1]], writes=[t_kt])
        kb.dma(kt[64:96, :], S["kropeT"][0], reads=[S["kropeT"][1]], writes=[t_kt])
        va, t_va = VAp.get()
        kb.dma(va[:, :, 0:64], S["vm"][0][:, h * 64:(h + 1) * 64].rearrange("(k p) d -> p k d", p=128),
               reads=[S["vm"][1]], writes=[t_va])
        its = []
        for (t0, n, is_ctx, pos0) in g.tiles:
            if is_ctx and last:
                continue
            nkt = NCTX // 128 if is_ctx else NKT
            qd = {"t0": t0, "n": n, "nkt": nkt, "qt": None}
            for ki in range(nkt):
                its.append((qd, ki))

        def issue_qk(qd, ki):
            n = qd["n"]
            if qd["qt"] is None:
                qt, t_qt = QT.get()
                kb.dma(qt[:, :n], S["qmT"][0][h, :, qd["t0"]:qd["t0"] + n], reads=[S["qmT"][1]], writes=[t_qt])
                qd["qt"] = (qt, t_qt)
                qi[0] += 1
                qd["po"] = pos[qi[0] % 2]
            qt, t_qt = qd["qt"]
            ps_, tps = g.psum()
            P(lambda: nc.tensor.matmul(ps_[:, :n], kt[:96, ki * 128:(ki + 1) * 128], qt[:96, :n], start=True, stop=True),
              r=[t_kt, t_qt], w=[tps])
            return ps_, tps

        def finish(qd, ki, ps_, tps):
            n, nkt = qd["n"], qd["nkt"]
            po, tpo = qd["po"]
            pt, t_pt = PT.get()
            A(lambda: nc.scalar.activation(pt[:, :n], ps_[:, :n], AF.Exp), r=[tps], w=[t_pt])
            P(lambda: nc.tensor.matmul(po[:65, :n], va[:, ki, 0:65], pt[:, :n], start=(ki == 0), stop=(ki == nkt - 1)),
              r=[t_va, t_pt], w=[tpo], acc=(ki > 0), inc=(ki == nkt - 1))
            if ki < nkt - 1:
                return
            ot, t_ot = OT.get()
            V(lambda: nc.vector.tensor_copy(ot[:65, :n], po[:65, :n]), r=[tpo], w=[t_ot])
            pd, tpd = g.psum()
            P(lambda: nc.tensor.matmul(pd[:64, :n], g.ones_f[64:65, 0:64], ot[64:65, :n], start=True, stop=True),
              r=[t_ot, g.t_c], w=[tpd])
            rd, t_rd = RD.get()
            V(lambda: nc.vector.reciprocal(rd[:64, :n], pd[:64, :n]), r=[tpd], w=[t_rd])
            ob, t_ob = OB.get()
            V(lambda: nc.vector.tensor_tensor(ob[:64, :n], ot[:64, :n], rd[:64, :n], ALU.mult), r=[t_ot, t_rd], w=[t_ob])
            store(g, "br3T", S["br3T"][0][h * 64:(h + 1) * 64, qd["t0"]:qd["t0"] + n], ob[:64, :n], t_ob)

        queue = []
        for (qd, ki) in its:
            ps_, tps = issue_qk(qd, ki)
            queue.append((qd, ki, ps_, tps))
            if len(queue) > LOOK:
                finish(*queue.pop(0))
        while queue:
            finish(*queue.pop(0))
    g.ps_skip.difference_update((0, 1))


def phase_CD_stub(g, li, last=False):
    kb, nc, V = g.kb, g.nc, g.V
    z = kb.sb("cd_z", [128, 2048], BF16); t_z = kb.trk()
    V(lambda: nc.vector.memset(z, 0.0), w=[t_z])
    for nm in ("br2T",):
        for j in range(4):
            for c0 in range(0, g.T, 2048):
                w = min(2048, g.T - c0)
                store(g, nm, g.S[nm][0][j * 128:(j + 1) * 128, c0:c0 + w], z[:, :w], t_z)


def phase_F(g, li, last=False):
    kb, nc, V, A, P, G = g.kb, g.nc, g.V, g.A, g.P, g.G
    S, I, L, T = g.S, g.I, g.L, g.T
    kb_ = kb
    c = Ctx()
    c.xt = RPool(kb, "f_xt", [128, 4, D], F32, 1)
    c.xn = RPool(kb, "f_xn", [128, 4, D], BF16, 1)
    c.hT = RPool(kb, "f_hT", [128, 8, 512], BF16, 1)
    c.ss = RPool(kb, "f_ss", [128, 8], F32, 2)
    c.junk = RPool(kb, "f_junk", [128, D], BF16, 1)
    c.wt = RPool(kb, "f_wt", [128, 8, 512], BF16, 4)
    c.stb = RPool(kb, "f_stb", [128, 512], BF16, 2)
    c.stf = RPool(kb, "f_stf", [128, 512], F32, 8)
    mg_p = RPool(kb, "f_mg", [128, 8, 512], F32, 1)
    mgb_p = RPool(kb, "f_mgb", [128, 8, 512], BF16, 1)
    act_p = RPool(kb, "f_act", [128, 22, 512], BF16, 1)
    wf2_p = RPool(kb, "f_wf2", [128, 22, 512], BF16, 1)
    bt_p = RPool(kb, "f_bt", [128, 4, 512], BF16, 2)
    wpf_begin(g)
    for (t0, n, is_ctx, pos0) in g.tiles:
        if is_ctx and last:
            continue
        wpf_tile(g, is_ctx)
        j = 1 if is_ctx else 0
        nb = n // 128
        src = (I["ctx"] if is_ctx else I["x"][pos0:pos0 + n]) if li == 0 else S["xs"][0][t0:t0 + n]
        t_src = kb.trk() if li == 0 else S["xs"][1]
        hT, t_h, xt, t_xt, _, _ = norm_tile(g, c, src, t_src, n, g.G1c, g.modc[:, 0], j)
        mg, t_mg = mg_p.get()
        for i in range(4):
            bt, t_bt = bt_p.get()
            kb.dma(bt[:, :, :n], S["br%dT" % i][0].rearrange("(k p) t -> p k t", p=128)[:, :, t0:t0 + n],
                   reads=[S["br%dT" % i][1]], writes=[t_bt])
            gen_gl = fm_chunks(g, c, "Wgl", 8, i * 1024, 1024, hT, t_h, n)
            gen_pr = fm_chunks(g, c, "Wb", 4, 0, 1024, bt, t_bt, n, kc0=i * 4)
            for n8 in range(8):
                _, _, pg, tpg = next(gen_gl)
                _, _, pp, tpp = next(gen_pr)
                sg, t_sg = c.stf.get()
                A(lambda: nc.scalar.activation(sg[:, :n], pg[:, :n], AF.Sigmoid), r=[tpg], w=[t_sg])
                if i == 0:
                    V(lambda: nc.vector.tensor_tensor(mg[:, n8, :n], pp[:, :n], sg[:, :n], ALU.mult), r=[tpp, t_sg], w=[t_mg])
                else:
                    V(lambda: nc.vector.tensor_tensor(sg[:, :n], pp[:, :n], sg[:, :n], ALU.mult), r=[tpp, t_sg], w=[t_sg])
                    G(lambda: nc.gpsimd.tensor_tensor(mg[:, n8, :n], mg[:, n8, :n], sg[:, :n], ALU.add), r=[t_sg, t_mg], w=[t_mg])
            for _ in gen_gl:
                pass
            for _ in gen_pr:
                pass
        mgb, t_mgb = mgb_p.get()
        A(lambda: nc.scalar.copy(mgb[:, :, :n], mg[:, :, :n]), r=[t_mg], w=[t_mgb])
        for (off, w, b, ps_, tp) in tm_chunks(g, c, "Wout", 8, 0, D, mgb, t_mgb, n):
            tm, t_tm = c.stf.get()
            V(lambda: nc.vector.tensor_tensor(tm, ps_, g.modr[:, j, 0, off:off + 512], ALU.mult), r=[tp, g.t_modr], w=[t_tm])
            G(lambda: nc.gpsimd.tensor_tensor(xt[:, b, off:off + 512], xt[:, b, off:off + 512], tm, ALU.add), r=[t_tm, t_xt], w=[t_xt])
        h2, t_h2, _, _, _, _ = norm_tile(g, c, None, None, n, g.G2c, g.modc[:, 3], j, xt_pair=(xt, t_xt))
        act, t_act = act_p.get()
        gen_a = fm_chunks(g, c, "Wf1", 8, 0, FFN, h2, t_h2, n)
        gen_g = fm_chunks(g, c, "Wf1", 8, FFN, FFN, h2, t_h2, n)
        for kc in range(22):
            _, _, pa, tpa = next(gen_a)
            _, _, pg, tpg = next(gen_g)
            sg, t_sg = c.stf.get()
            A(lambda: nc.scalar.activation(sg[:, :n], pg[:, :n], AF.Silu), r=[tpg], w=[t_sg])
            V(lambda: nc.vector.tensor_tensor(act[:, kc, :n], pa[:, :n], sg[:, :n], ALU.mult), r=[tpa, t_sg], w=[t_act])
        for _ in gen_a:
            pass
        for _ in gen_g:
            pass
        for (off, w, b, ps_, tp) in tm_chunks(g, c, "Wf2", 22, 0, D, act, t_act, n, pool=wf2_p):
            tm, t_tm = c.stf.get()
            V(lambda: nc.vector.tensor_tensor(tm, ps_, g.modr[:, j, 1, off:off + 512], ALU.mult), r=[tp, g.t_modr], w=[t_tm])
            G(lambda: nc.gpsimd.tensor_tensor(xt[:, b, off:off + 512], xt[:, b, off:off + 512], tm, ALU.add), r=[t_tm, t_xt], w=[t_xt])
        if not last:
            kb.dma(S["xs"][0][t0:t0 + n].rearrange("(b p) d -> p b d", p=128), xt[:, :nb, :], reads=[t_xt], writes=[S["xs"][1]])
        else:
            ss, t_ss = c.ss.get(); junk, t_junk = c.junk.get()
            for b in range(nb):
                A(lambda: nc.scalar.activation(junk, xt[:, b, :], AF.Square, accum_out=ss[:, b:b + 1]), r=[t_xt], w=[t_junk, t_ss])
            V(lambda: nc.vector.tensor_scalar(ss[:, 0:nb], ss[:, 0:nb], 1.0 / D, EPS, ALU.mult, ALU.add), r=[t_ss], w=[t_ss])
            A(lambda: nc.scalar.activation(ss[:, 0:nb], ss[:, 0:nb], AF.Sqrt), r=[t_ss], w=[t_ss])
            V(lambda: nc.vector.reciprocal(ss[:, 0:nb], ss[:, 0:nb]), r=[t_ss], w=[t_ss])
            for b in range(nb):
                V(lambda: nc.vector.scalar_tensor_tensor(xt[:, b, :], xt[:, b, :], ss[:, b:b + 1], g.fin_g, ALU.mult, ALU.mult),
                  r=[t_xt, t_ss, g.t_c], w=[t_xt])
            kb.dma(g.out_d[pos0:pos0 + n].rearrange("(b p) d -> p b d", p=128), xt[:, :nb, :], reads=[t_xt], writes=[g.t_out])


def tm_store(g, c_ps, o_sb, t_o, nb, name, dram3, pool):
    kb, nc, P, A = g.kb, g.nc, g.P, g.A
    ps_, tp = g.psum(); psv = ps_.bitcast(BF16)
    for b in range(nb):
        P(lambda: nc.tensor.transpose(psv[:, b * 128:(b + 1) * 128], o_sb[:, b * 128:(b + 1) * 128], g.ident_b),
          r=[t_o, g.t_c], w=[tp], acc=(b > 0), inc=(b == nb - 1))
    kt, t_kt = pool.get()
    A(lambda: nc.scalar.copy(kt[:, :nb, :], psv[:, :nb * 128].rearrange("p (b f) -> p b f", f=128)), r=[tp], w=[t_kt])
    store(g, name, dram3, kt[:, :nb, :], t_kt)


def phase_C(g, li, last=False):
    kb, nc, V, A, P, G = g.kb, g.nc, g.V, g.A, g.P, g.G
    S, I, L, T = g.S, g.I, g.L, g.T
    cw = kb.sb("c_cw", [128, 8, 5], F32); t_cw = kb.trk()
    for tap in range(5):
        for j in range(8):
            kb.dma(cw[:, j, tap:tap + 1], I["ssm_conv_w"][li, tap, j * 128:(j + 1) * 128].rearrange("(p o) -> p o", o=1), writes=[t_cw])
    scb = kb.sb("c_scb", [128, 8], F32); t_scb = kb.trk()
    g.coldma(scb, I["ssm_conv_b"][li], 8, t_scb)
    Dg = kb.sb("c_Dg", [128, 8, 5, 128], BF16); t_Dg = kb.trk()
    for j in range(8):
        for tap in range(5):
            V(lambda: nc.vector.tensor_scalar(Dg[:, j, tap, :], g.ident_b, cw[:, j, tap:tap + 1], None, ALU.mult), r=[t_cw, g.t_c], w=[t_Dg])
    ut_p = RPool(kb, "c_ut", [128, 8, 516], BF16, 2)
    ob_p = RPool(kb, "c_ob", [128, 512], BF16, 12)
    tk_p = RPool(kb, "c_tk", [128, 4, 128], BF16, 12)
    x3 = S["xbcT"][0].rearrange("(j p) t -> p j t", p=128)
    for (t0, n, is_ctx, pos0) in g.tiles:
        nb = n // 128
        s_lo, s_hi = (0, NCTX) if is_ctx else (NCTX, T)
        lo = max(s_lo, t0 - 2); hi = min(s_hi, t0 + n + 2)
        ut, t_ut = ut_p.get()
        if lo > t0 - 2 or hi < t0 + n + 2:
            V(lambda: nc.vector.memset(ut, 0.0), w=[t_ut])
        kb.dma(ut[:, :, lo - (t0 - 2):hi - (t0 - 2)], x3[:, :, lo:hi], reads=[S["xbcT"][1]], writes=[t_ut])
        for j in range(8):
            ps_, tp = g.psum()
            for tap in range(5):
                P(lambda: nc.tensor.matmul(ps_[:, :n], Dg[:, j, tap, :], ut[:, j, tap:tap + n], start=(tap == 0), stop=(tap == 4)),
                  r=[t_Dg, t_ut], w=[tp], acc=(tap > 0), inc=(tap == 4))
            o, t_o = ob_p.get()
            A(lambda: nc.scalar.activation(o[:, :n], ps_[:, :n], AF.Silu, bias=scb[:, j:j + 1]), r=[tp, t_scb], w=[t_o])
            if j < 4:
                tm_store(g, None, o, t_o, nb, "ssx", S["ssx"][0][t0:t0 + n, j * 128:(j + 1) * 128].rearrange("(b p) f -> p b f", p=128), tk_p)
            elif j < 6:
                store(g, "ssBT", S["ssBT"][0][(j - 4) * 128:(j - 3) * 128, t0:t0 + n], o[:, :n], t_o)
                tm_store(g, None, o, t_o, nb, "ssB", S["ssB"][0][t0:t0 + n, (j - 4) * 128:(j - 3) * 128].rearrange("(b p) f -> p b f", p=128), tk_p)
            else:
                store(g, "ssCT", S["ssCT"][0][(j - 6) * 128:(j - 5) * 128, t0:t0 + n], o[:, :n], t_o)
    if ASTOP == 60:
        return
    NC_ = T // 128
    nctx = NCTX // 128
    Dr = kb.sb("c_Dr", [128, 8], F32); ngr = kb.sb("c_ngr", [128, 512], F32); t_cst = kb.trk()
    kb.dma(Dr, I["ssm_d"][li].partition_broadcast(128), writes=[t_cst])
    kb.dma(ngr, I["ssm_norm_g"][li].partition_broadcast(128), writes=[t_cst])
    hst = kb.sb("c_h", [128, 512], F32); hb = kb.sb("c_hb", [128, 512], BF16); t_h = kb.trk(); t_hb = kb.trk()
    sm_p = RPool(kb, "c_sm", [128, 16], F32, 4)
    s8_p = RPool(kb, "c_s8", [128, 8], F32, 12)
    x_p = RPool(kb, "c_x", [128, 512], BF16, 3)
    b_p = RPool(kb, "c_b", [128, 256], BF16, 3)
    bt_p = RPool(kb, "c_bt", [128, 2, 128], BF16, 3)
    ct_p = RPool(kb, "c_ct", [128, 2, 128], BF16, 3)
    xd_p = RPool(kb, "c_xd", [128, 512], BF16, 4)
    gm_p = RPool(kb, "c_gm", [128, 2, 128], F32, 2)
    R_p = RPool(kb, "c_R", [128, 8, 128], F32, 2)
    E_p = RPool(kb, "c_E", [128, 8, 128], F32, 2)
    pm_p = RPool(kb, "c_pm", [128, 8, 128], BF16, 2)
    f5_p = RPool(kb, "c_f5", [128, 512], F32, 12)
    z_p = RPool(kb, "c_z", [128, 512], BF16, 2)
    o5_p = RPool(kb, "c_o5", [128, 512], BF16, 2)
    jk_p = RPool(kb, "c_jk", [128, 256], BF16, 1)
    BT3 = S["ssBT"][0].rearrange("(g p) t -> p g t", p=128)
    CT3 = S["ssCT"][0].rearrange("(g p) t -> p g t", p=128)
    br3 = S["br1T"][0].rearrange("(j p) t -> p j t", p=128)

    def b8(ap, w):
        return ap.unsqueeze(2).broadcast_to([128, 8, w])

    for d in range(2):
        Ud = g.Uf if d == 0 else g.Lf
        order = list(range(NC_)) if d == 0 else (list(range(nctx - 1, -1, -1)) + list(range(NC_ - 1, nctx - 1, -1)))
        V(lambda: nc.vector.memset(hst, 0.0), w=[t_h])
        V(lambda: nc.vector.memset(hb, 0.0), w=[t_hb])
        for ck in order:
            tk = ck * 128
            is_ctx = ck < nctx
            sm, t_sm = sm_p.get()
            kb.dma(sm[:, 0:8], S["la"][0][tk:tk + 128, d * 8:(d + 1) * 8], reads=[S["la"][1]], writes=[t_sm])
            kb.dma(sm[:, 8:16], S["dt"][0][tk:tk + 128, d * 8:(d + 1) * 8], reads=[S["dt"][1]], writes=[t_sm])
            la8, dt8 = sm[:, 0:8], sm[:, 8:16]
            x, t_x = x_p.get(); kb.dma(x, S["ssx"][0][tk:tk + 128, :], reads=[S["ssx"][1]], writes=[t_x])
            Bt, t_B = b_p.get(); kb.dma(Bt, S["ssB"][0][tk:tk + 128, :], reads=[S["ssB"][1]], writes=[t_B])
            BT, t_BT = bt_p.get(); kb.dma(BT, BT3[:, :, tk:tk + 128], reads=[S["ssBT"][1]], writes=[t_BT])
            CT, t_CT = ct_p.get(); kb.dma(CT, CT3[:, :, tk:tk + 128], reads=[S["ssCT"][1]], writes=[t_CT])
            pc, tpc = g.psum()
            P(lambda: nc.tensor.matmul(pc[:, 0:8], Ud, la8, start=True, stop=True), r=[g.t_c, t_sm], w=[tpc])
            P(lambda: nc.tensor.matmul(pc[:, 8:16], g.ones_f, la8, start=True, stop=True), r=[g.t_c, t_sm], w=[tpc], acc=True)
            cum, t_cum = s8_p.get(); tot, t_tot = s8_p.get()
            V(lambda: nc.vector.tensor_copy(cum, pc[:, 0:8]), r=[tpc], w=[t_cum])
            V(lambda: nc.vector.tensor_copy(tot, pc[:, 8:16]), r=[tpc], w=[t_tot])
            ecum, t_ec = s8_p.get(); dst, t_ds = s8_p.get(); etot, t_et = s8_p.get(); dtd, t_dd = s8_p.get()
            A(lambda: nc.scalar.activation(ecum, cum, AF.Exp), r=[t_cum], w=[t_ec])
            V(lambda: nc.vector.tensor_tensor(dst, tot, cum, ALU.subtract), r=[t_tot, t_cum], w=[t_ds])
            A(lambda: nc.scalar.activation(dst, dst, AF.Exp), r=[t_ds], w=[t_ds])
            A(lambda: nc.scalar.activation(etot, tot, AF.Exp), r=[t_tot], w=[t_et])
            V(lambda: nc.vector.tensor_tensor(dtd, dst, dt8, ALU.mult), r=[t_ds, t_sm], w=[t_dd])
            xdt, t_xdt = xd_p.get(); xd, t_xd = xd_p.get()
            x3_ = x.rearrange("p (h e) -> p h e", e=64)
            V(lambda: nc.vector.tensor_tensor(xdt.rearrange("p (h e) -> p h e", e=64), x3_, b8(dt8, 64), ALU.mult), r=[t_x, t_sm], w=[t_xdt])
            V(lambda: nc.vector.tensor_tensor(xd.rearrange("p (h e) -> p h e", e=64), x3_, b8(dtd, 64), ALU.mult), r=[t_x, t_dd], w=[t_xd])
            pg, tpg = g.psum()
            for gi in range(2):
                P(lambda: nc.tensor.matmul(pg[:, gi * 128:(gi + 1) * 128], BT[:, gi, :], CT[:, gi, :], start=True, stop=True),
                  r=[t_BT, t_CT], w=[tpg], acc=(gi > 0))
            gm, t_gm = gm_p.get()
            V(lambda: nc.vector.tensor_tensor(gm, pg[:, 0:256].rearrange("p (g l) -> p g l", l=128),
                                              Ud.unsqueeze(1).broadcast_to([128, 2, 128]), ALU.mult), r=[tpg, g.t_c], w=[t_gm])
            Rt, t_R = R_p.get()
            V(lambda: nc.vector.tensor_tensor(Rt, b8(la8, 128), Ud.unsqueeze(1).broadcast_to([128, 8, 128]), ALU.mult), r=[t_sm, g.t_c], w=[t_R])
            Et, t_E = E_p.get()
            for hf in range(2):
                pb, tpb = g.psum()
                P(lambda: nc.tensor.matmul(pb, g.ones_f, Rt[:, hf * 4:(hf + 1) * 4, :].rearrange("p h l -> p (h l)"), start=True, stop=True),
                  r=[g.t_c, t_R], w=[tpb])
                for hh in range(4):
                    h = hf * 4 + hh
                    V(lambda: nc.vector.tensor_scalar(Et[:, h, :], pb[:, hh * 128:(hh + 1) * 128], cum[:, h:h + 1], 0.0, ALU.subtract, ALU.min),
                      r=[tpb, t_cum], w=[t_E])
            A(lambda: nc.scalar.activation(Et, Et, AF.Exp), r=[t_E], w=[t_E])
            pm, t_pm = pm_p.get()
            for gi in range(2):
                V(lambda: nc.vector.tensor_tensor(pm[:, gi * 4:(gi + 1) * 4, :], Et[:, gi * 4:(gi + 1) * 4, :],
                                                  gm[:, gi, :].unsqueeze(1).broadcast_to([128, 4, 128]), ALU.mult), r=[t_E, t_gm], w=[t_pm])
            py, tpy = g.psum()
            for h in range(8):
                P(lambda: nc.tensor.matmul(py[:, h * 64:(h + 1) * 64], pm[:, h, :], xdt[:, h * 64:(h + 1) * 64], start=True, stop=True),
                  r=[t_pm, t_xdt], w=[tpy], acc=(h > 0), inc=(h == 7))
            pi_, tpi = g.psum()
            for gi in range(2):
                P(lambda: nc.tensor.matmul(pi_[:, gi * 256:(gi + 1) * 256], CT[:, gi, :], hb[:, gi * 256:(gi + 1) * 256], start=True, stop=True),
                  r=[t_CT, t_hb], w=[tpi], acc=(gi > 0), inc=(gi == 1))
            t1, t_t1 = f5_p.get(); y, t_y = f5_p.get()
            V(lambda: nc.vector.tensor_tensor(t1.rearrange("p (h e) -> p h e", e=64), pi_.rearrange("p (h e) -> p h e", e=64), b8(ecum, 64), ALU.mult),
              r=[tpi, t_ec], w=[t_t1])
            V(lambda: nc.vector.tensor_tensor(y, py, t1, ALU.add), r=[tpy, t_t1], w=[t_y])
            pS, tpS = g.psum()
            for gi in range(2):
                P(lambda: nc.tensor.matmul(pS[:, gi * 256:(gi + 1) * 256], Bt[:, gi * 128:(gi + 1) * 128], xd[:, gi * 256:(gi + 1) * 256], start=True, stop=True),
                  r=[t_B, t_xd], w=[tpS], acc=(gi > 0), inc=(gi == 1))
            V(lambda: nc.vector.tensor_tensor(hst.rearrange("p (h e) -> p h e", e=64), hst.rearrange("p (h e) -> p h e", e=64), b8(etot, 64), ALU.mult),
              r=[t_h, t_et], w=[t_h])
            V(lambda: nc.vector.tensor_tensor(hst, hst, pS, ALU.add), r=[t_h, tpS], w=[t_h])
            A(lambda: nc.scalar.copy(hb, hst), r=[t_h], w=[t_hb])
            if d == 0:
                kb.dma(S["yf"][0][tk:tk + 128, :], y, reads=[t_y], writes=[S["yf"][1]])
                continue
            if is_ctx and last:
                continue
            yf, t_yf = f5_p.get(); kb.dma(yf, S["yf"][0][tk:tk + 128, :], reads=[S["yf"][1]], writes=[t_yf])
            zt, t_z = z_p.get(); kb.dma(zt, S["zs"][0][tk:tk + 128, :], reads=[S["zs"][1]], writes=[t_z])
            G(lambda: nc.gpsimd.tensor_tensor(y, y, yf, ALU.add), r=[t_y, t_yf], w=[t_y])
            V(lambda: nc.vector.tensor_tensor(t1.rearrange("p (h e) -> p h e", e=64), x3_, b8(Dr, 64), ALU.mult), r=[t_x, t_cst], w=[t_t1])
            G(lambda: nc.gpsimd.tensor_tensor(y, y, t1, ALU.add), r=[t_y, t_t1], w=[t_y])
            V(lambda: nc.vector.tensor_tensor(y, y, zt, ALU.mult), r=[t_y, t_z], w=[t_y])
            ss, t_ss = s8_p.get(); jk, t_jk = jk_p.get()
            for gi in range(2):
                A(lambda: nc.scalar.activation(jk, y[:, gi * 256:(gi + 1) * 256], AF.Square, accum_out=ss[:, gi:gi + 1]), r=[t_y], w=[t_jk, t_ss])
            V(lambda: nc.vector.tensor_scalar(ss[:, 0:2], ss[:, 0:2], 1.0 / 256.0, EPS, ALU.mult, ALU.add), r=[t_ss], w=[t_ss])
            A(lambda: nc.scalar.activation(ss[:, 0:2], ss[:, 0:2], AF.Sqrt), r=[t_ss], w=[t_ss])
            V(lambda: nc.vector.reciprocal(ss[:, 0:2], ss[:, 0:2]), r=[t_ss], w=[t_ss])
            o5, t_o5 = o5_p.get()
            for gi in range(2):
                V(lambda: nc.vector.scalar_tensor_tensor(o5[:, gi * 256:(gi + 1) * 256], y[:, gi * 256:(gi + 1) * 256], ss[:, gi:gi + 1],
                                                         ngr[:, gi * 256:(gi + 1) * 256], ALU.mult, ALU.mult), r=[t_y, t_ss, t_cst], w=[t_o5])
            ps_, tp = g.psum(); psv = ps_.bitcast(BF16)
            for j in range(4):
                P(lambda: nc.tensor.transpose(psv[:, j * 128:(j + 1) * 128], o5[:, j * 128:(j + 1) * 128], g.ident_b),
                  r=[t_o5, g.t_c], w=[tp], acc=(j > 0), inc=(j == 3))
            kt, t_kt = tk_p.get()
            A(lambda: nc.scalar.copy(kt, psv[:, 0:512].rearrange("p (j t) -> p j t", t=128)), r=[tp], w=[t_kt])
            store(g, "br1T", br3[:, :, tk:tk + 128], kt, t_kt)


def phase_D(g, li, last=False):
    kb, nc, V, A, P, G = g.kb, g.nc, g.V, g.A, g.P, g.G
    S, I, L, T = g.S, g.I, g.L, g.T
    NC_ = T // 128
    nctx = NCTX // 128
    lgb = kb.sb("d_lgb", [128, 8], F32); t_k = kb.trk()
    kb.dma(lgb, I["ret_decay"][li].partition_broadcast(128), writes=[t_k])
    A(lambda: nc.scalar.activation(lgb, lgb, AF.Exp), r=[t_k], w=[t_k])
    V(lambda: nc.vector.tensor_scalar(lgb, lgb, -1.0, None, ALU.mult), r=[t_k], w=[t_k])
    gr = kb.sb("d_gr", [128, 2, 512], F32)
    kb.dma(gr[:, 0, :], I["ret_gn_g"][li].partition_broadcast(128), writes=[t_k])
    kb.dma(gr[:, 1, :], I["ret_gn_b"][li].partition_broadcast(128), writes=[t_k])
    ii = kb.sb("d_ii", [128, 132], I32); ff = kb.sb("d_ff", [128, 133], F32)
    G(lambda: nc.gpsimd.iota(ii[:, 0:128], pattern=[[1, 128]], base=0, channel_multiplier=-1), w=[t_k])
    for cidx, (base, cm) in enumerate(((1, 1), (127, -1), (128, -1), (0, 1))):
        G(lambda: nc.gpsimd.iota(ii[:, 128 + cidx:129 + cidx], pattern=[[0, 1]], base=base, channel_multiplier=cm), w=[t_k])
    V(lambda: nc.vector.tensor_copy(ff[:, 0:132], ii), r=[t_k], w=[t_k])
    A(lambda: nc.scalar.activation(ff[:, 0:128], ff[:, 0:128], AF.Abs), r=[t_k], w=[t_k])
    V(lambda: nc.vector.memset(ff[:, 132:133], 128.0), r=[t_k], w=[t_k])
    Dm = kb.sb("d_Dm", [128, 2, 4, 128], F32)
    vec = kb.sb("d_vec", [128, 2, 3, 4], F32)
    for d in range(2):
        Ud = g.Uf if d == 0 else g.Lf
        for h in range(4):
            k = d * 4 + h
            A(lambda: nc.scalar.activation(Dm[:, d, h, :], ff[:, 0:128], AF.Exp, scale=lgb[:, k:k + 1]), r=[t_k], w=[t_k])
            V(lambda: nc.vector.tensor_tensor(Dm[:, d, h, :], Dm[:, d, h, :], Ud, ALU.mult), r=[t_k, g.t_c], w=[t_k])
            c_e, c_d = (128, 129) if d == 0 else (130, 131)
            A(lambda: nc.scalar.activation(vec[:, d, 0, h:h + 1], ff[:, c_e:c_e + 1], AF.Exp, scale=lgb[:, k:k + 1]), r=[t_k], w=[t_k])
            A(lambda: nc.scalar.activation(vec[:, d, 1, h:h + 1], ff[:, c_d:c_d + 1], AF.Exp, scale=lgb[:, k:k + 1]), r=[t_k], w=[t_k])
            A(lambda: nc.scalar.activation(vec[:, d, 2, h:h + 1], ff[:, 132:133], AF.Exp, scale=lgb[:, k:k + 1]), r=[t_k], w=[t_k])
    hst = kb.sb("d_h", [64, 4, 128], F32); hb = kb.sb("d_hb", [64, 4, 128], BF16); t_h = kb.trk(); t_hb = kb.trk()
    q_p = RPool(kb, "d_q", [64, 4, 128], BF16, 3)
    k_p = RPool(kb, "d_k", [64, 4, 128], BF16, 3)
    km_p = RPool(kb, "d_km", [128, 256], BF16, 3)
    kd_p = RPool(kb, "d_kd", [128, 256], BF16, 2)
    v_p = RPool(kb, "d_v", [128, 512], BF16, 3)
    pm_p = RPool(kb, "d_pm", [128, 4, 128], BF16, 2)
    f5_p = RPool(kb, "d_f5", [128, 512], F32, 12)
    g_p = RPool(kb, "d_g", [128, 512], BF16, 2)
    o5_p = RPool(kb, "d_o5", [128, 512], BF16, 2)
    s8_p = RPool(kb, "d_s8", [128, 16], F32, 4)
    jk_p = RPool(kb, "d_jk", [128, 128], BF16, 1)
    tk_p = RPool(kb, "d_tk", [128, 4, 128], BF16, 10)
    q3 = S["qrT"][0].rearrange("(h d) t -> d h t", d=64)
    k3 = S["krT"][0].rearrange("(h d) t -> d h t", d=64)
    br3 = S["br2T"][0].rearrange("(j p) t -> p j t", p=128)

    def b4(ap, w):
        return ap.unsqueeze(2).broadcast_to([ap.shape[0], 4, w])

    for d in range(2):
        order = list(range(NC_)) if d == 0 else (list(range(nctx - 1, -1, -1)) + list(range(NC_ - 1, nctx - 1, -1)))
        V(lambda: nc.vector.memset(hst, 0.0), w=[t_h])
        V(lambda: nc.vector.memset(hb, 0.0), w=[t_hb])
        for ck in order:
            tk = ck * 128
            is_ctx = ck < nctx
            qT, t_q = q_p.get(); kb.dma(qT, q3[:, :, tk:tk + 128], reads=[S["qrT"][1]], writes=[t_q])
            kT, t_kT = k_p.get(); kb.dma(kT, k3[:, :, tk:tk + 128], reads=[S["krT"][1]], writes=[t_kT])
            km, t_km = km_p.get(); kb.dma(km, S["kr"][0][tk:tk + 128, :], reads=[S["kr"][1]], writes=[t_km])
            v, t_v = v_p.get(); kb.dma(v, S["rv"][0][tk:tk + 128, :], reads=[S["rv"][1]], writes=[t_v])
            pg, tpg = g.psum()
            for h in range(4):
                P(lambda: nc.tensor.matmul(pg[:, h * 128:(h + 1) * 128], kT[:, h, :], qT[:, h, :], start=True, stop=True),
                  r=[t_kT, t_q], w=[tpg], acc=(h > 0), inc=(h == 3))
            pm, t_pm = pm_p.get()
            V(lambda: nc.vector.tensor_tensor(pm, pg.rearrange("p (h l) -> p h l", l=128), Dm[:, d], ALU.mult), r=[tpg, t_k], w=[t_pm])
            py, tpy = g.psum()
            for h in range(4):
                P(lambda: nc.tensor.matmul(py[:, h * 128:(h + 1) * 128], pm[:, h, :], v[:, h * 128:(h + 1) * 128], start=True, stop=True),
                  r=[t_pm, t_v], w=[tpy], acc=(h > 0), inc=(h == 3))
            pi_, tpi = g.psum()
            for h in range(4):
                P(lambda: nc.tensor.matmul(pi_[:, h * 128:(h + 1) * 128], qT[:, h, :], hb[:, h, :], start=True, stop=True),
                  r=[t_q, t_hb], w=[tpi], acc=(h > 0), inc=(h == 3))
            t1, t_t1 = f5_p.get(); y, t_y = f5_p.get()
            V(lambda: nc.vector.tensor_tensor(t1.rearrange("p (h e) -> p h e", e=128), pi_.rearrange("p (h e) -> p h e", e=128),
                                              b4(vec[:, d, 0, :], 128), ALU.mult), r=[tpi, t_k], w=[t_t1])
            V(lambda: nc.vector.tensor_tensor(y, py, t1, ALU.add), r=[tpy, t_t1], w=[t_y])
            kd, t_kd = kd_p.get()
            V(lambda: nc.vector.tensor_tensor(kd.rearrange("p (h e) -> p h e", e=64), km.rearrange("p (h e) -> p h e", e=64),
                                              b4(vec[:, d, 1, :], 64), ALU.mult), r=[t_km, t_k], w=[t_kd])
            pS, tpS = g.psum()
            for h in range(4):
                P(lambda: nc.tensor.matmul(pS[:64, h * 128:(h + 1) * 128], kd[:, h * 64:(h + 1) * 64], v[:, h * 128:(h + 1) * 128], start=True, stop=True),
                  r=[t_kd, t_v], w=[tpS], acc=(h > 0), inc=(h == 3))
            V(lambda: nc.vector.tensor_tensor(hst, hst, b4(vec[:64, d, 2, :], 128), ALU.mult), r=[t_h, t_k], w=[t_h])
            V(lambda: nc.vector.tensor_tensor(hst, hst, pS[:64, :].rearrange("p (h e) -> p h e", e=128), ALU.add), r=[t_h, tpS], w=[t_h])
            A(lambda: nc.scalar.copy(hb, hst), r=[t_h], w=[t_hb])
            if d == 0:
                kb.dma(S["ryf"][0][tk:tk + 128, :], y, reads=[t_y], writes=[S["ryf"][1]])
                continue
            if is_ctx and last:
                continue
            yf, t_yf = f5_p.get(); kb.dma(yf, S["ryf"][0][tk:tk + 128, :], reads=[S["ryf"][1]], writes=[t_yf])
            gt, t_g = g_p.get(); kb.dma(gt, S["rg"][0][tk:tk + 128, :], reads=[S["rg"][1]], writes=[t_g])
            G(lambda: nc.gpsimd.tensor_tensor(y, y, yf, ALU.add), r=[t_y, t_yf], w=[t_y])
            st, t_st = s8_p.get(); jk, t_jk = jk_p.get()
            for h in range(4):
                A(lambda: nc.scalar.activation(jk, y[:, h * 128:(h + 1) * 128], AF.Identity, accum_out=st[:, h:h + 1]), r=[t_y], w=[t_jk, t_st])
                A(lambda: nc.scalar.activation(jk, y[:, h * 128:(h + 1) * 128], AF.Square, accum_out=st[:, 4 + h:5 + h]), r=[t_y], w=[t_jk, t_st])
            V(lambda: nc.vector.tensor_scalar(st[:, 0:8], st[:, 0:8], 1.0 / 128.0, None, ALU.mult), r=[t_st], w=[t_st])
            V(lambda: nc.vector.tensor_tensor(st[:, 8:12], st[:, 0:4], st[:, 0:4], ALU.mult), r=[t_st], w=[t_st])
            V(lambda: nc.vector.tensor_tensor(st[:, 12:16], st[:, 4:8], st[:, 8:12], ALU.subtract), r=[t_st], w=[t_st])
            V(lambda: nc.vector.tensor_scalar(st[:, 12:16], st[:, 12:16], EPS, None, ALU.add), r=[t_st], w=[t_st])
            A(lambda: nc.scalar.activation(st[:, 12:16], st[:, 12:16], AF.Sqrt), r=[t_st], w=[t_st])
            V(lambda: nc.vector.reciprocal(st[:, 12:16], st[:, 12:16]), r=[t_st], w=[t_st])
            for h in range(4):
                V(lambda: nc.vector.tensor_scalar(y[:, h * 128:(h + 1) * 128], y[:, h * 128:(h + 1) * 128], st[:, h:h + 1], st[:, 12 + h:13 + h],
                                                  ALU.subtract, ALU.mult), r=[t_y, t_st], w=[t_y])
            V(lambda: nc.vector.tensor_tensor(y, y, gr[:, 0, :], ALU.mult), r=[t_y, t_k], w=[t_y])
            G(lambda: nc.gpsimd.tensor_tensor(y, y, gr[:, 1, :], ALU.add), r=[t_y, t_k], w=[t_y])
            o5, t_o5 = o5_p.get()
            V(lambda: nc.vector.tensor_tensor(o5, y, gt, ALU.mult), r=[t_y, t_g], w=[t_o5])
            ps_, tp = g.psum(); psv = ps_.bitcast(BF16)
            for j in range(4):
                P(lambda: nc.tensor.transpose(psv[:, j * 128:(j + 1) * 128], o5[:, j * 128:(j + 1) * 128], g.ident_b),
                  r=[t_o5, g.t_c], w=[tp], acc=(j > 0), inc=(j == 3))
            kt, t_kt = tk_p.get()
            A(lambda: nc.scalar.copy(kt, psv[:, 0:512].rearrange("p (j t) -> p j t", t=128)), r=[tp], w=[t_kt])
            store(g, "br2T", br3[:, :, tk:tk + 128], kt, t_kt)


_CACHE = {}


def kernel(**inputs):
    x = np.asarray(inputs["x"], dtype=np.float32)
    B, L, _ = x.shape
    if L not in _CACHE:
        _CACHE[L] = build(L, n_layers=2, debug=False)[0]
    nc = _CACHE[L]
    tabs = rope_tables(L)

    def pack(t):
        return np.ascontiguousarray(np.concatenate([t[0], t[1], t[2], t[3]], axis=1), dtype=np.float32)

    def core_in(b):
        d = {"x": np.ascontiguousarray(x[b]), "ctx": np.ascontiguousarray(inputs["ctx"][b], dtype=np.float32),
             "c2": np.ascontiguousarray(np.stack([inputs["c"][b], inputs["c_ctx"]]), dtype=np.float32),
             "t_ret": pack(tabs["ret"]), "t_mla": pack(tabs["mla"]), "t_mlk": pack(tabs["mlk"])}
        for k, v in inputs.items():
            if k in ("x", "ctx", "c", "c_ctx"):
                continue
            v = np.asarray(v, dtype=np.float32)
            if k in ("ssm_dt_bias", "ssm_a_log", "ret_decay"):
                v = v.reshape(2, -1)
            if k == "final_norm_g":
                v = v.reshape(1, -1)
            d[k] = np.ascontiguousarray(v)
        return d

    res = run_bass_kernel_spmd(nc, [core_in(b) for b in range(B)], core_ids=list(range(B)))
    return np.stack([np.asarray(res.results[b]["out"], dtype=np.float32) for b in range(B)], axis=0)
```

```python
import math
import numpy as np
import concourse.bass as bass
import concourse.mybir as mybir
from concourse.bass_utils import run_bass_kernel_spmd

F32 = mybir.dt.float32
BF16 = mybir.dt.bfloat16
I32 = mybir.dt.int32
AF = mybir.ActivationFunctionType
ALU = mybir.AluOpType
AX = mybir.AxisListType


class Trk:
    __slots__ = ("w", "r", "wpe", "name", "excl", "pend")

    def __init__(self, name=""):
        self.w = None
        self.r = {}
        self.wpe = False
        self.name = name
        self.pend = 0
        self.excl = False


class Eng:
    def __init__(self, kb, name, h):
        self.kb = kb
        self.name = name
        self.h = h
        self.sem = kb.nc.alloc_semaphore("sem_" + name)
        self.key = "E_" + name
        kb.sems[self.key] = self.sem
        self.count = 0
        self.known = {}


class KB:
    def __init__(self, nc, n_dma_sems=40):
        self.nc = nc
        self.sems = {}
        self.pe = Eng(self, "pe", nc.tensor)
        self.act = Eng(self, "act", nc.scalar)
        self.dve = Eng(self, "dve", nc.vector)
        self.pool = Eng(self, "pool", nc.gpsimd)
        self.sp = Eng(self, "sp", nc.sync)
        self.engs = [self.pe, self.act, self.dve, self.pool, self.sp]
        self.dsems = []
        for i in range(n_dma_sems):
            s = nc.alloc_semaphore("dsem%d" % i)
            k = "D%d" % i
            self.sems[k] = s
            self.dsems.append([k, s, 0])
        self.dnext = 0
        self.psems = []
        for i in range(24):
            s_ = nc.alloc_semaphore("psem%d" % i)
            k = "Q%d" % i
            self.sems[k] = s_
            self.psems.append([k, s_, 0])
        self.pnext = 0
        self.n_inst = 0
        self.uid = 0
        self.deferred = []

    def sb(self, name, shape, dt=F32):
        self.uid += 1
        name = "%s_u%d" % (name, self.uid)
        if getattr(self, "stack", None) is not None:
            return self.stack.enter_context(self.nc.sbuf_tensor(name, list(shape), dt)).ap()
        return self.nc.alloc_sbuf_tensor(name, list(shape), dt).ap()

    def ps(self, name, shape, dt=F32):
        return self.nc.alloc_psum_tensor(name, list(shape), dt).ap()

    def dram(self, name, shape, dt=F32, kind="Internal"):
        return self.nc.dram_tensor(name, list(shape), dt, kind=kind).ap()

    def trk(self, name=""):
        return Trk(name)

    def _wait(self, eng, evs):
        need = {}
        for k, v in evs:
            if need.get(k, 0) < v:
                need[k] = v
        for k, v in need.items():
            if eng.known.get(k, 0) < v:
                eng.h.wait_ge(self.sems[k], v)
                eng.known[k] = v

    def _deps(self, eng, reads, writes, acc):
        if self.deferred:
            for t in reads:
                if t.pend:
                    self.flush_deferred()
                    break
            else:
                for t in writes:
                    if t.pend:
                        self.flush_deferred()
                        break
        evs = []
        for t in reads:
            if t.w is not None:
                evs.append(t.w)
            if t.excl:
                evs.extend((k, v) for k, v in t.r.items() if k != eng.key)
        for t in writes:
            if t.w is not None and not (acc and t.wpe and eng is self.pe):
                evs.append(t.w)
            evs.extend(t.r.items())
        self._wait(eng, evs)

    def _post(self, ev, reads, writes, is_pe):
        k, v = ev
        for t in reads:
            if t.r.get(k, 0) < v:
                t.r[k] = v
        for t in writes:
            t.w = ev
            t.r = {}
            t.wpe = is_pe

    def op(self, eng, fn, reads=(), writes=(), inc=True, acc=False):
        self._deps(eng, reads, writes, acc)
        inst = fn()
        self.n_inst += 1
        if inc:
            eng.count += 1
            inst.then_inc(eng.sem, 1)
            ev = (eng.key, eng.count)
        else:
            ev = (eng.key, eng.count + 1)
        self._post(ev, reads, writes, eng is self.pe)
        return inst

    def dma(self, out, in_, reads=(), writes=(), q=None, **kw):
        q = q or self.sp
        if q is self.pool:
            d = self.psems[self.pnext]
            self.pnext = (self.pnext + 1) % len(self.psems)
        else:
            d = self.dsems[self.dnext]
            self.dnext = (self.dnext + 1) % len(self.dsems)
        self._deps(q, reads, writes, False)
        if d[2] > 0:
            self._wait(q, [(d[0], d[2])])
        inst = q.h.dma_start(out=out, in_=in_, **kw)
        d[2] += 16
        inst.then_inc(d[1], 16)
        self.n_inst += 1
        ev = (d[0], d[2])
        self._post(ev, reads, writes, False)
        return inst

    def dma_deferred(self, out, in_, reads=(), writes=(), defer=8):
        for t in list(reads) + list(writes):
            t.pend += 1
        self.deferred.append((out, in_, list(reads), list(writes)))
        while len(self.deferred) > defer:
            self._emit_deferred()

    def _emit_deferred(self):
        out, in_, reads, writes = self.deferred.pop(0)
        for t in reads + writes:
            t.pend -= 1
        self.dma(out, in_, reads=reads, writes=writes)

    def flush_deferred(self):
        while self.deferred:
            self._emit_deferred()

    def finish(self, trks):
        self.flush_deferred()
        evs = []
        for t in trks:
            if t.w is not None:
                evs.append(t.w)
        self._wait(self.sp, evs)


def _kb_barrier(self):
    self.flush_deferred()
    evs = [(e.key, e.count) for e in self.engs if e.count > 0]
    evs += [(d[0], d[2]) for d in self.dsems + self.psems if d[2] > 0]
    for e in self.engs:
        self._wait(e, evs)


KB.barrier = _kb_barrier


D = 1024
NCTX = 256
EPS = 1e-6
FFN = 2816
C_CONV, C_Z, C_XBC, C_DT, C_RQ, C_RK, C_RV, C_RG, C_CQ, C_CKV, C_KR, C_GL = (
    0, 1024, 1536, 2560, 2576, 2832, 3088, 3600, 4112, 4496, 4752, 4784)
N_IN = 8880
E_RQS, E_RKS, E_KRS, N_EXT = 4784, 5040, 5296, 5328


def rope_tables(L):
    def tab(nf, nrows):
        inv = 10000.0 ** (-np.arange(nf, dtype=np.float32) / nf)
        rows = np.arange(nrows, dtype=np.float32)[:, None] * inv
        cols = np.arange(64, dtype=np.float32)[:, None] * inv
        return rows.astype(np.float32), cols.astype(np.float32)
    nrows = max(L // 64, 1)
    out = {}
    rr, cc = tab(16, nrows)
    TRc = np.ones((128, nrows), np.float32); TRs = np.zeros((128, nrows), np.float32)
    TCc = np.ones((128, 64), np.float32); TCs = np.zeros((128, 64), np.float32)
    for p in range(128):
        d = p % 64
        f = d % 32
        sgn = -1.0 if d < 32 else 1.0
        if f < 16:
            TRc[p] = np.cos(rr[:, f]); TRs[p] = sgn * np.sin(rr[:, f])
        else:
            TCc[p] = np.cos(cc[:, f - 16]); TCs[p] = sgn * np.sin(cc[:, f - 16])
    out["ret"] = (TRc, TRs, TCc, TCs)
    rr, cc = tab(8, nrows)
    TRc = np.ones((128, nrows), np.float32); TRs = np.zeros((128, nrows), np.float32)
    TCc = np.ones((128, 64), np.float32); TCs = np.zeros((128, 64), np.float32)
    for key, prange in (("mla", range(64, 96)), ("mlk", range(0, 32))):
        TRc = np.ones((128, nrows), np.float32); TRs = np.zeros((128, nrows), np.float32)
        TCc = np.ones((128, 64), np.float32); TCs = np.zeros((128, 64), np.float32)
        for p in prange:
            d = p % 32
            f = d % 16
            sgn = -1.0 if d < 16 else 1.0
            if f < 8:
                TRc[p] = np.cos(rr[:, f]); TRs[p] = sgn * np.sin(rr[:, f])
            else:
                TCc[p] = np.cos(cc[:, f - 8]); TCs[p] = sgn * np.sin(cc[:, f - 8])
        out[key] = (TRc, TRs, TCc, TCs)
    return out


class Ctx:
    pass


def build(L, n_layers=2, debug=False, upto="Z"):
    nc = bass.Bass("TRN2", target_bir_lowering=False)
    kb = KB(nc, n_dma_sems=48)
    T = NCTX + L
    NR = max(L // 64, 1)
    g = Ctx()
    g.nc, g.kb, g.L, g.T = nc, kb, L, T

    def din(name, shape):
        return nc.dram_tensor(name, list(shape), F32, kind="ExternalInput").ap()

    NL = 2
    I = {}
    I["x"] = din("x", [L, D]); I["ctx"] = din("ctx", [NCTX, D]); I["c2"] = din("c2", [2, D])
    for nm, shp in [("w_ada", [NL, D, 6 * D]), ("b_ada", [NL, 6 * D]), ("norm1_g", [NL, D]), ("norm2_g", [NL, D]),
                    ("w_in", [NL, D, N_IN]), ("conv_w", [NL, 31, 512]), ("conv_b", [NL, 512]),
                    ("conv_ln_g", [NL, 512]), ("conv_ln_b", [NL, 512]), ("ssm_conv_w", [NL, 5, 1024]),
                    ("ssm_conv_b", [NL, 1024]), ("ssm_dt_bias", [NL, 16]), ("ssm_a_log", [NL, 16]),
                    ("ssm_d", [NL, 8]), ("ssm_norm_g", [NL, 512]), ("ret_decay", [NL, 8]),
                    ("ret_gn_g", [NL, 512]), ("ret_gn_b", [NL, 512]), ("mla_q_norm_g", [NL, 384]),
                    ("mla_kv_norm_g", [NL, 256]), ("mla_w_uq", [NL, 384, 768]), ("mla_w_ukv", [NL, 256, 1024]),
                    ("w_branch", [NL, 4, 512, D]), ("w_out", [NL, D, D]), ("w_ffn_in", [NL, D, 2 * FFN]),
                    ("w_ffn_out", [NL, FFN, D]), ("final_norm_g", [1, D]),
                    ("t_ret", [128, 2 * NR + 128]), ("t_mla", [128, 2 * NR + 128]), ("t_mlk", [128, 2 * NR + 128])]:
        I[nm] = din(nm, shp)
    out_d = nc.dram_tensor("out", [L, D], F32, kind="ExternalOutput").ap()

    dbg = {}

    def scratch(name, shape, dt=BF16):
        if debug:
            ap = nc.dram_tensor(name, list(shape), dt, kind="ExternalOutput").ap()
            dbg[name] = ap
        else:
            ap = nc.dram_tensor(name, list(shape), dt, kind="Internal").ap()
        return ap, kb.trk(name)

    S = {}
    for nm, shp, dt in [("Win", [128, 8, N_EXT], BF16), ("Wgl", [128, 8, 4096], BF16), ("Wuq", [128, 3, 1536], BF16),
                        ("Wk", [128, 2, 512], BF16), ("Wv", [128, 2, 512], BF16), ("Wb", [128, 16, D], BF16),
                        ("Wout", [128, 8, D], BF16), ("Wf1", [128, 8, 2 * FFN], BF16), ("Wf2", [128, 22, D], BF16),
                        ("uT", [512, T], BF16), ("zs", [T, 512], BF16), ("xbcT", [1024, T], BF16),
                        ("dt", [T, 16], F32), ("la", [T, 16], F32),
                        ("qrT", [256, T], BF16), ("krT", [256, T], BF16), ("kr", [T, 256], BF16),
                        ("rv", [T, 512], BF16), ("rg", [T, 512], BF16),
                        ("qmT", [8, 96, T], BF16), ("kmT", [512, T], BF16), ("kropeT", [32, T], BF16),
                        ("vm", [T, 512], BF16),
                        ("ssx", [T, 512], BF16), ("ssB", [T, 256], BF16), ("ssBT", [256, T], BF16),
                        ("ssCT", [256, T], BF16), ("yf", [T, 512], F32), ("ryf", [T, 512], F32), ("yb", [T, 512], F32), ("ryb", [T, 512], F32),
                        ("br0T", [512, T], BF16), ("br1T", [512, T], BF16), ("br2T", [512, T], BF16),
                        ("br3T", [512, T], BF16), ("xs", [T, D], F32)]:
        S[nm] = scratch(nm, shp, dt)
    for nm, kc, n_ in (("Win", 8, N_EXT), ("Wgl", 8, 4096), ("Wb", 16, D), ("Wout", 8, D), ("Wf1", 8, 2 * FFN), ("Wf2", 22, D)):
        S[nm + "_pk"] = (nc.dram_tensor(nm + "_pk", [128, kc * n_], BF16, kind="Internal").ap(), kb.trk(nm + "_pk"))
    g.S, g.I = S, I

    def V(fn, r=(), w=(), **k): return kb.op(kb.dve, fn, r, w, **k)
    def A(fn, r=(), w=(), **k): return kb.op(kb.act, fn, r, w, **k)
    def P(fn, r=(), w=(), **k): return kb.op(kb.pe, fn, r, w, **k)
    def G(fn, r=(), w=(), **k): return kb.op(kb.pool, fn, r, w, **k)
    g.V, g.A, g.P, g.G = V, A, P, G
    rr = [0]

    def VA(fnv, fna, r=(), w=()):
        rr[0] += 1
        if rr[0] % 2:
            return V(fnv, r, w)
        return A(fna, r, w)

    psb = [(kb.ps("psb%d" % i, [128, 512], F32), kb.trk()) for i in range(8)]
    for _, t_ in psb:
        t_.excl = True
    pi = [0]

    g.ps_skip = set()
    g.psb = psb

    def psum():
        pi[0] = (pi[0] + 1) % 8
        while pi[0] in g.ps_skip:
            pi[0] = (pi[0] + 1) % 8
        return psb[pi[0]]
    g.psum = psum

    class Pool:
        def __init__(s, name, shape, dt, n):
            s.tiles = [(kb.sb("%s%d" % (name, i), shape, dt), kb.trk()) for i in range(n)]
            s.i = 0

        def get(s):
            s.i = (s.i + 1) % len(s.tiles)
            return s.tiles[s.i]

    ident_b = kb.sb("ident_b", [128, 128], BF16); t_c = kb.trk()
    ones_b = kb.sb("ones_b", [128, 128], BF16)
    ones_f = kb.sb("ones_f", [128, 128], F32)
    Uf = kb.sb("Uf", [128, 128], F32)
    Lf = kb.sb("Lf", [128, 128], F32)
    G(lambda: nc.gpsimd.memset(ones_b, 1.0), w=[t_c])
    G(lambda: nc.gpsimd.memset(ones_f, 1.0), w=[t_c])
    G(lambda: nc.gpsimd.affine_select(ident_b, ones_b, pattern=[[-1, 128]], compare_op=ALU.is_equal, fill=0.0,
                                      base=0, channel_multiplier=1), r=[t_c], w=[t_c])
    G(lambda: nc.gpsimd.affine_select(Uf, ones_f, pattern=[[1, 128]], compare_op=ALU.is_ge, fill=0.0,
                                      base=0, channel_multiplier=-1), r=[t_c], w=[t_c])
    G(lambda: nc.gpsimd.affine_select(Lf, ones_f, pattern=[[-1, 128]], compare_op=ALU.is_ge, fill=0.0,
                                      base=0, channel_multiplier=1), r=[t_c], w=[t_c])
    g.ident_b, g.ones_b, g.ones_f, g.Uf, g.Lf, g.t_c = ident_b, ones_b, ones_f, Uf, Lf, t_c
    g.VA = VA
    tab_ret = kb.sb("tab_ret", [128, 2 * NR + 128], F32)
    tab_mla = kb.sb("tab_mla", [128, 2 * NR + 128], F32)
    kb.dma(tab_ret, I["t_ret"], writes=[t_c])
    kb.dma(tab_mla, I["t_mla"], writes=[t_c])
    tab_mlk = kb.sb("tab_mlk", [128, 2 * NR + 128], F32)
    kb.dma(tab_mlk, I["t_mlk"], writes=[t_c])
    g.tab_mlk = tab_mlk
    g.tab_ret, g.tab_mla = tab_ret, tab_mla
    g.out_d = out_d
    g.t_out = kb.trk()
    fin_g = kb.sb("fin_g", [128, D], F32)
    g.fin_g = fin_g
    kb.dma(fin_g, I["final_norm_g"][0].partition_broadcast(128), writes=[t_c])

    tiles = [(0, NCTX, True, 0)]
    for t in range(L // 512):
        tiles.append((NCTX + t * 512, 512, False, t * 512))
    g.tiles = tiles

    modc = kb.sb("modc", [128, 6, 8, 2], F32); t_modc = kb.trk()
    modr = kb.sb("modr", [128, 2, 2, D], F32); t_modr = kb.trk()
    G1c = kb.sb("G1c", [128, 8, 2], F32); G2c = kb.sb("G2c", [128, 8, 2], F32); t_gc = kb.trk()
    colv = kb.sb("colv", [128, 64], F32); t_colv = kb.trk()
    g.modc, g.modr, g.G1c, g.G2c = modc, modr, G1c, G2c

    def coldma(dst, v, k, t_dst):
        for j_ in range(k):
            kb.dma(dst[:, j_:j_ + 1], v[j_ * 128:(j_ + 1) * 128].rearrange("(p o) -> p o", o=1), writes=[t_dst])
    g.coldma = coldma

    for layer in range(n_layers):
        last = (layer == n_layers - 1)
        li = layer
        g.packed, g.pk_off = {}, {}
        g.wpf = None
        kb.barrier()
        with nc.sbuf_tensor("w_stg%d" % li, [128, 3, 2048], F32) as stg_t, nc.sbuf_tensor("w_stb%d" % li, [128, 3, 2048], BF16) as stb_t, \
                nc.sbuf_tensor("w_rs%d" % li, [128, 8], F32) as rs_t:
            stg, stb, rs = stg_t.ap(), stb_t.ap(), rs_t.ap()
            t_stg = [kb.trk() for _ in range(3)]; t_stb = [kb.trk() for _ in range(3)]; t_rs = kb.trk()
            wi = [0]

            def prep(src, dst, t_dst, dc0, n, rowscale=None, mul=None, swap=None):
                i = wi[0] % 3; wi[0] += 1
                kb.dma(stg[:, i, :n], src, writes=[t_stg[i]])
                o = stb[:, i, :n]; s_ = stg[:, i, :n]
                if swap is not None:
                    hd = swap
                    ov = o.rearrange("p (h two e) -> p h two e", two=2, e=hd)
                    sv = s_.rearrange("p (h two e) -> p h two e", two=2, e=hd)
                    m_ = 1.0 if mul is None else mul
                    V(lambda: nc.vector.tensor_scalar(ov[:, :, 0, :], sv[:, :, 1, :], m_, None, ALU.mult),
                      r=[t_stg[i]], w=[t_stb[i]])
                    V(lambda: nc.vector.tensor_scalar(ov[:, :, 1, :], sv[:, :, 0, :], m_, None, ALU.mult),
                      r=[t_stg[i]], w=[t_stb[i]])
                elif rowscale is not None:
                    V(lambda: nc.vector.tensor_scalar(o, s_, rowscale, None, ALU.mult), r=[t_stg[i], t_rs], w=[t_stb[i]])
                elif mul is not None:
                    V(lambda: nc.vector.tensor_scalar(o, s_, mul, None, ALU.mult), r=[t_stg[i]], w=[t_stb[i]])
                else:
                    VA(lambda: nc.vector.tensor_copy(o, s_), lambda: nc.scalar.copy(o, s_), r=[t_stg[i]], w=[t_stb[i]])
                kb.dma(dst, o, reads=[t_stb[i]], writes=[t_dst])

            def prep_mat(src2d, K, N, dname, dk0=0, dc0=0, sc0=0, **kw):
                dst, t_dst = S[dname]
                for kc in range(K // 128):
                    for c0 in range(0, N, 2048):
                        n = min(2048, N - c0)
                        prep(src2d[kc * 128:(kc + 1) * 128, sc0 + c0:sc0 + c0 + n],
                             dst[:, dk0 + kc, dc0 + c0:dc0 + c0 + n], t_dst, 0, n, **kw)

            w_in = I["w_in"][li]
            prep_mat(w_in, D, C_RK, "Win")
            prep_mat(w_in, D, 256, "Win", dc0=C_RK, sc0=C_RK, mul=0.125)
            prep_mat(w_in, D, C_GL - C_RV, "Win", dc0=C_RV, sc0=C_RV)
            prep_mat(w_in, D, 256, "Win", dc0=E_RQS, sc0=C_RQ, swap=32)
            prep_mat(w_in, D, 256, "Win", dc0=E_RKS, sc0=C_RK, swap=32, mul=0.125)
            prep_mat(w_in, D, 32, "Win", dc0=E_KRS, sc0=C_KR, swap=16)
            prep_mat(w_in, D, 4096, "Wgl", sc0=C_GL)
            coldma(rs[:, 0:3], I["mla_q_norm_g"][li], 3, t_rs)
            coldma(rs[:, 3:5], I["mla_kv_norm_g"][li], 2, t_rs)
            uq = I["mla_w_uq"][li]
            dst, t_dst = S["Wuq"]
            for kc in range(3):
                i = wi[0] % 3; wi[0] += 1
                kb.dma(stg[:, i, :768], uq[kc * 128:(kc + 1) * 128, :], writes=[t_stg[i]])
                o = stb[:, i, :1536]; s_ = stg[:, i, :768]
                V(lambda: nc.vector.tensor_scalar(o[:, 0:768], s_, rs[:, kc:kc + 1], None, ALU.mult),
                  r=[t_stg[i], t_rs], w=[t_stb[i]])
                ov = o[:, 768:1536].rearrange("p (h e) -> p h e", e=96)
                sv = o[:, 0:768].rearrange("p (h e) -> p h e", e=96)
                V(lambda: nc.vector.tensor_copy(ov[:, :, 0:64], sv[:, :, 0:64]), r=[t_stb[i]], w=[t_stb[i]])
                V(lambda: nc.vector.tensor_copy(ov[:, :, 64:80], sv[:, :, 80:96]), r=[t_stb[i]], w=[t_stb[i]])
                V(lambda: nc.vector.tensor_copy(ov[:, :, 80:96], sv[:, :, 64:80]), r=[t_stb[i]], w=[t_stb[i]])
                kb.dma(dst[:, kc, :], o, reads=[t_stb[i]], writes=[t_dst])
            ukv = I["mla_w_ukv"][li]
            for kc in range(2):
                i = wi[0] % 3; wi[0] += 1
                kb.dma(stg[:, i, :1024], ukv[kc * 128:(kc + 1) * 128, :], writes=[t_stg[i]])
                o = stb[:, i, :1024]; s_ = stg[:, i, :1024]
                sv = s_.rearrange("p (h two e) -> p h two e", two=2, e=64)
                ov = o.rearrange("p (two h e) -> p two h e", two=2, e=64)
                V(lambda: nc.vector.tensor_scalar(ov[:, 0], sv[:, :, 0, :], rs[:, 3 + kc:4 + kc], None, ALU.mult),
                  r=[t_stg[i], t_rs], w=[t_stb[i]])
                V(lambda: nc.vector.tensor_scalar(ov[:, 1], sv[:, :, 1, :], rs[:, 3 + kc:4 + kc], None, ALU.mult),
                  r=[t_stg[i], t_rs], w=[t_stb[i]])
                kb.dma(S["Wk"][0][:, kc, :], o[:, 0:512], reads=[t_stb[i]], writes=[S["Wk"][1]])
                kb.dma(S["Wv"][0][:, kc, :], o[:, 512:1024], reads=[t_stb[i]], writes=[S["Wv"][1]])
            for b in range(4):
                prep_mat(I["w_branch"][li, b], 512, D, "Wb", dk0=b * 4)
            prep_mat(I["w_out"][li], D, D, "Wout")
            prep_mat(I["w_ffn_in"][li], D, 2 * FFN, "Wf1")
            prep_mat(I["w_ffn_out"][li], FFN, D, "Wf2")
        if upto == "W":
            break
        kb.barrier()
        with nc.sbuf_tensor("m_w%d" % li, [128, 8, 1024], F32) as mw_t, nc.sbuf_tensor("m_cs%d" % li, [128, 8, 2], F32) as cs_t, \
                nc.sbuf_tensor("m_rep%d" % li, [128, 8, 2, 128], F32) as rep_t, nc.sbuf_tensor("m_bc%d" % li, [128, 48], F32) as bc_t, \
                nc.sbuf_tensor("m_br%d" % li, [128, 2, D], F32) as br_t, nc.sbuf_tensor("m_ng%d" % li, [128, 16], F32) as ng_t:
            mw, cs, rep, bc, br, ng = mw_t.ap(), cs_t.ap(), rep_t.ap(), bc_t.ap(), br_t.ap(), ng_t.ap()
            t_mw, t_cs, t_rep, t_bc, t_br, t_ng = [kb.trk() for _ in range(6)]
            for j in range(2):
                for k_ in range(8):
                    kb.dma(cs[:, k_, j:j + 1], I["c2"][j, k_ * 128:(k_ + 1) * 128].rearrange("(p o) -> p o", o=1), writes=[t_cs])
            A(lambda: nc.scalar.activation(cs, cs, AF.Silu), r=[t_cs], w=[t_cs])
            V(lambda: nc.vector.tensor_copy(rep, cs.unsqueeze(3).broadcast_to([128, 8, 2, 128])), r=[t_cs], w=[t_rep])
            coldma(bc, I["b_ada"][li], 48, t_bc)
            kb.dma(br[:, 0, :], I["b_ada"][li, 2 * D:3 * D].partition_broadcast(128), writes=[t_br])
            kb.dma(br[:, 1, :], I["b_ada"][li, 5 * D:6 * D].partition_broadcast(128), writes=[t_br])
            coldma(ng[:, 0:8], I["norm1_g"][li], 8, t_ng)
            coldma(ng[:, 8:16], I["norm2_g"][li], 8, t_ng)
            for m in range(6):
                kb.dma(mw, I["w_ada"][li][:, m * 1024:(m + 1) * 1024].rearrange("(k p) n -> p k n", p=128), writes=[t_mw])
                for n8 in range(8):
                    ps_, tp = psum()
                    for k in range(8):
                        P(lambda: nc.tensor.matmul(ps_[:, 0:2], mw[:, k, n8 * 128:(n8 + 1) * 128], cs[:, k, :],
                                                   start=(k == 0), stop=(k == 7)), r=[t_mw, t_cs], w=[tp], acc=(k > 0), inc=(k == 7))
                    V(lambda: nc.vector.tensor_scalar(modc[:, m, n8, :], ps_[:, 0:2], bc[:, m * 8 + n8:m * 8 + n8 + 1], None, ALU.add),
                      r=[tp, t_bc], w=[t_modc])
                if m in (2, 5):
                    gi = 0 if m == 2 else 1
                    for j in range(2):
                        for hf in range(2):
                            ps_, tp = psum()
                            for k in range(8):
                                P(lambda: nc.tensor.matmul(ps_, rep[:, k, j, :], mw[:, k, hf * 512:(hf + 1) * 512],
                                                           start=(k == 0), stop=(k == 7)), r=[t_mw, t_rep], w=[tp], acc=(k > 0), inc=(k == 7))
                            V(lambda: nc.vector.tensor_tensor(modr[:, j, gi, hf * 512:(hf + 1) * 512], ps_,
                                                              br[:, gi, hf * 512:(hf + 1) * 512], ALU.add), r=[tp, t_br], w=[t_modr])
            for (Gc, ms, o8) in ((G1c, 1, 0), (G2c, 4, 8)):
                V(lambda: nc.vector.tensor_scalar(Gc, modc[:, ms], 1.0, None, ALU.add), r=[t_modc], w=[t_gc])
                V(lambda: nc.vector.tensor_tensor(Gc, Gc, ng[:, o8:o8 + 8].unsqueeze(2).broadcast_to([128, 8, 2]), ALU.mult),
                  r=[t_ng, t_gc], w=[t_gc])
        g.t_modc, g.t_modr, g.t_gc = t_modc, t_modr, t_gc
        if upto == "M":
            break
        from contextlib import ExitStack
        done = False
        for ph in "ABCDEF":
            fn = {"A": phase_A, "B": phase_B, "C": phase_C, "D": phase_D, "E": phase_E, "F": phase_F}[ph]
            kb.barrier()
            with ExitStack() as st:
                kb.stack = st
                fn(g, li) if ph == 'A' else fn(g, li, last)
                g.wpf = None
                kb.barrier()
            kb.stack = None
            if upto == ph:
                done = True
                break
        if done:
            break

    kb.barrier()
    if debug:
        for nm, ap in (("d_modc", modc), ("d_modr", modr), ("d_G1c", G1c)):
            d_ = nc.dram_tensor(nm, list(ap.shape), F32, kind="ExternalOutput").ap()
            kb.dma(d_, ap, reads=[t_modc, t_modr, t_gc], writes=[kb.trk()])
            dbg[nm] = d_
    kb.barrier()
    kb.finish([g.t_out])
    return nc, dbg


ASTOP = 99


class RPool:
    def __init__(s, kb, name, shape, dt, n):
        s.tiles = [(kb.sb("%s%d" % (name, i), shape, dt), kb.trk()) for i in range(n)]
        s.i = 0

    def get(s):
        s.i = (s.i + 1) % len(s.tiles)
        return s.tiles[s.i]


def mk_common(g):
    kb = g.kb
    c = Ctx()
    c.xt = RPool(kb, "c_xt", [128, 4, D], F32, 1)
    c.xn = RPool(kb, "c_xn", [128, 4, D], BF16, 1)
    c.hT = RPool(kb, "c_hT", [128, 8, 512], BF16, 2)
    c.ss = RPool(kb, "c_ss", [128, 8], F32, 2)
    c.junk = RPool(kb, "c_junk", [128, D], BF16, 1)
    c.wt = RPool(kb, "c_wt", [128, 8, 512], BF16, 4)
    c.stb = RPool(kb, "c_stb", [128, 512], BF16, 20)
    c.stf = RPool(kb, "c_stf", [128, 512], F32, 8)
    return c


def norm_tile(g, c, src, t_src, n, Gc, Sc, j, xt_pair=None):
    kb, nc, V, A, P = g.kb, g.nc, g.V, g.A, g.P
    nb = n // 128
    if xt_pair is None:
        xt, t_xt = c.xt.get()
        kb.dma(xt[:, :nb, :], src.rearrange("(b p) d -> p b d", p=128), reads=[t_src], writes=[t_xt])
    else:
        xt, t_xt = xt_pair
    ss, t_ss = c.ss.get()
    junk, t_junk = c.junk.get()
    for b in range(nb):
        A(lambda: nc.scalar.activation(junk, xt[:, b, :], AF.Square, accum_out=ss[:, b:b + 1]), r=[t_xt], w=[t_junk, t_ss])
    V(lambda: nc.vector.tensor_scalar(ss[:, 0:nb], ss[:, 0:nb], 1.0 / D, EPS, ALU.mult, ALU.add), r=[t_ss], w=[t_ss])
    A(lambda: nc.scalar.activation(ss[:, 0:nb], ss[:, 0:nb], AF.Sqrt), r=[t_ss], w=[t_ss])
    V(lambda: nc.vector.reciprocal(ss[:, 0:nb], ss[:, 0:nb]), r=[t_ss], w=[t_ss])
    xn, t_xn = c.xn.get()
    for b in range(nb):
        V(lambda: nc.vector.tensor_scalar(xn[:, b, :], xt[:, b, :], ss[:, b:b + 1], None, ALU.mult), r=[t_xt, t_ss], w=[t_xn])
    hT, t_hT = c.hT.get()
    for k in range(8):
        ps_, tp = g.psum()
        psv = ps_.bitcast(BF16)
        for b in range(nb):
            P(lambda: nc.tensor.transpose(psv[:, b * 128:(b + 1) * 128], xn[:, b, k * 128:(k + 1) * 128], g.ident_b),
              r=[t_xn, g.t_c], w=[tp], acc=(b > 0), inc=(b == nb - 1))
        V(lambda: nc.vector.tensor_scalar(hT[:, k, :n], psv[:, :n], Gc[:, k, j:j + 1], Sc[:, k, j:j + 1], ALU.mult, ALU.add),
          r=[tp, g.t_gc, g.t_modc], w=[t_hT])
    return hT, t_hT, xt, t_xt, ss, t_ss


def _wissue(g, c, Wname, kc0, KC, c0, w, pool):
    W, t_W = g.S[Wname]
    Wp, t_Wp = g.S[Wname + "_pk"]
    key = (Wname, kc0, KC, c0, w)
    off = g.packed.get(key)
    if off is None:
        off = g.pk_off.get(Wname, 0)
        g.pk_off[Wname] = off + KC * w
        g.packed[key] = off
        g.kb.dma(Wp[:, off:off + KC * w].rearrange("p (k n) -> p k n", n=w), W[:, kc0:kc0 + KC, c0:c0 + w],
                 reads=[t_W], writes=[t_Wp])
    wt, t_w = (pool or c.wt).get()
    flat = wt.rearrange("p k n -> p (k n)")
    g.kb.dma(flat[:, :KC * w], Wp[:, off:off + KC * w], reads=[t_Wp], writes=[t_w])
    return flat[:, :KC * w].rearrange("p (k n) -> p k n", n=w), t_w


def wload(g, c, Wname, kc0, KC, c0, w, pool=None):
    req = (Wname, kc0, KC, c0, w, pool)
    st = g.wpf
    if st is None or st["mode"] == "plain":
        return _wissue(g, c, *req)
    if st["mode"] == "record":
        st["seq"].append(req)
        return _wissue(g, c, *req)
    seq = st["seq"]; n = len(seq)
    r = st["consumed"]
    assert seq[r % n][:5] == req[:5], (seq[r % n][:5], req[:5])
    while st["issued"] <= r + st["depth"]:
        q = seq[st["issued"] % n]
        if q[5] is not None and st["issued"] > r:
            break
        st["map"][st["issued"]] = _wissue(g, c, *q)
        st["issued"] += 1
    st["consumed"] += 1
    return st["map"].pop(r)


def wpf_tile(g, is_ctx):
    st = g.wpf
    if is_ctx:
        st["mode"] = "plain"
    elif not st["seq"]:
        st["mode"] = "record"
    else:
        st["mode"] = "prefetch"


def wpf_begin(g):
    g.wpf = {"mode": "plain", "seq": [], "consumed": 0, "issued": 0, "map": {}, "depth": 2}


def fm_chunks(g, c, Wname, KC, c0, ncols, hT, t_h, n, kc0=0, pool=None, msz=128):
    nc, P = g.nc, g.P
    for cc in range(c0, c0 + ncols, 512):
        w = min(512, c0 + ncols - cc)
        wt, t_w = wload(g, c, Wname, kc0, KC, cc, w, pool)
        for j in range(0, w, msz):
            m = min(msz, w - j)
            ps_, tp = g.psum()
            for k in range(KC):
                P(lambda: nc.tensor.matmul(ps_[:m, :n], wt[:, k, j:j + m], hT[:, k, :n], start=(k == 0), stop=(k == KC - 1)),
                  r=[t_w, t_h], w=[tp], acc=(k > 0), inc=(k == KC - 1))
            yield (cc - c0 + j, m, ps_, tp)


def tm_chunks(g, c, Wname, KC, c0, ncols, hT, t_h, n, kc0=0, pool=None):
    nc, P = g.nc, g.P
    for cc in range(c0, c0 + ncols, 512):
        w = min(512, c0 + ncols - cc)
        wt, t_w = wload(g, c, Wname, kc0, KC, cc, w, pool)
        for b in range(n // 128):
            ps_, tp = g.psum()
            for k in range(KC):
                P(lambda: nc.tensor.matmul(ps_[:, :w], hT[:, k, b * 128:(b + 1) * 128], wt[:, k, :w], start=(k == 0), stop=(k == KC - 1)),
                  r=[t_w, t_h], w=[tp], acc=(k > 0), inc=(k == KC - 1))
            yield (cc - c0, w, b, ps_, tp)


def store(g, name, dram_ap, sb_ap, t_sb):
    g.kb.dma_deferred(dram_ap, sb_ap, reads=[t_sb], writes=[g.S[name][1]])


def phase_A(g, li):
    kb, nc, V, A, P, G = g.kb, g.nc, g.V, g.A, g.P, g.G
    S, I, L, T = g.S, g.I, g.L, g.T
    NR = max(L // 64, 1)
    c = mk_common(g)
    cos_r = kb.sb("a_cos_r", [128, 512], F32); sin_r = kb.sb("a_sin_r", [128, 512], F32)
    cos_m = kb.sb("a_cos_m", [128, 512], F32); sin_m = kb.sb("a_sin_m", [128, 512], F32); t_tab = kb.trk()
    cos_k = kb.sb("a_cos_k", [128, 512], F32); sin_k = kb.sb("a_sin_k", [128, 512], F32)
    dtb = kb.sb("a_dtb", [128, 32], F32); t_dtb = kb.trk()
    kb.dma(dtb[:, 0:16], I["ssm_dt_bias"][li].partition_broadcast(128), writes=[t_dtb])
    kb.dma(dtb[:, 16:32], I["ssm_a_log"][li].partition_broadcast(128), writes=[t_dtb])
    A(lambda: nc.scalar.activation(dtb[:, 16:32], dtb[:, 16:32], AF.Exp), r=[t_dtb], w=[t_dtb])
    V(lambda: nc.vector.tensor_scalar(dtb[:, 16:32], dtb[:, 16:32], -1.0, None, ALU.mult), r=[t_dtb], w=[t_dtb])
    cq = RPool(kb, "a_cq", [128, 3, 512], BF16, 1)
    sq = RPool(kb, "a_sq", [128, 3, 512], BF16, 1)
    rsb = RPool(kb, "a_rsb", [128, 512], F32, 2)
    wuq = kb.sb("a_wuq", [128, 3, 1536], BF16); t_wuq = kb.trk()
    wk = kb.sb("a_wk", [128, 2, 512], BF16); wv = kb.sb("a_wv", [128, 2, 512], BF16); t_wkv = kb.trk()
    kb.dma(wuq, S["Wuq"][0], reads=[S["Wuq"][1]], writes=[t_wuq])
    kb.dma(wk, S["Wk"][0], reads=[S["Wk"][1]], writes=[t_wkv])
    kb.dma(wv, S["Wv"][0], reads=[S["Wv"][1]], writes=[t_wkv])
    tk_ = RPool(kb, "a_tk", [128, 4, 128], BF16, 6)
    sm = RPool(kb, "a_sm", [128, 16], F32, 2)
    st16 = RPool(kb, "a_st16", [128, 32], F32, 12)
    SC_Q = 96.0 ** -0.5

    wpf_begin(g)
    for (t0, n, is_ctx, pos0) in g.tiles:
        wpf_tile(g, is_ctx)
        j = 1 if is_ctx else 0
        nb = n // 128
        src = (I["ctx"] if is_ctx else I["x"][pos0:pos0 + n]) if li == 0 else S["xs"][0][t0:t0 + n]
        t_src = kb.trk() if li == 0 else S["xs"][1]
        hT, t_h, _, _, _, _ = norm_tile(g, c, src, t_src, n, g.G1c, g.modc[:, 0], j)
        if not is_ctx:
            r0 = pos0 // 64
            for (tab, cs_, sn_) in ((g.tab_ret, cos_r, sin_r), (g.tab_mla, cos_m, sin_m), (g.tab_mlk, cos_k, sin_k)):
                V(lambda: nc.vector.tensor_tensor(cs_.rearrange("p (r c) -> p r c", c=64),
                                                  tab[:, r0:r0 + 8].unsqueeze(2).broadcast_to([128, 8, 64]),
                                                  tab[:, 2 * NR:2 * NR + 64].unsqueeze(1).broadcast_to([128, 8, 64]), ALU.mult),
                  r=[g.t_c], w=[t_tab])
                V(lambda: nc.vector.tensor_tensor(sn_.rearrange("p (r c) -> p r c", c=64),
                                                  tab[:, NR + r0:NR + r0 + 8].unsqueeze(2).broadcast_to([128, 8, 64]),
                                                  tab[:, 2 * NR + 64:2 * NR + 128].unsqueeze(1).broadcast_to([128, 8, 64]), ALU.add),
                  r=[g.t_c], w=[t_tab])
        if ASTOP == 0:
            return
        ga = fm_chunks(g, c, "Win", 8, C_CONV, 512, hT, t_h, n)
        gg = fm_chunks(g, c, "Win", 8, C_CONV + 512, 512, hT, t_h, n)
        for (off, m, pa, tpa), (_, _, pg, tpg) in zip(ga, gg):
            sg, t_sg = c.stf.get()
            A(lambda: nc.scalar.activation(sg[:, :n], pg[:, :n], AF.Sigmoid), r=[tpg], w=[t_sg])
            u, t_u = c.stb.get()
            V(lambda: nc.vector.tensor_tensor(u[:, :n], pa[:, :n], sg[:, :n], ALU.mult), r=[tpa, t_sg], w=[t_u])
            store(g, "uT", S["uT"][0][off:off + 128, t0:t0 + n], u[:, :n], t_u)
        if ASTOP == 1:
            return
        for (c0, nm, fn) in ((C_Z, "zs", AF.Silu), (C_RV, "rv", AF.Copy), (C_RG, "rg", AF.Silu)):
            for (off, w, b, ps_, tp) in tm_chunks(g, c, "Win", 8, c0, 512, hT, t_h, n):
                o, t_o = c.stb.get()
                A(lambda: nc.scalar.activation(o, ps_, fn), r=[tp], w=[t_o])
                store(g, nm, S[nm][0][t0 + b * 128:t0 + (b + 1) * 128, :], o, t_o)
        if ASTOP == 2:
            return
        for (off, m, ps_, tp) in fm_chunks(g, c, "Win", 8, C_XBC, 1024, hT, t_h, n):
            o, t_o = c.stb.get()
            g.VA(lambda: nc.vector.tensor_copy(o[:, :n], ps_[:, :n]), lambda: nc.scalar.copy(o[:, :n], ps_[:, :n]), r=[tp], w=[t_o])
            store(g, "xbcT", S["xbcT"][0][off:off + 128, t0:t0 + n], o[:, :n], t_o)
        if ASTOP == 3:
            return
        for (off, w, b, ps_, tp) in tm_chunks(g, c, "Win", 8, C_DT, 16, hT, t_h, n):
            o, t_o = st16.get()
            V(lambda: nc.vector.tensor_tensor(o[:, 0:16], ps_[:, 0:16], dtb[:, 0:16], ALU.add), r=[tp, t_dtb], w=[t_o])
            A(lambda: nc.scalar.activation(o[:, 0:16], o[:, 0:16], AF.Exp), r=[t_o], w=[t_o])
            A(lambda: nc.scalar.activation(o[:, 0:16], o[:, 0:16], AF.Ln, bias=1.0), r=[t_o], w=[t_o])
            V(lambda: nc.vector.tensor_tensor(o[:, 16:32], o[:, 0:16], dtb[:, 16:32], ALU.mult), r=[t_o, t_dtb], w=[t_o])
            store(g, "dt", S["dt"][0][t0 + b * 128:t0 + (b + 1) * 128, :], o[:, 0:16], t_o)
            store(g, "la", S["la"][0][t0 + b * 128:t0 + (b + 1) * 128, :], o[:, 16:32], t_o)
        if ASTOP == 4:
            return
        for (cN, cS_, nm, is_k) in ((C_RQ, E_RQS, "qrT", False), (C_RK, E_RKS, "krT", True)):
            gn = fm_chunks(g, c, "Win", 8, cN, 256, hT, t_h, n)
            gs = fm_chunks(g, c, "Win", 8, cS_, 256, hT, t_h, n) if not is_ctx else None
            for ci in range(2):
                off, m, pn, tpn = next(gn)
                o, t_o = c.stb.get()
                if is_ctx:
                    V(lambda: nc.vector.tensor_copy(o[:, :n], pn[:, :n]), r=[tpn], w=[t_o])
                else:
                    _, _, pw, tpw = next(gs)
                    t1, t_1 = c.stf.get(); t2, t_2 = c.stf.get()
                    V(lambda: nc.vector.tensor_tensor(t1[:, :n], pn[:, :n], cos_r[:, :n], ALU.mult), r=[tpn, t_tab], w=[t_1])
                    V(lambda: nc.vector.tensor_tensor(t2[:, :n], pw[:, :n], sin_r[:, :n], ALU.mult), r=[tpw, t_tab], w=[t_2])
                    G(lambda: nc.gpsimd.tensor_tensor(o[:, :n], t1[:, :n], t2[:, :n], ALU.add), r=[t_1, t_2], w=[t_o])
                store(g, nm, S[nm][0][off:off + 128, t0:t0 + n], o[:, :n], t_o)
                if is_k:
                    ps_, tp = g.psum(); psv = ps_.bitcast(BF16)
                    for b in range(nb):
                        P(lambda: nc.tensor.transpose(psv[:, b * 128:(b + 1) * 128], o[:, b * 128:(b + 1) * 128], g.ident_b),
                          r=[t_o, g.t_c], w=[tp], acc=(b > 0), inc=(b == nb - 1))
                    kt, t_kt = tk_.get()
                    A(lambda: nc.scalar.copy(kt[:, :nb, :], psv[:, :n].rearrange("p (b f) -> p b f", f=128)), r=[tp], w=[t_kt])
                    store(g, "kr", S["kr"][0][t0:t0 + n, off:off + 128].rearrange("(b p) f -> p b f", p=128), kt[:, :nb, :], t_kt)
            if gs is not None:
                for _ in gs:
                    pass
            for _ in gn:
                pass
        if ASTOP == 5:
            return
        cqT, t_cq = cq.get(); sqT, t_sq = sq.get()
        for (off, m, ps_, tp) in fm_chunks(g, c, "Win", 8, C_CQ, 384, hT, t_h, n):
            ci = off // 128
            V(lambda: nc.vector.tensor_copy(cqT[:, ci, :n], ps_[:, :n]), r=[tp], w=[t_cq])
            A(lambda: nc.scalar.activation(sqT[:, ci, :n], ps_[:, :n], AF.Square), r=[tp], w=[t_sq])
        if ASTOP == 51:
            return
        ps_, tp = g.psum()
        for ci in range(3):
            P(lambda: nc.tensor.matmul(ps_[:, :n], g.ones_b, sqT[:, ci, :n], start=(ci == 0), stop=(ci == 2)),
              r=[t_sq, g.t_c], w=[tp], acc=(ci > 0), inc=(ci == 2))
        rq_, t_rq = rsb.get()
        V(lambda: nc.vector.tensor_scalar(rq_[:, :n], ps_[:, :n], 96.0 / 384.0, EPS * 96.0, ALU.mult, ALU.add), r=[tp], w=[t_rq])
        A(lambda: nc.scalar.activation(rq_[:, :n], rq_[:, :n], AF.Sqrt), r=[t_rq], w=[t_rq])
        V(lambda: nc.vector.reciprocal(rq_[:, :n], rq_[:, :n]), r=[t_rq], w=[t_rq])
        if ASTOP == 50:
            return
        for h in range(8):
            pn, tpn = g.psum()
            for ci in range(3):
                P(lambda: nc.tensor.matmul(pn[:96, :n], wuq[:, ci, h * 96:(h + 1) * 96], cqT[:, ci, :n], start=(ci == 0), stop=(ci == 2)),
                  r=[t_wuq, t_cq], w=[tpn], acc=(ci > 0), inc=(ci == 2))
            o, t_o = c.stb.get()
            if is_ctx:
                V(lambda: nc.vector.tensor_tensor(o[:96, :n], pn[:96, :n], rq_[:96, :n], ALU.mult), r=[tpn, t_rq], w=[t_o])
            else:
                pw, tpw = g.psum()
                for ci in range(3):
                    P(lambda: nc.tensor.matmul(pw[:96, :n], wuq[:, ci, 768 + h * 96:768 + (h + 1) * 96], cqT[:, ci, :n],
                                               start=(ci == 0), stop=(ci == 2)), r=[t_wuq, t_cq], w=[tpw], acc=(ci > 0), inc=(ci == 2))
                t1, t_1 = c.stf.get(); t2, t_2 = c.stf.get()
                V(lambda: nc.vector.tensor_tensor(t1[:96, :n], pn[:96, :n], cos_m[:96, :n], ALU.mult), r=[tpn, t_tab], w=[t_1])
                V(lambda: nc.vector.tensor_tensor(t2[:96, :n], pw[:96, :n], sin_m[:96, :n], ALU.mult), r=[tpw, t_tab], w=[t_2])
                G(lambda: nc.gpsimd.tensor_tensor(t1[:96, :n], t1[:96, :n], t2[:96, :n], ALU.add), r=[t_1, t_2], w=[t_1])
                V(lambda: nc.vector.tensor_tensor(o[:96, :n], t1[:96, :n], rq_[:96, :n], ALU.mult), r=[t_1, t_rq], w=[t_o])
            store(g, "qmT", S["qmT"][0][h, :, t0:t0 + n], o[:96, :n], t_o)
        if ASTOP == 6:
            return
        ckT, t_ck = cq.get(); sk, t_sk = sq.get()
        for (off, m, ps_, tp) in fm_chunks(g, c, "Win", 8, C_CKV, 256, hT, t_h, n):
            ci = off // 128
            V(lambda: nc.vector.tensor_copy(ckT[:, ci, :n], ps_[:, :n]), r=[tp], w=[t_ck])
            A(lambda: nc.scalar.activation(sk[:, ci, :n], ps_[:, :n], AF.Square), r=[tp], w=[t_sk])
        ps_, tp = g.psum()
        for ci in range(2):
            P(lambda: nc.tensor.matmul(ps_[:, :n], g.ones_b, sk[:, ci, :n], start=(ci == 0), stop=(ci == 1)),
              r=[t_sk, g.t_c], w=[tp], acc=(ci > 0), inc=(ci == 1))
        rk_, t_rk = rsb.get()
        V(lambda: nc.vector.tensor_scalar(rk_[:, :n], ps_[:, :n], 1.0 / 256.0, EPS, ALU.mult, ALU.add), r=[tp], w=[t_rk])
        A(lambda: nc.scalar.activation(rk_[:, :n], rk_[:, :n], AF.Sqrt), r=[t_rk], w=[t_rk])
        V(lambda: nc.vector.reciprocal(rk_[:, :n], rk_[:, :n]), r=[t_rk], w=[t_rk])
        for hc in range(4):
            pn, tpn = g.psum()
            for ci in range(2):
                P(lambda: nc.tensor.matmul(pn[:, :n], wk[:, ci, hc * 128:(hc + 1) * 128], ckT[:, ci, :n], start=(ci == 0), stop=(ci == 1)),
                  r=[t_wkv, t_ck], w=[tpn], acc=(ci > 0), inc=(ci == 1))
            o, t_o = c.stb.get()
            V(lambda: nc.vector.tensor_tensor(o[:, :n], pn[:, :n], rk_[:, :n], ALU.mult), r=[tpn, t_rk], w=[t_o])
            store(g, "kmT", S["kmT"][0][hc * 128:(hc + 1) * 128, t0:t0 + n], o[:, :n], t_o)
        if ASTOP == 7:
            return
        smt, t_sm = sm.get()
        ps_, tp = g.psum()
        for b in range(nb):
            for ci in range(2):
                P(lambda: nc.tensor.matmul(ps_[:, b:b + 1], sk[:, ci, b * 128:(b + 1) * 128], g.ones_b[:, 0:1], start=(ci == 0), stop=(ci == 1)),
                  r=[t_sk, g.t_c], w=[tp], acc=(ci > 0 or b > 0), inc=(ci == 1 and b == nb - 1))
        V(lambda: nc.vector.tensor_scalar(smt[:, :nb], ps_[:, :nb], 1.0 / 256.0, EPS, ALU.mult, ALU.add), r=[tp], w=[t_sm])
        A(lambda: nc.scalar.activation(smt[:, :nb], smt[:, :nb], AF.Sqrt), r=[t_sm], w=[t_sm])
        V(lambda: nc.vector.reciprocal(smt[:, :nb], smt[:, :nb]), r=[t_sm], w=[t_sm])
        for b in range(nb):
            pn, tpn = g.psum()
            for ci in range(2):
                P(lambda: nc.tensor.matmul(pn, ckT[:, ci, b * 128:(b + 1) * 128], wv[:, ci, :], start=(ci == 0), stop=(ci == 1)),
                  r=[t_wkv, t_ck], w=[tpn], acc=(ci > 0), inc=(ci == 1))
            o, t_o = c.stb.get()
            A(lambda: nc.scalar.activation(o, pn, AF.Identity, scale=smt[:, b:b + 1]), r=[tpn, t_sm], w=[t_o])
            store(g, "vm", S["vm"][0][t0 + b * 128:t0 + (b + 1) * 128, :], o, t_o)
        if ASTOP == 8:
            return
        gn = fm_chunks(g, c, "Win", 8, C_KR, 32, hT, t_h, n)
        off, m, pn, tpn = next(gn)
        o, t_o = c.stb.get()
        if is_ctx:
            V(lambda: nc.vector.tensor_copy(o[:32, :n], pn[:32, :n]), r=[tpn], w=[t_o])
        else:
            gs = fm_chunks(g, c, "Win", 8, E_KRS, 32, hT, t_h, n)
            _, _, pw, tpw = next(gs)
            t1, t_1 = c.stf.get(); t2, t_2 = c.stf.get()
            V(lambda: nc.vector.tensor_tensor(t1[:32, :n], pn[:32, :n], cos_k[:32, :n], ALU.mult), r=[tpn, t_tab], w=[t_1])
            V(lambda: nc.vector.tensor_tensor(t2[:32, :n], pw[:32, :n], sin_k[:32, :n], ALU.mult), r=[tpw, t_tab], w=[t_2])
            G(lambda: nc.gpsimd.tensor_tensor(o[:32, :n], t1[:32, :n], t2[:32, :n], ALU.add), r=[t_1, t_2], w=[t_o])
            for _ in gs:
                pass
        for _ in gn:
            pass
        store(g, "kropeT", S["kropeT"][0][:, t0:t0 + n], o[:32, :n], t_o)


def phase_B(g, li, last=False):
    kb, nc, V, A, P, G = g.kb, g.nc, g.V, g.A, g.P, g.G
    S, I, L, T = g.S, g.I, g.L, g.T
    cwT = kb.sb("b_cwT", [128, 4, 31], F32); t_cw = kb.trk()
    for tap in range(31):
        for j in range(4):
            kb.dma(cwT[:, j, tap:tap + 1], I["conv_w"][li, tap, j * 128:(j + 1) * 128].rearrange("(p o) -> p o", o=1), writes=[t_cw])
    vec = kb.sb("b_vec", [128, 12], F32); t_vec = kb.trk()
    g.coldma(vec[:, 0:4], I["conv_b"][li], 4, t_vec)
    g.coldma(vec[:, 4:8], I["conv_ln_g"][li], 4, t_vec)
    g.coldma(vec[:, 8:12], I["conv_ln_b"][li], 4, t_vec)
    Dg = kb.sb("b_Dg", [128, 4, 31, 128], BF16); t_Dg = kb.trk()
    for j in range(4):
        for tap in range(31):
            V(lambda: nc.vector.tensor_scalar(Dg[:, j, tap, :], g.ident_b, cwT[:, j, tap:tap + 1], None, ALU.mult),
              r=[t_cw, g.t_c], w=[t_Dg])
    ut_p = RPool(kb, "b_ut", [128, 4, 542], BF16, 2)
    hc_p = RPool(kb, "b_hc", [128, 4, 512], F32, 2)
    hb_p = RPool(kb, "b_hb", [128, 4, 512], BF16, 2)
    sq_p = RPool(kb, "b_sq", [128, 4, 512], BF16, 2)
    st_p = RPool(kb, "b_st", [128, 512], F32, 6)
    ob_p = RPool(kb, "b_ob", [128, 512], BF16, 12)
    uT3 = S["uT"][0].rearrange("(j p) t -> p j t", p=128)
    for (t0, n, is_ctx, pos0) in g.tiles:
        if is_ctx and last:
            continue
        s_lo, s_hi = (0, NCTX) if is_ctx else (NCTX, T)
        lo = max(s_lo, t0 - 15); hi = min(s_hi, t0 + n + 15)
        ut, t_ut = ut_p.get()
        if lo > t0 - 15 or hi < t0 + n + 15:
            V(lambda: nc.vector.memset(ut, 0.0), w=[t_ut])
        kb.dma(ut[:, :, lo - (t0 - 15):hi - (t0 - 15)], uT3[:, :, lo:hi], reads=[S["uT"][1]], writes=[t_ut])
        hc, t_hc = hc_p.get(); hb, t_hb = hb_p.get(); sq, t_sq = sq_p.get()
        for j in range(4):
            ps_, tp = g.psum()
            for tap in range(31):
                P(lambda: nc.tensor.matmul(ps_[:, :n], Dg[:, j, tap, :], ut[:, j, tap:tap + n], start=(tap == 0), stop=(tap == 30)),
                  r=[t_Dg, t_ut], w=[tp], acc=(tap > 0), inc=(tap == 30))
            A(lambda: nc.scalar.activation(hc[:, j, :n], ps_[:, :n], AF.Identity, bias=vec[:, j:j + 1]), r=[tp, t_vec], w=[t_hc])
        V(lambda: nc.vector.tensor_copy(hb[:, :, :n], hc[:, :, :n]), r=[t_hc], w=[t_hb])
        A(lambda: nc.scalar.activation(sq[:, :, :n], hc[:, :, :n], AF.Square), r=[t_hc], w=[t_sq])
        p1, tp1 = g.psum(); p2, tp2 = g.psum()
        for j in range(4):
            P(lambda: nc.tensor.matmul(p1[:, :n], g.ones_b, hb[:, j, :n], start=(j == 0), stop=(j == 3)),
              r=[t_hb, g.t_c], w=[tp1], acc=(j > 0), inc=(j == 3))
        for j in range(4):
            P(lambda: nc.tensor.matmul(p2[:, :n], g.ones_b, sq[:, j, :n], start=(j == 0), stop=(j == 3)),
              r=[t_sq, g.t_c], w=[tp2], acc=(j > 0), inc=(j == 3))
        mu, t_mu = st_p.get(); m2, t_m2 = st_p.get(); rs, t_rs = st_p.get()
        V(lambda: nc.vector.tensor_scalar(mu[:, :n], p1[:, :n], 1.0 / 512.0, None, ALU.mult), r=[tp1], w=[t_mu])
        G(lambda: nc.gpsimd.tensor_tensor(m2[:, :n], mu[:, :n], mu[:, :n], ALU.mult), r=[t_mu], w=[t_m2])
        V(lambda: nc.vector.scalar_tensor_tensor(rs[:, :n], p2[:, :n], 1.0 / 512.0, m2[:, :n], ALU.mult, ALU.subtract),
          r=[tp2, t_m2], w=[t_rs])
        V(lambda: nc.vector.tensor_scalar(rs[:, :n], rs[:, :n], EPS, None, ALU.add), r=[t_rs], w=[t_rs])
        A(lambda: nc.scalar.activation(rs[:, :n], rs[:, :n], AF.Sqrt), r=[t_rs], w=[t_rs])
        V(lambda: nc.vector.reciprocal(rs[:, :n], rs[:, :n]), r=[t_rs], w=[t_rs])
        for j in range(4):
            tm, t_tm = st_p.get()
            G(lambda: nc.gpsimd.tensor_tensor(tm[:, :n], hc[:, j, :n], mu[:, :n], ALU.subtract), r=[t_hc, t_mu], w=[t_tm])
            V(lambda: nc.vector.tensor_tensor(tm[:, :n], tm[:, :n], rs[:, :n], ALU.mult), r=[t_tm, t_rs], w=[t_tm])
            ob, t_ob = ob_p.get()
            A(lambda: nc.scalar.activation(ob[:, :n], tm[:, :n], AF.Silu, scale=vec[:, 4 + j:5 + j], bias=vec[:, 8 + j:9 + j]),
              r=[t_tm, t_vec], w=[t_ob])
            store(g, "br0T", S["br0T"][0][j * 128:(j + 1) * 128, t0:t0 + n], ob[:, :n], t_ob)


def phase_E(g, li, last=False):
    kb, nc, V, A, P, G = g.kb, g.nc, g.V, g.A, g.P, g.G
    S, I, L, T = g.S, g.I, g.L, g.T
    NKT = T // 128
    LOOK = 3
    KT = RPool(kb, "e_kt", [96, T], BF16, 2)
    VAp = RPool(kb, "e_va", [128, NKT, 65], BF16, 2)
    for (va, t_va) in VAp.tiles:
        V(lambda: nc.vector.memset(va[:, :, 64:65], 1.0), w=[t_va])
    QT = RPool(kb, "e_qt", [96, 512], BF16, 3)
    PT = RPool(kb, "e_pt", [128, 512], BF16, 6)
    OT = RPool(kb, "e_ot", [65, 512], F32, 2)
    RD = RPool(kb, "e_rd", [64, 512], F32, 2)
    OB = RPool(kb, "e_ob", [64, 512], BF16, 2)
    g.ps_skip.update((0, 1))
    pos = [g.psb[0], g.psb[1]]
    qi = [0]
    for h in range(8):
        kt, t_kt = KT.get()
        kb.dma(kt[0:64, :], S["kmT"][0][h * 64:(h + 1) * 64, :], reads=[S["kmT"][1]], writes=[t_kt])
        kb.dma(kt[64:96, :], S["kropeT"][0], reads=[S["kropeT"][1]], writes=[t_kt])
        va, t_va = VAp.get()
        kb.dma(va[:, :, 0:64], S["vm"][0][:, h * 64:(h + 1) * 64].rearrange("(k p) d -> p k d", p=128),
               reads=[S["vm"][1]], writes=[t_va])
        its = []
        for (t0, n, is_ctx, pos0) in g.tiles:
            if is_ctx and last:
                continue
            nkt = NCTX // 128 if is_ctx else NKT
            qd = {"t0": t0, "n": n, "nkt": nkt, "qt": None}
            for ki in range(nkt):
                its.append((qd, ki))

        def issue_qk(qd, ki):
            n = qd["n"]
            if qd["qt"] is None:
                qt, t_qt = QT.get()
                kb.dma(qt[:, :n], S["qmT"][0][h, :, qd["t0"]:qd["t0"] + n], reads=[S["qmT"][1]], writes=[t_qt])
                qd["qt"] = (qt, t_qt)
                qi[0] += 1
                qd["po"] = pos[qi[0] % 2]
            qt, t_qt = qd["qt"]
            ps_, tps = g.psum()
            P(lambda: nc.tensor.matmul(ps_[:, :n], kt[:96, ki * 128:(ki + 1) * 128], qt[:96, :n], start=True, stop=True),
              r=[t_kt, t_qt], w=[tps])
            return ps_, tps

        def finish(qd, ki, ps_, tps):
            n, nkt = qd["n"], qd["nkt"]
            po, tpo = qd["po"]
            pt, t_pt = PT.get()
            A(lambda: nc.scalar.activation(pt[:, :n], ps_[:, :n], AF.Exp), r=[tps], w=[t_pt])
            P(lambda: nc.tensor.matmul(po[:65, :n], va[:, ki, 0:65], pt[:, :n], start=(ki == 0), stop=(ki == nkt - 1)),
              r=[t_va, t_pt], w=[tpo], acc=(ki > 0), inc=(ki == nkt - 1))
            if ki < nkt - 1:
                return
            ot, t_ot = OT.get()
            V(lambda: nc.vector.tensor_copy(ot[:65, :n], po[:65, :n]), r=[tpo], w=[t_ot])
            pd, tpd = g.psum()
            P(lambda: nc.tensor.matmul(pd[:64, :n], g.ones_f[64:65, 0:64], ot[64:65, :n], start=True, stop=True),
              r=[t_ot, g.t_c], w=[tpd])
            rd, t_rd = RD.get()
            V(lambda: nc.vector.reciprocal(rd[:64, :n], pd[:64, :n]), r=[tpd], w=[t_rd])
            ob, t_ob = OB.get()
            V(lambda: nc.vector.tensor_tensor(ob[:64, :n], ot[:64, :n], rd[:64, :n], ALU.mult), r=[t_ot, t_rd], w=[t_ob])
            store(g, "br3T", S["br3T"][0][h * 64:(h + 1) * 64, qd["t0"]:qd["t0"] + n], ob[:64, :n], t_ob)

        queue = []
        for (qd, ki) in its:
            ps_, tps = issue_qk(qd, ki)
            queue.append((qd, ki, ps_, tps))
            if len(queue) > LOOK:
                finish(*queue.pop(0))
        while queue:
            finish(*queue.pop(0))
    g.ps_skip.difference_update((0, 1))


def phase_CD_stub(g, li, last=False):
    kb, nc, V = g.kb, g.nc, g.V
    z = kb.sb("cd_z", [128, 2048], BF16); t_z = kb.trk()
    V(lambda: nc.vector.memset(z, 0.0), w=[t_z])
    for nm in ("br2T",):
        for j in range(4):
            for c0 in range(0, g.T, 2048):
                w = min(2048, g.T - c0)
                store(g, nm, g.S[nm][0][j * 128:(j + 1) * 128, c0:c0 + w], z[:, :w], t_z)


def phase_F(g, li, last=False):
    kb, nc, V, A, P, G = g.kb, g.nc, g.V, g.A, g.P, g.G
    S, I, L, T = g.S, g.I, g.L, g.T
    kb_ = kb
    c = Ctx()
    c.xt = RPool(kb, "f_xt", [128, 4, D], F32, 1)
    c.xn = RPool(kb, "f_xn", [128, 4, D], BF16, 1)
    c.hT = RPool(kb, "f_hT", [128, 8, 512], BF16, 1)
    c.ss = RPool(kb, "f_ss", [128, 8], F32, 2)
    c.junk = RPool(kb, "f_junk", [128, D], BF16, 1)
    c.wt = RPool(kb, "f_wt", [128, 8, 512], BF16, 4)
    c.stb = RPool(kb, "f_stb", [128, 512], BF16, 2)
    c.stf = RPool(kb, "f_stf", [128, 512], F32, 8)
    mg_p = RPool(kb, "f_mg", [128, 8, 512], F32, 1)
    mgb_p = RPool(kb, "f_mgb", [128, 8, 512], BF16, 1)
    act_p = RPool(kb, "f_act", [128, 22, 512], BF16, 1)
    wf2_p = RPool(kb, "f_wf2", [128, 22, 512], BF16, 1)
    bt_p = RPool(kb, "f_bt", [128, 4, 512], BF16, 2)
    wpf_begin(g)
    for (t0, n, is_ctx, pos0) in g.tiles:
        if is_ctx and last:
            continue
        wpf_tile(g, is_ctx)
        j = 1 if is_ctx else 0
        nb = n // 128
        src = (I["ctx"] if is_ctx else I["x"][pos0:pos0 + n]) if li == 0 else S["xs"][0][t0:t0 + n]
        t_src = kb.trk() if li == 0 else S["xs"][1]
        hT, t_h, xt, t_xt, _, _ = norm_tile(g, c, src, t_src, n, g.G1c, g.modc[:, 0], j)
        mg, t_mg = mg_p.get()
        for i in range(4):
            bt, t_bt = bt_p.get()
            kb.dma(bt[:, :, :n], S["br%dT" % i][0].rearrange("(k p) t -> p k t", p=128)[:, :, t0:t0 + n],
                   reads=[S["br%dT" % i][1]], writes=[t_bt])
            gen_gl = fm_chunks(g, c, "Wgl", 8, i * 1024, 1024, hT, t_h, n)
            gen_pr = fm_chunks(g, c, "Wb", 4, 0, 1024, bt, t_bt, n, kc0=i * 4)
            for n8 in range(8):
                _, _, pg, tpg = next(gen_gl)
                _, _, pp, tpp = next(gen_pr)
                sg, t_sg = c.stf.get()
                A(lambda: nc.scalar.activation(sg[:, :n], pg[:, :n], AF.Sigmoid), r=[tpg], w=[t_sg])
                if i == 0:
                    V(lambda: nc.vector.tensor_tensor(mg[:, n8, :n], pp[:, :n], sg[:, :n], ALU.mult), r=[tpp, t_sg], w=[t_mg])
                else:
                    V(lambda: nc.vector.tensor_tensor(sg[:, :n], pp[:, :n], sg[:, :n], ALU.mult), r=[tpp, t_sg], w=[t_sg])
                    G(lambda: nc.gpsimd.tensor_tensor(mg[:, n8, :n], mg[:, n8, :n], sg[:, :n], ALU.add), r=[t_sg, t_mg], w=[t_mg])
            for _ in gen_gl:
                pass
            for _ in gen_pr:
                pass
        mgb, t_mgb = mgb_p.get()
        A(lambda: nc.scalar.copy(mgb[:, :, :n], mg[:, :, :n]), r=[t_mg], w=[t_mgb])
        for (off, w, b, ps_, tp) in tm_chunks(g, c, "Wout", 8, 0, D, mgb, t_mgb, n):
            tm, t_tm = c.stf.get()
            V(lambda: nc.vector.tensor_tensor(tm, ps_, g.modr[:, j, 0, off:off + 512], ALU.mult), r=[tp, g.t_modr], w=[t_tm])
            G(lambda: nc.gpsimd.tensor_tensor(xt[:, b, off:off + 512], xt[:, b, off:off + 512], tm, ALU.add), r=[t_tm, t_xt], w=[t_xt])
        h2, t_h2, _, _, _, _ = norm_tile(g, c, None, None, n, g.G2c, g.modc[:, 3], j, xt_pair=(xt, t_xt))
        act, t_act = act_p.get()
        gen_a = fm_chunks(g, c, "Wf1", 8, 0, FFN, h2, t_h2, n)
        gen_g = fm_chunks(g, c, "Wf1", 8, FFN, FFN, h2, t_h2, n)
        for kc in range(22):
            _, _, pa, tpa = next(gen_a)
            _, _, pg, tpg = next(gen_g)
            sg, t_sg = c.stf.get()
            A(lambda: nc.scalar.activation(sg[:, :n], pg[:, :n], AF.Silu), r=[tpg], w=[t_sg])
            V(lambda: nc.vector.tensor_tensor(act[:, kc, :n], pa[:, :n], sg[:, :n], ALU.mult), r=[tpa, t_sg], w=[t_act])
        for _ in gen_a:
            pass
        for _ in gen_g:
            pass
        for (off, w, b, ps_, tp) in tm_chunks(g, c, "Wf2", 22, 0, D, act, t_act, n, pool=wf2_p):
            tm, t_tm = c.stf.get()
            V(lambda: nc.vector.tensor_tensor(tm, ps_, g.modr[:, j, 1, off:off + 512], ALU.mult), r=[tp, g.t_modr], w=[t_tm])
            G(lambda: nc.gpsimd.tensor_tensor(xt[:, b, off:off + 512], xt[:, b, off:off + 512], tm, ALU.add), r=[t_tm, t_xt], w=[t_xt])
        if not last:
            kb.dma(S["xs"][0][t0:t0 + n].rearrange("(b p) d -> p b d", p=128), xt[:, :nb, :], reads=[t_xt], writes=[S["xs"][1]])
        else:
            ss, t_ss = c.ss.get(); junk, t_junk = c.junk.get()
            for b in range(nb):
                A(lambda: nc.scalar.activation(junk, xt[:, b, :], AF.Square, accum_out=ss[:, b:b + 1]), r=[t_xt], w=[t_junk, t_ss])
            V(lambda: nc.vector.tensor_scalar(ss[:, 0:nb], ss[:, 0:nb], 1.0 / D, EPS, ALU.mult, ALU.add), r=[t_ss], w=[t_ss])
            A(lambda: nc.scalar.activation(ss[:, 0:nb], ss[:, 0:nb], AF.Sqrt), r=[t_ss], w=[t_ss])
            V(lambda: nc.vector.reciprocal(ss[:, 0:nb], ss[:, 0:nb]), r=[t_ss], w=[t_ss])
            for b in range(nb):
                V(lambda: nc.vector.scalar_tensor_tensor(xt[:, b, :], xt[:, b, :], ss[:, b:b + 1], g.fin_g, ALU.mult, ALU.mult),
                  r=[t_xt, t_ss, g.t_c], w=[t_xt])
            kb.dma(g.out_d[pos0:pos0 + n].rearrange("(b p) d -> p b d", p=128), xt[:, :nb, :], reads=[t_xt], writes=[g.t_out])


def tm_store(g, c_ps, o_sb, t_o, nb, name, dram3, pool):
    kb, nc, P, A = g.kb, g.nc, g.P, g.A
    ps_, tp = g.psum(); psv = ps_.bitcast(BF16)
    for b in range(nb):
        P(lambda: nc.tensor.transpose(psv[:, b * 128:(b + 1) * 128], o_sb[:, b * 128:(b + 1) * 128], g.ident_b),
          r=[t_o, g.t_c], w=[tp], acc=(b > 0), inc=(b == nb - 1))
    kt, t_kt = pool.get()
    A(lambda: nc.scalar.copy(kt[:, :nb, :], psv[:, :nb * 128].rearrange("p (b f) -> p b f", f=128)), r=[tp], w=[t_kt])
    store(g, name, dram3, kt[:, :nb, :], t_kt)


def phase_C(g, li, last=False):
    kb, nc, V, A, P, G = g.kb, g.nc, g.V, g.A, g.P, g.G
    S, I, L, T = g.S, g.I, g.L, g.T
    cw = kb.sb("c_cw", [128, 8, 5], F32); t_cw = kb.trk()
    for tap in range(5):
        for j in range(8):
            kb.dma(cw[:, j, tap:tap + 1], I["ssm_conv_w"][li, tap, j * 128:(j + 1) * 128].rearrange("(p o) -> p o", o=1), writes=[t_cw])
    scb = kb.sb("c_scb", [128, 8], F32); t_scb = kb.trk()
    g.coldma(scb, I["ssm_conv_b"][li], 8, t_scb)
    Dg = kb.sb("c_Dg", [128, 8, 5, 128], BF16); t_Dg = kb.trk()
    for j in range(8):
        for tap in range(5):
            V(lambda: nc.vector.tensor_scalar(Dg[:, j, tap, :], g.ident_b, cw[:, j, tap:tap + 1], None, ALU.mult), r=[t_cw, g.t_c], w=[t_Dg])
    ut_p = RPool(kb, "c_ut", [128, 8, 516], BF16, 2)
    ob_p = RPool(kb, "c_ob", [128, 512], BF16, 12)
    tk_p = RPool(kb, "c_tk", [128, 4, 128], BF16, 12)
    x3 = S["xbcT"][0].rearrange("(j p) t -> p j t", p=128)
    for (t0, n, is_ctx, pos0) in g.tiles:
        nb = n // 128
        s_lo, s_hi = (0, NCTX) if is_ctx else (NCTX, T)
        lo = max(s_lo, t0 - 2); hi = min(s_hi, t0 + n + 2)
        ut, t_ut = ut_p.get()
        if lo > t0 - 2 or hi < t0 + n + 2:
            V(lambda: nc.vector.memset(ut, 0.0), w=[t_ut])
        kb.dma(ut[:, :, lo - (t0 - 2):hi - (t0 - 2)], x3[:, :, lo:hi], reads=[S["xbcT"][1]], writes=[t_ut])
        for j in range(8):
            ps_, tp = g.psum()
            for tap in range(5):
                P(lambda: nc.tensor.matmul(ps_[:, :n], Dg[:, j, tap, :], ut[:, j, tap:tap + n], start=(tap == 0), stop=(tap == 4)),
                  r=[t_Dg, t_ut], w=[tp], acc=(tap > 0), inc=(tap == 4))
            o, t_o = ob_p.get()
            A(lambda: nc.scalar.activation(o[:, :n], ps_[:, :n], AF.Silu, bias=scb[:, j:j + 1]), r=[tp, t_scb], w=[t_o])
            if j < 4:
                tm_store(g, None, o, t_o, nb, "ssx", S["ssx"][0][t0:t0 + n, j * 128:(j + 1) * 128].rearrange("(b p) f -> p b f", p=128), tk_p)
            elif j < 6:
                store(g, "ssBT", S["ssBT"][0][(j - 4) * 128:(j - 3) * 128, t0:t0 + n], o[:, :n], t_o)
                tm_store(g, None, o, t_o, nb, "ssB", S["ssB"][0][t0:t0 + n, (j - 4) * 128:(j - 3) * 128].rearrange("(b p) f -> p b f", p=128), tk_p)
            else:
                store(g, "ssCT", S["ssCT"][0][(j - 6) * 128:(j - 5) * 128, t0:t0 + n], o[:, :n], t_o)
    if ASTOP == 60:
        return
    NC_ = T // 128
    nctx = NCTX // 128
    Dr = kb.sb("c_Dr", [128, 8], F32); ngr = kb.sb("c_ngr", [128, 512], F32); t_cst = kb.trk()
    kb.dma(Dr, I["ssm_d"][li].partition_broadcast(128), writes=[t_cst])
    kb.dma(ngr, I["ssm_norm_g"][li].partition_broadcast(128), writes=[t_cst])
    hst = kb.sb("c_h", [128, 512], F32); hb = kb.sb("c_hb", [128, 512], BF16); t_h = kb.trk(); t_hb = kb.trk()
    sm_p = RPool(kb, "c_sm", [128, 16], F32, 8)
    s8_p = RPool(kb, "c_s8", [128, 8], F32, 28)
    x_p = RPool(kb, "c_x", [128, 512], BF16, 6)
    b_p = RPool(kb, "c_b", [128, 256], BF16, 6)
    bt_p = RPool(kb, "c_bt", [128, 2, 128], BF16, 6)
    ct_p = RPool(kb, "c_ct", [128, 2, 128], BF16, 6)
    xd_p = RPool(kb, "c_xd", [128, 512], BF16, 8)
    gm_p = RPool(kb, "c_gm", [128, 2, 128], F32, 4)
    R_p = RPool(kb, "c_R", [128, 8, 128], F32, 4)
    E_p = RPool(kb, "c_E", [128, 8, 128], F32, 4)
    pm_p = RPool(kb, "c_pm", [128, 8, 128], BF16, 4)
    f5_p = RPool(kb, "c_f5", [128, 512], F32, 12)
    z_p = RPool(kb, "c_z", [128, 512], BF16, 4)
    o5_p = RPool(kb, "c_o5", [128, 512], BF16, 4)
    jk_p = RPool(kb, "c_jk", [128, 256], BF16, 1)
    BT3 = S["ssBT"][0].rearrange("(g p) t -> p g t", p=128)
    CT3 = S["ssCT"][0].rearrange("(g p) t -> p g t", p=128)
    br3 = S["br1T"][0].rearrange("(j p) t -> p j t", p=128)

    def b8(ap, w):
        return ap.unsqueeze(2).broadcast_to([128, 8, w])

    orders = [list(range(NC_)), list(range(nctx - 1, -1, -1)) + list(range(NC_ - 1, nctx - 1, -1))]
    hst2 = [hst, kb.sb("c_h1", [128, 512], F32)]; hb2 = [hb, kb.sb("c_hb1", [128, 512], BF16)]
    t_h2 = [t_h, kb.trk()]; t_hb2 = [t_hb, kb.trk()]
    for d in range(2):
        V(lambda: nc.vector.memset(hst2[d], 0.0), w=[t_h2[d]])
        V(lambda: nc.vector.memset(hb2[d], 0.0), w=[t_hb2[d]])
    for step in range(NC_):
        for d in range(2):
            Ud = g.Uf if d == 0 else g.Lf
            hst, hb, t_h, t_hb = hst2[d], hb2[d], t_h2[d], t_hb2[d]
            ck = orders[d][step]
            tk = ck * 128
            is_ctx = ck < nctx
            sm, t_sm = sm_p.get()
            kb.dma(sm[:, 0:8], S["la"][0][tk:tk + 128, d * 8:(d + 1) * 8], reads=[S["la"][1]], writes=[t_sm])
            kb.dma(sm[:, 8:16], S["dt"][0][tk:tk + 128, d * 8:(d + 1) * 8], reads=[S["dt"][1]], writes=[t_sm])
            la8, dt8 = sm[:, 0:8], sm[:, 8:16]
            x, t_x = x_p.get(); kb.dma(x, S["ssx"][0][tk:tk + 128, :], reads=[S["ssx"][1]], writes=[t_x])
            Bt, t_B = b_p.get(); kb.dma(Bt, S["ssB"][0][tk:tk + 128, :], reads=[S["ssB"][1]], writes=[t_B])
            BT, t_BT = bt_p.get(); kb.dma(BT, BT3[:, :, tk:tk + 128], reads=[S["ssBT"][1]], writes=[t_BT])
            CT, t_CT = ct_p.get(); kb.dma(CT, CT3[:, :, tk:tk + 128], reads=[S["ssCT"][1]], writes=[t_CT])
            pc, tpc = g.psum()
            P(lambda: nc.tensor.matmul(pc[:, 0:8], Ud, la8, start=True, stop=True), r=[g.t_c, t_sm], w=[tpc])
            P(lambda: nc.tensor.matmul(pc[:, 8:16], g.ones_f, la8, start=True, stop=True), r=[g.t_c, t_sm], w=[tpc], acc=True)
            cum, t_cum = s8_p.get(); tot, t_tot = s8_p.get()
            V(lambda: nc.vector.tensor_copy(cum, pc[:, 0:8]), r=[tpc], w=[t_cum])
            V(lambda: nc.vector.tensor_copy(tot, pc[:, 8:16]), r=[tpc], w=[t_tot])
            ecum, t_ec = s8_p.get(); dst, t_ds = s8_p.get(); etot, t_et = s8_p.get(); dtd, t_dd = s8_p.get()
            A(lambda: nc.scalar.activation(ecum, cum, AF.Exp), r=[t_cum], w=[t_ec])
            V(lambda: nc.vector.tensor_tensor(dst, tot, cum, ALU.subtract), r=[t_tot, t_cum], w=[t_ds])
            A(lambda: nc.scalar.activation(dst, dst, AF.Exp), r=[t_ds], w=[t_ds])
            A(lambda: nc.scalar.activation(etot, tot, AF.Exp), r=[t_tot], w=[t_et])
            V(lambda: nc.vector.tensor_tensor(dtd, dst, dt8, ALU.mult), r=[t_ds, t_sm], w=[t_dd])
            xdt, t_xdt = xd_p.get(); xd, t_xd = xd_p.get()
            x3_ = x.rearrange("p (h e) -> p h e", e=64)
            V(lambda: nc.vector.tensor_tensor(xdt.rearrange("p (h e) -> p h e", e=64), x3_, b8(dt8, 64), ALU.mult), r=[t_x, t_sm], w=[t_xdt])
            V(lambda: nc.vector.tensor_tensor(xd.rearrange("p (h e) -> p h e", e=64), x3_, b8(dtd, 64), ALU.mult), r=[t_x, t_dd], w=[t_xd])
            pg, tpg = g.psum()
            for gi in range(2):
                P(lambda: nc.tensor.matmul(pg[:, gi * 128:(gi + 1) * 128], BT[:, gi, :], CT[:, gi, :], start=True, stop=True),
                  r=[t_BT, t_CT], w=[tpg], acc=(gi > 0))
            gm, t_gm = gm_p.get()
            V(lambda: nc.vector.tensor_tensor(gm, pg[:, 0:256].rearrange("p (g l) -> p g l", l=128),
                                              Ud.unsqueeze(1).broadcast_to([128, 2, 128]), ALU.mult), r=[tpg, g.t_c], w=[t_gm])
            Rt, t_R = R_p.get()
            V(lambda: nc.vector.tensor_tensor(Rt, b8(la8, 128), Ud.unsqueeze(1).broadcast_to([128, 8, 128]), ALU.mult), r=[t_sm, g.t_c], w=[t_R])
            Et, t_E = E_p.get()
            for hf in range(2):
                pb, tpb = g.psum()
                P(lambda: nc.tensor.matmul(pb, g.ones_f, Rt[:, hf * 4:(hf + 1) * 4, :].rearrange("p h l -> p (h l)"), start=True, stop=True),
                  r=[g.t_c, t_R], w=[tpb])
                for hh in range(4):
                    h = hf * 4 + hh
                    V(lambda: nc.vector.tensor_scalar(Et[:, h, :], pb[:, hh * 128:(hh + 1) * 128], cum[:, h:h + 1], 0.0, ALU.subtract, ALU.min),
                      r=[tpb, t_cum], w=[t_E])
            A(lambda: nc.scalar.activation(Et, Et, AF.Exp), r=[t_E], w=[t_E])
            pm, t_pm = pm_p.get()
            for gi in range(2):
                V(lambda: nc.vector.tensor_tensor(pm[:, gi * 4:(gi + 1) * 4, :], Et[:, gi * 4:(gi + 1) * 4, :],
                                                  gm[:, gi, :].unsqueeze(1).broadcast_to([128, 4, 128]), ALU.mult), r=[t_E, t_gm], w=[t_pm])
            py, tpy = g.psum()
            for h in range(8):
                P(lambda: nc.tensor.matmul(py[:, h * 64:(h + 1) * 64], pm[:, h, :], xdt[:, h * 64:(h + 1) * 64], start=True, stop=True),
                  r=[t_pm, t_xdt], w=[tpy], acc=(h > 0), inc=(h == 7))
            pi_, tpi = g.psum()
            for gi in range(2):
                P(lambda: nc.tensor.matmul(pi_[:, gi * 256:(gi + 1) * 256], CT[:, gi, :], hb[:, gi * 256:(gi + 1) * 256], start=True, stop=True),
                  r=[t_CT, t_hb], w=[tpi], acc=(gi > 0), inc=(gi == 1))
            t1, t_t1 = f5_p.get(); y, t_y = f5_p.get()
            V(lambda: nc.vector.tensor_tensor(t1.rearrange("p (h e) -> p h e", e=64), pi_.rearrange("p (h e) -> p h e", e=64), b8(ecum, 64), ALU.mult),
              r=[tpi, t_ec], w=[t_t1])
            V(lambda: nc.vector.tensor_tensor(y, py, t1, ALU.add), r=[tpy, t_t1], w=[t_y])
            pS, tpS = g.psum()
            for gi in range(2):
                P(lambda: nc.tensor.matmul(pS[:, gi * 256:(gi + 1) * 256], Bt[:, gi * 128:(gi + 1) * 128], xd[:, gi * 256:(gi + 1) * 256], start=True, stop=True),
                  r=[t_B, t_xd], w=[tpS], acc=(gi > 0), inc=(gi == 1))
            V(lambda: nc.vector.tensor_tensor(hst.rearrange("p (h e) -> p h e", e=64), hst.rearrange("p (h e) -> p h e", e=64), b8(etot, 64), ALU.mult),
              r=[t_h, t_et], w=[t_h])
            V(lambda: nc.vector.tensor_tensor(hst, hst, pS, ALU.add), r=[t_h, tpS], w=[t_h])
            A(lambda: nc.scalar.copy(hb, hst), r=[t_h], w=[t_hb])
            yn_ = "yf" if d == 0 else "yb"
            kb.dma_deferred(S[yn_][0][tk:tk + 128, :], y, reads=[t_y], writes=[S[yn_][1]])
    if True:
        for ck in range(NC_):
            tk = ck * 128
            is_ctx = ck < nctx
            if is_ctx and last:
                continue
            x, t_x = x_p.get(); kb.dma(x, S["ssx"][0][tk:tk + 128, :], reads=[S["ssx"][1]], writes=[t_x])
            x3_ = x.rearrange("p (h e) -> p h e", e=64)
            y, t_y = f5_p.get(); kb.dma(y, S["yb"][0][tk:tk + 128, :], reads=[S["yb"][1]], writes=[t_y])
            t1, t_t1 = f5_p.get()
            yf, t_yf = f5_p.get(); kb.dma(yf, S["yf"][0][tk:tk + 128, :], reads=[S["yf"][1]], writes=[t_yf])
            zt, t_z = z_p.get(); kb.dma(zt, S["zs"][0][tk:tk + 128, :], reads=[S["zs"][1]], writes=[t_z])
            G(lambda: nc.gpsimd.tensor_tensor(y, y, yf, ALU.add), r=[t_y, t_yf], w=[t_y])
            V(lambda: nc.vector.tensor_tensor(t1.rearrange("p (h e) -> p h e", e=64), x3_, b8(Dr, 64), ALU.mult), r=[t_x, t_cst], w=[t_t1])
            G(lambda: nc.gpsimd.tensor_tensor(y, y, t1, ALU.add), r=[t_y, t_t1], w=[t_y])
            V(lambda: nc.vector.tensor_tensor(y, y, zt, ALU.mult), r=[t_y, t_z], w=[t_y])
            ss, t_ss = s8_p.get(); jk, t_jk = jk_p.get()
            for gi in range(2):
                A(lambda: nc.scalar.activation(jk, y[:, gi * 256:(gi + 1) * 256], AF.Square, accum_out=ss[:, gi:gi + 1]), r=[t_y], w=[t_jk, t_ss])
            V(lambda: nc.vector.tensor_scalar(ss[:, 0:2], ss[:, 0:2], 1.0 / 256.0, EPS, ALU.mult, ALU.add), r=[t_ss], w=[t_ss])
            A(lambda: nc.scalar.activation(ss[:, 0:2], ss[:, 0:2], AF.Sqrt), r=[t_ss], w=[t_ss])
            V(lambda: nc.vector.reciprocal(ss[:, 0:2], ss[:, 0:2]), r=[t_ss], w=[t_ss])
            o5, t_o5 = o5_p.get()
            for gi in range(2):
                V(lambda: nc.vector.scalar_tensor_tensor(o5[:, gi * 256:(gi + 1) * 256], y[:, gi * 256:(gi + 1) * 256], ss[:, gi:gi + 1],
                                                         ngr[:, gi * 256:(gi + 1) * 256], ALU.mult, ALU.mult), r=[t_y, t_ss, t_cst], w=[t_o5])
            ps_, tp = g.psum(); psv = ps_.bitcast(BF16)
            for j in range(4):
                P(lambda: nc.tensor.transpose(psv[:, j * 128:(j + 1) * 128], o5[:, j * 128:(j + 1) * 128], g.ident_b),
                  r=[t_o5, g.t_c], w=[tp], acc=(j > 0), inc=(j == 3))
            kt, t_kt = tk_p.get()
            A(lambda: nc.scalar.copy(kt, psv[:, 0:512].rearrange("p (j t) -> p j t", t=128)), r=[tp], w=[t_kt])
            store(g, "br1T", br3[:, :, tk:tk + 128], kt, t_kt)


def phase_D(g, li, last=False):
    kb, nc, V, A, P, G = g.kb, g.nc, g.V, g.A, g.P, g.G
    S, I, L, T = g.S, g.I, g.L, g.T
    NC_ = T // 128
    nctx = NCTX // 128
    lgb = kb.sb("d_lgb", [128, 8], F32); t_k = kb.trk()
    kb.dma(lgb, I["ret_decay"][li].partition_broadcast(128), writes=[t_k])
    A(lambda: nc.scalar.activation(lgb, lgb, AF.Exp), r=[t_k], w=[t_k])
    V(lambda: nc.vector.tensor_scalar(lgb, lgb, -1.0, None, ALU.mult), r=[t_k], w=[t_k])
    gr = kb.sb("d_gr", [128, 2, 512], F32)
    kb.dma(gr[:, 0, :], I["ret_gn_g"][li].partition_broadcast(128), writes=[t_k])
    kb.dma(gr[:, 1, :], I["ret_gn_b"][li].partition_broadcast(128), writes=[t_k])
    ii = kb.sb("d_ii", [128, 132], I32); ff = kb.sb("d_ff", [128, 133], F32)
    G(lambda: nc.gpsimd.iota(ii[:, 0:128], pattern=[[1, 128]], base=0, channel_multiplier=-1), w=[t_k])
    for cidx, (base, cm) in enumerate(((1, 1), (127, -1), (128, -1), (0, 1))):
        G(lambda: nc.gpsimd.iota(ii[:, 128 + cidx:129 + cidx], pattern=[[0, 1]], base=base, channel_multiplier=cm), w=[t_k])
    V(lambda: nc.vector.tensor_copy(ff[:, 0:132], ii), r=[t_k], w=[t_k])
    A(lambda: nc.scalar.activation(ff[:, 0:128], ff[:, 0:128], AF.Abs), r=[t_k], w=[t_k])
    V(lambda: nc.vector.memset(ff[:, 132:133], 128.0), r=[t_k], w=[t_k])
    Dm = kb.sb("d_Dm", [128, 2, 4, 128], F32)
    vec = kb.sb("d_vec", [128, 2, 3, 4], F32)
    for d in range(2):
        Ud = g.Uf if d == 0 else g.Lf
        for h in range(4):
            k = d * 4 + h
            A(lambda: nc.scalar.activation(Dm[:, d, h, :], ff[:, 0:128], AF.Exp, scale=lgb[:, k:k + 1]), r=[t_k], w=[t_k])
            V(lambda: nc.vector.tensor_tensor(Dm[:, d, h, :], Dm[:, d, h, :], Ud, ALU.mult), r=[t_k, g.t_c], w=[t_k])
            c_e, c_d = (128, 129) if d == 0 else (130, 131)
            A(lambda: nc.scalar.activation(vec[:, d, 0, h:h + 1], ff[:, c_e:c_e + 1], AF.Exp, scale=lgb[:, k:k + 1]), r=[t_k], w=[t_k])
            A(lambda: nc.scalar.activation(vec[:, d, 1, h:h + 1], ff[:, c_d:c_d + 1], AF.Exp, scale=lgb[:, k:k + 1]), r=[t_k], w=[t_k])
            A(lambda: nc.scalar.activation(vec[:, d, 2, h:h + 1], ff[:, 132:133], AF.Exp, scale=lgb[:, k:k + 1]), r=[t_k], w=[t_k])
    hst = kb.sb("d_h", [64, 4, 128], F32); hb = kb.sb("d_hb", [64, 4, 128], BF16); t_h = kb.trk(); t_hb = kb.trk()
    q_p = RPool(kb, "d_q", [64, 4, 128], BF16, 6)
    k_p = RPool(kb, "d_k", [64, 4, 128], BF16, 6)
    km_p = RPool(kb, "d_km", [128, 256], BF16, 6)
    kd_p = RPool(kb, "d_kd", [128, 256], BF16, 4)
    v_p = RPool(kb, "d_v", [128, 512], BF16, 6)
    pm_p = RPool(kb, "d_pm", [128, 4, 128], BF16, 4)
    f5_p = RPool(kb, "d_f5", [128, 512], F32, 12)
    g_p = RPool(kb, "d_g", [128, 512], BF16, 4)
    o5_p = RPool(kb, "d_o5", [128, 512], BF16, 4)
    s8_p = RPool(kb, "d_s8", [128, 16], F32, 8)
    jk_p = RPool(kb, "d_jk", [128, 128], BF16, 1)
    tk_p = RPool(kb, "d_tk", [128, 4, 128], BF16, 10)
    q3 = S["qrT"][0].rearrange("(h d) t -> d h t", d=64)
    k3 = S["krT"][0].rearrange("(h d) t -> d h t", d=64)
    br3 = S["br2T"][0].rearrange("(j p) t -> p j t", p=128)

    def b4(ap, w):
        return ap.unsqueeze(2).broadcast_to([ap.shape[0], 4, w])

    orders = [list(range(NC_)), list(range(nctx - 1, -1, -1)) + list(range(NC_ - 1, nctx - 1, -1))]
    hst2 = [hst, kb.sb("d_h1", [64, 4, 128], F32)]; hb2 = [hb, kb.sb("d_hb1", [64, 4, 128], BF16)]
    t_h2 = [t_h, kb.trk()]; t_hb2 = [t_hb, kb.trk()]
    for d in range(2):
        V(lambda: nc.vector.memset(hst2[d], 0.0), w=[t_h2[d]])
        V(lambda: nc.vector.memset(hb2[d], 0.0), w=[t_hb2[d]])
    for step in range(NC_):
        for d in range(2):
            hst, hb, t_h, t_hb = hst2[d], hb2[d], t_h2[d], t_hb2[d]
            ck = orders[d][step]
            tk = ck * 128
            is_ctx = ck < nctx
            qT, t_q = q_p.get(); kb.dma(qT, q3[:, :, tk:tk + 128], reads=[S["qrT"][1]], writes=[t_q])
            kT, t_kT = k_p.get(); kb.dma(kT, k3[:, :, tk:tk + 128], reads=[S["krT"][1]], writes=[t_kT])
            km, t_km = km_p.get(); kb.dma(km, S["kr"][0][tk:tk + 128, :], reads=[S["kr"][1]], writes=[t_km])
            v, t_v = v_p.get(); kb.dma(v, S["rv"][0][tk:tk + 128, :], reads=[S["rv"][1]], writes=[t_v])
            pg, tpg = g.psum()
            for h in range(4):
                P(lambda: nc.tensor.matmul(pg[:, h * 128:(h + 1) * 128], kT[:, h, :], qT[:, h, :], start=True, stop=True),
                  r=[t_kT, t_q], w=[tpg], acc=(h > 0), inc=(h == 3))
            pm, t_pm = pm_p.get()
            V(lambda: nc.vector.tensor_tensor(pm, pg.rearrange("p (h l) -> p h l", l=128), Dm[:, d], ALU.mult), r=[tpg, t_k], w=[t_pm])
            py, tpy = g.psum()
            for h in range(4):
                P(lambda: nc.tensor.matmul(py[:, h * 128:(h + 1) * 128], pm[:, h, :], v[:, h * 128:(h + 1) * 128], start=True, stop=True),
                  r=[t_pm, t_v], w=[tpy], acc=(h > 0), inc=(h == 3))
            pi_, tpi = g.psum()
            for h in range(4):
                P(lambda: nc.tensor.matmul(pi_[:, h * 128:(h + 1) * 128], qT[:, h, :], hb[:, h, :], start=True, stop=True),
                  r=[t_q, t_hb], w=[tpi], acc=(h > 0), inc=(h == 3))
            t1, t_t1 = f5_p.get(); y, t_y = f5_p.get()
            V(lambda: nc.vector.tensor_tensor(t1.rearrange("p (h e) -> p h e", e=128), pi_.rearrange("p (h e) -> p h e", e=128),
                                              b4(vec[:, d, 0, :], 128), ALU.mult), r=[tpi, t_k], w=[t_t1])
            V(lambda: nc.vector.tensor_tensor(y, py, t1, ALU.add), r=[tpy, t_t1], w=[t_y])
            kd, t_kd = kd_p.get()
            V(lambda: nc.vector.tensor_tensor(kd.rearrange("p (h e) -> p h e", e=64), km.rearrange("p (h e) -> p h e", e=64),
                                              b4(vec[:, d, 1, :], 64), ALU.mult), r=[t_km, t_k], w=[t_kd])
            pS, tpS = g.psum()
            for h in range(4):
                P(lambda: nc.tensor.matmul(pS[:64, h * 128:(h + 1) * 128], kd[:, h * 64:(h + 1) * 64], v[:, h * 128:(h + 1) * 128], start=True, stop=True),
                  r=[t_kd, t_v], w=[tpS], acc=(h > 0), inc=(h == 3))
            V(lambda: nc.vector.tensor_tensor(hst, hst, b4(vec[:64, d, 2, :], 128), ALU.mult), r=[t_h, t_k], w=[t_h])
            V(lambda: nc.vector.tensor_tensor(hst, hst, pS[:64, :].rearrange("p (h e) -> p h e", e=128), ALU.add), r=[t_h, tpS], w=[t_h])
            A(lambda: nc.scalar.copy(hb, hst), r=[t_h], w=[t_hb])
            yn_ = "ryf" if d == 0 else "ryb"
            kb.dma_deferred(S[yn_][0][tk:tk + 128, :], y, reads=[t_y], writes=[S[yn_][1]])
    if True:
        for ck in range(NC_):
            tk = ck * 128
            is_ctx = ck < nctx
            if is_ctx and last:
                continue
            y, t_y = f5_p.get(); kb.dma(y, S["ryb"][0][tk:tk + 128, :], reads=[S["ryb"][1]], writes=[t_y])
            yf, t_yf = f5_p.get(); kb.dma(yf, S["ryf"][0][tk:tk + 128, :], reads=[S["ryf"][1]], writes=[t_yf])
            gt, t_g = g_p.get(); kb.dma(gt, S["rg"][0][tk:tk + 128, :], reads=[S["rg"][1]], writes=[t_g])
            G(lambda: nc.gpsimd.tensor_tensor(y, y, yf, ALU.add), r=[t_y, t_yf], w=[t_y])
            st, t_st = s8_p.get(); jk, t_jk = jk_p.get()
            for h in range(4):
                A(lambda: nc.scalar.activation(jk, y[:, h * 128:(h + 1) * 128], AF.Identity, accum_out=st[:, h:h + 1]), r=[t_y], w=[t_jk, t_st])
                A(lambda: nc.scalar.activation(jk, y[:, h * 128:(h + 1) * 128], AF.Square, accum_out=st[:, 4 + h:5 + h]), r=[t_y], w=[t_jk, t_st])
            V(lambda: nc.vector.tensor_scalar(st[:, 0:8], st[:, 0:8], 1.0 / 128.0, None, ALU.mult), r=[t_st], w=[t_st])
            V(lambda: nc.vector.tensor_tensor(st[:, 8:12], st[:, 0:4], st[:, 0:4], ALU.mult), r=[t_st], w=[t_st])
            V(lambda: nc.vector.tensor_tensor(st[:, 12:16], st[:, 4:8], st[:, 8:12], ALU.subtract), r=[t_st], w=[t_st])
            V(lambda: nc.vector.tensor_scalar(st[:, 12:16], st[:, 12:16], EPS, None, ALU.add), r=[t_st], w=[t_st])
            A(lambda: nc.scalar.activation(st[:, 12:16], st[:, 12:16], AF.Sqrt), r=[t_st], w=[t_st])
            V(lambda: nc.vector.reciprocal(st[:, 12:16], st[:, 12:16]), r=[t_st], w=[t_st])
            for h in range(4):
                V(lambda: nc.vector.tensor_scalar(y[:, h * 128:(h + 1) * 128], y[:, h * 128:(h + 1) * 128], st[:, h:h + 1], st[:, 12 + h:13 + h],
                                                  ALU.subtract, ALU.mult), r=[t_y, t_st], w=[t_y])
            V(lambda: nc.vector.tensor_tensor(y, y, gr[:, 0, :], ALU.mult), r=[t_y, t_k], w=[t_y])
            G(lambda: nc.gpsimd.tensor_tensor(y, y, gr[:, 1, :], ALU.add), r=[t_y, t_k], w=[t_y])
            o5, t_o5 = o5_p.get()
            V(lambda: nc.vector.tensor_tensor(o5, y, gt, ALU.mult), r=[t_y, t_g], w=[t_o5])
            ps_, tp = g.psum(); psv = ps_.bitcast(BF16)
            for j in range(4):
                P(lambda: nc.tensor.transpose(psv[:, j * 128:(j + 1) * 128], o5[:, j * 128:(j + 1) * 128], g.ident_b),
                  r=[t_o5, g.t_c], w=[tp], acc=(j > 0), inc=(j == 3))
            kt, t_kt = tk_p.get()
            A(lambda: nc.scalar.copy(kt, psv[:, 0:512].rearrange("p (j t) -> p j t", t=128)), r=[tp], w=[t_kt])
            store(g, "br2T", br3[:, :, tk:tk + 128], kt, t_kt)


_CACHE = {}


def kernel(**inputs):
    x = np.asarray(inputs["x"], dtype=np.float32)
    B, L, _ = x.shape
    if L not in _CACHE:
        _CACHE[L] = build(L, n_layers=2, debug=False)[0]
    nc = _CACHE[L]
    tabs = rope_tables(L)

    def pack(t):
        return np.ascontiguousarray(np.concatenate([t[0], t[1], t[2], t[3]], axis=1), dtype=np.float32)

    def core_in(b):
        d = {"x": np.ascontiguousarray(x[b]), "ctx": np.ascontiguousarray(inputs["ctx"][b], dtype=np.float32),
             "c2": np.ascontiguousarray(np.stack([inputs["c"][b], inputs["c_ctx"]]), dtype=np.float32),
             "t_ret": pack(tabs["ret"]), "t_mla": pack(tabs["mla"]), "t_mlk": pack(tabs["mlk"])}
        for k, v in inputs.items():
            if k in ("x", "ctx", "c", "c_ctx"):
                continue
            v = np.asarray(v, dtype=np.float32)
            if k in ("ssm_dt_bias", "ssm_a_log", "ret_decay"):
                v = v.reshape(2, -1)
            if k == "final_norm_g":
                v = v.reshape(1, -1)
            d[k] = np.ascontiguousarray(v)
        return d

    res = run_bass_kernel_spmd(nc, [core_in(b) for b in range(B)], core_ids=list(range(B)))
    return np.stack([np.asarray(res.results[b]["out"], dtype=np.float32) for b in range(B)], axis=0)
```
